# Optimizing a Trainium2 kernel written in Bass

```python
import math
import jax, jax.numpy as jnp
from jax import lax
import numpy as np

D_MODEL = 1024
BATCH = 16
SEQ = 2048
DEPTH = 4

HEAD_DIM = 64
ROT_DIM = HEAD_DIM // 4
ROPE_THETA = 500000.0

A_Q_HEADS = 8
A_KV_HEADS = 2
A_WINDOW = 128

B_GROUPS = ((128, 1), (512, 4), (2048, 16))
N_B_GROUPS = 3
B_HEADS_PER_GROUP = 4

D_FF = 4 * D_MODEL
PLE_DIM = 256
N_BRANCHES = 2
LN_EPS = 1e-5
DEEPNORM_ALPHA = (2 * DEPTH) ** 0.25
DEEPNORM_BETA = (8 * DEPTH) ** -0.25

A_Q_W = A_Q_HEADS * HEAD_DIM
A_KV_W = A_KV_HEADS * HEAD_DIM
B_HEADS = N_B_GROUPS * B_HEADS_PER_GROUP
B_W = B_HEADS * HEAD_DIM
B_OUT_W = B_HEADS_PER_GROUP * HEAD_DIM
GATE_W = N_BRANCHES * D_MODEL
D_IN = A_Q_W + 2 * A_KV_W + 3 * B_W + GATE_W
IN_SPLITS = (A_Q_W, A_Q_W + A_KV_W, A_Q_W + 2 * A_KV_W,
             A_Q_W + 2 * A_KV_W + B_W, A_Q_W + 2 * A_KV_W + 2 * B_W,
             A_Q_W + 2 * A_KV_W + 3 * B_W)

kernel_name = "hybrid_gated_window_dilated_encoder"


def _layer_norm(x, g, b):
    xf = x.astype(jnp.float32)
    mu = jnp.mean(xf, axis=-1, keepdims=True)
    var = jnp.mean(jnp.square(xf - mu), axis=-1, keepdims=True)
    y = (xf - mu) * lax.rsqrt(var + LN_EPS) * g.astype(jnp.float32) + b.astype(jnp.float32)
    return y.astype(x.dtype)


def _rope_tables(seq):
    pos = jnp.arange(seq, dtype=jnp.float32)
    inv_freq = ROPE_THETA ** (-jnp.arange(0, ROT_DIM, 2, dtype=jnp.float32) / ROT_DIM)
    ang = pos[:, None] * inv_freq[None, :]
    return jnp.cos(ang), jnp.sin(ang)


def _rope(t, cos, sin):
    half = ROT_DIM // 2
    tf = t.astype(jnp.float32)
    t1, t2 = tf[..., :half], tf[..., half:ROT_DIM]
    c, s = cos[:, None, :], sin[:, None, :]
    out = jnp.concatenate([t1 * c - t2 * s, t2 * c + t1 * s, tf[..., ROT_DIM:]], axis=-1)
    return out.astype(t.dtype)


def _banded_attention(q, k, v, window, sink=None):
    n, seq, hq, dh = q.shape
    hkv = k.shape[2]
    grp = hq // hkv
    bs = window
    nblk = -(-seq // bs)
    lp = nblk * bs
    scale = 1.0 / math.sqrt(dh)
    qp = jnp.pad(q, ((0, 0), (0, lp - seq), (0, 0), (0, 0)))
    kp = jnp.pad(k, ((0, 0), (bs, lp - seq + bs), (0, 0), (0, 0))).astype(jnp.float32)
    vp = jnp.pad(v, ((0, 0), (bs, lp - seq + bs), (0, 0), (0, 0))).astype(jnp.float32)
    qb = qp.reshape(n, nblk, bs, hkv, grp, dh).transpose(1, 0, 2, 3, 4, 5)
    sink_b = None if sink is None else sink.astype(jnp.float32).reshape(1, hkv, grp, 1)

    def block(args):
        j, qj = args
        kj = lax.dynamic_slice_in_dim(kp, j * bs, 3 * bs, axis=1)
        vj = lax.dynamic_slice_in_dim(vp, j * bs, 3 * bs, axis=1)
        s = jnp.einsum('nqkgd,nskd->nkgqs', qj.astype(jnp.float32) * scale, kj)
        qpos = j * bs + jnp.arange(bs)
        kpos = j * bs - bs + jnp.arange(3 * bs)
        mask = (jnp.abs(qpos[:, None] - kpos[None, :]) <= window) & (kpos >= 0)[None, :] & (kpos < seq)[None, :]
        s = jnp.where(mask, s, -jnp.inf)
        m = jnp.max(s, axis=-1)
        if sink_b is not None:
            m = jnp.maximum(m, sink_b)
        e = jnp.exp(s - m[..., None])
        denom = jnp.sum(e, axis=-1)
        if sink_b is not None:
            denom = denom + jnp.exp(sink_b - m)
        o = jnp.einsum('nkgqs,nskd->nqkgd', e, vj) / denom.transpose(0, 3, 1, 2)[..., None]
        lse = (m + jnp.log(denom)).transpose(0, 3, 1, 2)
        return o, lse

    o, lse = lax.map(block, (jnp.arange(nblk), qb))
    o = o.transpose(1, 0, 2, 3, 4, 5).reshape(n, lp, hq, dh)[:, :seq].astype(q.dtype)
    lse = lse.transpose(1, 0, 2, 3, 4).reshape(n, lp, hq)[:, :seq]
    return o, lse


def _dilated_attention(q, k, v, window, dilation):
    n, seq, h, dh = q.shape
    r = dilation

    def sub(t):
        return t.reshape(n, seq // r, r, h, dh).transpose(0, 2, 1, 3, 4).reshape(n * r, seq // r, h, dh)

    o, lse = _banded_attention(sub(q), sub(k), sub(v), (window // 2) // r)
    o = o.reshape(n, r, seq // r, h, dh).transpose(0, 2, 1, 3, 4).reshape(n, seq, h, dh)
    lse = lse.reshape(n, r, seq // r, h).transpose(0, 2, 1, 3).reshape(n, seq, h)
    return o, lse


def _mixer_sublayer(x, cos, sin, w_in, b_gate, a_sink, w_branch_a, w_branch_b, w_out):
    bsz, seq, _ = x.shape
    proj = jnp.einsum('bsd,de->bse', x, w_in)
    qa, ka, va, qb, kb, vb, gl = jnp.split(proj, IN_SPLITS, axis=-1)

    qa = _rope(qa.reshape(bsz, seq, A_Q_HEADS, HEAD_DIM), cos, sin)
    ka = _rope(ka.reshape(bsz, seq, A_KV_HEADS, HEAD_DIM), cos, sin)
    va = va.reshape(bsz, seq, A_KV_HEADS, HEAD_DIM)
    oa, _ = _banded_attention(qa, ka, va, A_WINDOW, a_sink)
    ya = jnp.einsum('bse,ed->bsd', oa.reshape(bsz, seq, A_Q_W), w_branch_a)

    qb = _rope(qb.reshape(bsz, seq, B_HEADS, HEAD_DIM), cos, sin)
    kb = _rope(kb.reshape(bsz, seq, B_HEADS, HEAD_DIM), cos, sin)
    vb = vb.reshape(bsz, seq, B_HEADS, HEAD_DIM)
    outs, lses = [], []
    for g, (window, dil) in enumerate(B_GROUPS):
        hs = slice(g * B_HEADS_PER_GROUP, (g + 1) * B_HEADS_PER_GROUP)
        o, lse = _dilated_attention(qb[:, :, hs], kb[:, :, hs], vb[:, :, hs], window, dil)
        outs.append(o)
        lses.append(lse)
    wts = jax.nn.softmax(jnp.stack(lses), axis=0)
    ob = jnp.einsum('gbsh,gbshd->bshd', wts, jnp.stack(outs).astype(jnp.float32)).astype(x.dtype)
    yb = jnp.einsum('bse,ed->bsd', ob.reshape(bsz, seq, B_OUT_W), w_branch_b)

    gates = jax.nn.sigmoid((gl + b_gate).astype(jnp.float32)).astype(x.dtype)
    ga, gb = jnp.split(gates, N_BRANCHES, axis=-1)
    return jnp.einsum('bsd,de->bse', ga * ya + gb * yb, w_out)


def setup_inputs(seed: int = 0) -> dict:
    key = jax.random.key(seed)
    ks = jax.random.split(key, 24)
    L, D = DEPTH, D_MODEL
    beta = DEEPNORM_BETA

    def dense(k, shape, fan_in, scale=1.0):
        return jax.random.normal(k, shape, jnp.float32) * (scale * fan_in ** -0.5)

    x = jax.random.normal(ks[0], (BATCH, SEQ, D), jnp.float32)
    p = jax.random.normal(ks[1], (DEPTH, BATCH, SEQ, PLE_DIM), jnp.float32)
    w_in = jnp.concatenate([
        dense(ks[2], (L, D, A_Q_W), D),
        dense(ks[3], (L, D, A_KV_W), D),
        dense(ks[4], (L, D, A_KV_W), D, beta),
        dense(ks[5], (L, D, B_W), D),
        dense(ks[6], (L, D, B_W), D),
        dense(ks[7], (L, D, B_W), D, beta),
        dense(ks[8], (L, D, GATE_W), D),
    ], axis=-1)
    b_gate = 0.1 * jax.random.normal(ks[9], (L, GATE_W), jnp.float32)
    a_sink = 0.5 * jax.random.normal(ks[10], (L, A_Q_HEADS), jnp.float32)
    w_branch_a = dense(ks[11], (L, A_Q_W, D), A_Q_W, beta)
    w_branch_b = dense(ks[12], (L, B_OUT_W, D), B_OUT_W, beta)
    w_out = dense(ks[13], (L, D, D), D, beta)
    ln1_g = 1.0 + 0.02 * jax.random.normal(ks[14], (L, D), jnp.float32)
    ln1_b = 0.02 * jax.random.normal(ks[15], (L, D), jnp.float32)
    w_up = dense(ks[16], (L, D, D_FF), D)
    w_down = dense(ks[17], (L, D_FF, D), D_FF, beta)
    w_ple_gate = dense(ks[18], (L, D, D), D)
    b_ple_gate = 0.1 * jax.random.normal(ks[19], (L, D), jnp.float32)
    w_ple = dense(ks[20], (L, PLE_DIM, D), PLE_DIM, beta)
    ln2_g = 1.0 + 0.02 * jax.random.normal(ks[21], (L, D), jnp.float32)
    ln2_b = 0.02 * jax.random.normal(ks[22], (L, D), jnp.float32)
    return {"x": x, "p": p, "w_in": w_in, "b_gate": b_gate, "a_sink": a_sink,
            "w_branch_a": w_branch_a, "w_branch_b": w_branch_b, "w_out": w_out,
            "ln1_g": ln1_g, "ln1_b": ln1_b, "w_up": w_up, "w_down": w_down,
            "w_ple_gate": w_ple_gate, "b_ple_gate": b_ple_gate, "w_ple": w_ple,
            "ln2_g": ln2_g, "ln2_b": ln2_b}


def reference(x, p, w_in, b_gate, a_sink, w_branch_a, w_branch_b, w_out,
              ln1_g, ln1_b, w_up, w_down, w_ple_gate, b_ple_gate, w_ple,
              ln2_g, ln2_b):
    cos, sin = _rope_tables(x.shape[1])
    for i in range(DEPTH):
        h = _mixer_sublayer(x, cos, sin, w_in[i], b_gate[i], a_sink[i],
                            w_branch_a[i], w_branch_b[i], w_out[i])
        x = _layer_norm(DEEPNORM_ALPHA * x + h, ln1_g[i], ln1_b[i])
        mlp = jnp.einsum('bsf,fd->bsd', jnp.square(jax.nn.relu(jnp.einsum('bsd,df->bsf', x, w_up[i]))), w_down[i])
        gate = jax.nn.sigmoid((jnp.einsum('bsd,de->bse', x, w_ple_gate[i]) + b_ple_gate[i]).astype(jnp.float32)).astype(x.dtype)
        ple = gate * jnp.einsum('bsr,rd->bsd', p[i], w_ple[i])
        x = _layer_norm(DEEPNORM_ALPHA * x + mlp + ple, ln2_g[i], ln2_b[i])
    return x
```

```python
import contextlib
import numpy as np
import ml_dtypes
import concourse.bass as bass
import concourse.mybir as mybir
from concourse.bass_utils import run_bass_kernel_spmd

F32 = mybir.dt.float32
BF = mybir.dt.bfloat16
AF = mybir.ActivationFunctionType
ALU = mybir.AluOpType

ENGS = ("pe", "act", "dve", "pool", "sp")
S_LEN = 2048
NT = 16
D = 1024
ALPHA = 8.0 ** 0.25
EPS_P = 1e-5 / (ALPHA * ALPHA)
C_HALF = 0.5 / ALPHA
RELU_S = ALPHA ** -0.5
A_PERM = [0, 4, 1, 5, 2, 6, 3, 7]
GROUPS = [("A", 128, 1), ("g0", 64, 1), ("g1", 256, 4), ("g2", 1024, 16)]
GD = {"A": 1, "g0": 1, "g1": 2, "g2": 8}


class Sched:
    def __init__(self, nc):
        self.nc = nc
        self.ops = {e: [] for e in ENGS}
        self.lastw = {}
        self.readers = {}
        self.slot_cnt = {}
        self.slots = []

    def op(self, eng, fn, r=(), w=(), dma=None):
        idx = len(self.ops[eng])
        w = list(w) + [("psr",) + tuple(k[1:]) for k in r if isinstance(k, tuple) and k[0] == "ps"]
        deps = set()
        for k in r:
            t = self.lastw.get(k)
            if t is not None:
                deps.add((t, "raw"))
        for k in w:
            t = self.lastw.get(k)
            if t is not None:
                deps.add((t, "waw"))
            for t in self.readers.get(k, ()):
                deps.add((t, "war"))
        if dma is not None:
            if dma not in self.slot_cnt:
                self.slot_cnt[dma] = 0
                self.slots.append(dma)
            self.slot_cnt[dma] += 1
            tok = ("d", dma, self.slot_cnt[dma])
        else:
            tok = ("c", eng, idx)
        for k in w:
            self.lastw[k] = tok
            self.readers[k] = []
        for k in r:
            self.readers.setdefault(k, []).append(tok)
        self.ops[eng].append(dict(fn=fn, deps=deps, dma=dma, sig=False))
        return tok

    def emit(self, final_slots=()):
        nc = self.nc
        for e in ENGS:
            for i, o in enumerate(self.ops[e]):
                lst = {}
                for (t, kind) in o["deps"]:
                    if t[0] == "c":
                        _, pe_, pi = t
                        if pe_ == e and o["dma"] is None:
                            if pe_ == "pe":
                                continue
                            if kind != "raw" or pi < i - 2:
                                continue
                        k = ("c", pe_)
                        lst[k] = max(lst.get(k, -1), pi)
                    else:
                        _, slot, cnt = t
                        k = ("d", slot)
                        lst[k] = max(lst.get(k, -1), cnt)
                o["need"] = lst
                for k, v in lst.items():
                    if k[0] == "c":
                        self.ops[k[1]][v]["sig"] = True
        cum = {}
        for e in ENGS:
            c = 0
            arr = []
            for o in self.ops[e]:
                if o["sig"] and o["dma"] is None:
                    c += 1
                arr.append(c)
            cum[e] = arr
        with contextlib.ExitStack() as st:
            esem = {e: st.enter_context(nc.semaphore("s_" + e)) for e in ENGS}
            dsem = {s: st.enter_context(nc.semaphore("d_%d" % i)) for i, s in enumerate(self.slots)}
            block = st.enter_context(nc.Block())

            def run(e, eng):
                waited = {}
                for o in self.ops[e]:
                    for k, v in o["need"].items():
                        if k[0] == "c":
                            val = cum[k[1]][v]
                            sem = esem[k[1]]
                        else:
                            val = 16 * v
                            sem = dsem[k[1]]
                        if waited.get(k, 0) < val:
                            eng.wait_ge(sem, val)
                            waited[k] = val
                    ins = o["fn"](eng)
                    if o["dma"] is not None:
                        ins.then_inc(dsem[o["dma"]], 16)
                    elif o["sig"]:
                        ins.then_inc(esem[e], 1)
                if e == "sp":
                    for s in final_slots:
                        eng.wait_ge(dsem[s], 16 * self.slot_cnt[s])

            @block.tensor
            def _(eng):
                run("pe", eng)

            @block.scalar
            def _(eng):
                run("act", eng)

            @block.vector
            def _(eng):
                run("dve", eng)

            @block.gpsimd
            def _(eng):
                run("pool", eng)

            @block.sync
            def _(eng):
                run("sp", eng)


def _qkv_cols():
    def head(base, h):
        return list(range(base + h * 64, base + (h + 1) * 64))
    chunks = []
    for (h0, h1) in [(0, 4), (1, 5), (2, 6), (3, 7)]:
        chunks.append(head(0, h0) + head(0, h1))
    for i in range(6):
        chunks.append(head(768, 2 * i) + head(768, 2 * i + 1))
    chunks.append(head(512, 0) + head(512, 1))
    for i in range(6):
        chunks.append(head(1536, 2 * i) + head(1536, 2 * i + 1))
    blocks = []
    for (a, b) in [(0, 4), (4, 8), (8, 11), (11, 15), (15, 17)]:
        blocks.append(sum(chunks[a:b], []))
    vheads = [head(640, 0), head(640, 1)] + [head(2304, i) for i in range(12)]
    blocks.append(sum(vheads[0:8], []))
    blocks.append(sum(vheads[8:14], []))
    return blocks


QKV_BLOCKS = _qkv_cols()
QK_CHUNK_RANGES = [(0, 4), (4, 8), (8, 11), (11, 15), (15, 17)]


def piece_table():
    tab = {}
    off = 0

    def add(name, kc, ncols):
        nonlocal off
        tab[name] = (off, kc, ncols)
        off += kc * ncols
    for b in range(7):
        for kh in range(2):
            add(("qkv", b, kh), 4, len(QKV_BLOCKS[b]))
    for cb in range(4):
        for kh in range(2):
            add(("wg", cb, kh), 4, 512)
    for cb in range(2):
        add(("wa", cb), 4, 512)
    add(("wb",), 2, 1024)
    for cb in range(2):
        for kh in range(2):
            add(("wo", cb, kh), 4, 512)
    for fb in range(8):
        for kh in range(2):
            add(("wu", fb, kh), 4, 512)
    for cb in range(2):
        for fg in range(8):
            add(("wd", cb, fg), 4, 512)
    for cb in range(2):
        for kh in range(2):
            add(("wpg", cb, kh), 4, 512)
    add(("wp",), 2, 1024)
    return tab, off


PIECES, WTOT = piece_table()


def _pm(W):
    K, C = W.shape
    return np.ascontiguousarray(W.reshape(K // 128, 128, C).transpose(1, 0, 2))


def pack_weights(inp):
    out = np.empty((4, 128, WTOT), np.float32)
    for l in range(4):
        def put(name, arr):
            off, kc, nc_ = PIECES[name]
            out[l, :, off:off + kc * nc_] = arr.reshape(128, kc * nc_)
        w_in = inp["w_in"][l]
        for b in range(7):
            wp = _pm(w_in[:, QKV_BLOCKS[b]])
            for kh in range(2):
                put(("qkv", b, kh), wp[:, kh * 4:(kh + 1) * 4, :])
        wg = _pm(w_in[:, 3072:5120])
        for cb in range(4):
            for kh in range(2):
                put(("wg", cb, kh), wg[:, kh * 4:(kh + 1) * 4, cb * 512:(cb + 1) * 512])
        rows = sum([list(range(h * 64, (h + 1) * 64)) for h in A_PERM], [])
        wa = _pm(inp["w_branch_a"][l][rows, :])
        for cb in range(2):
            put(("wa", cb), wa[:, :, cb * 512:(cb + 1) * 512])
        put(("wb",), _pm(inp["w_branch_b"][l]))
        wo = _pm(inp["w_out"][l])
        for cb in range(2):
            for kh in range(2):
                put(("wo", cb, kh), wo[:, kh * 4:(kh + 1) * 4, cb * 512:(cb + 1) * 512])
        wu = _pm(inp["w_up"][l])
        for fb in range(8):
            for kh in range(2):
                put(("wu", fb, kh), wu[:, kh * 4:(kh + 1) * 4, fb * 512:(fb + 1) * 512])
        wd = _pm(inp["w_down"][l])
        for cb in range(2):
            for fg in range(8):
                put(("wd", cb, fg), wd[:, fg * 4:(fg + 1) * 4, cb * 512:(cb + 1) * 512])
        wpg = _pm(inp["w_ple_gate"][l])
        for cb in range(2):
            for kh in range(2):
                put(("wpg", cb, kh), wpg[:, kh * 4:(kh + 1) * 4, cb * 512:(cb + 1) * 512])
        put(("wp",), _pm(inp["w_ple"][l]))
    return out


def make_consts(inp, lastl=3):
    c = {}
    c["identf"] = np.eye(128, dtype=np.float32)
    kl = np.arange(128)[:, None]
    ql = np.arange(128)[None, :]
    cols = []
    for (name, W, r) in GROUPS:
        Dm = GD[name]
        dls = range(-Dm, Dm + 1) if name != "g2" else [-8, 0, 0, 0, 0, 8]
        for dl in dls:
            diff = 128 * dl + kl - ql
            cols.append((((np.abs(diff) <= W) & ((kl - ql) % r == 0)).astype(np.float32) - 1.0) * 30000.0)
    c["masks"] = np.concatenate(cols, axis=1).astype(ml_dtypes.bfloat16)
    pos = np.arange(S_LEN, dtype=np.float32)
    inv = (np.float32(500000.0) ** (-np.arange(0, 16, 2, dtype=np.float32) / np.float32(16))).astype(np.float32)
    ang = (pos[:, None] * inv[None, :]).astype(np.float32)
    c["cos"] = np.ascontiguousarray(np.broadcast_to(np.cos(ang).astype(np.float32).reshape(16, 128, 1, 8).transpose(1, 0, 2, 3), (128, 16, 4, 8)))
    c["sin"] = np.ascontiguousarray(np.broadcast_to(np.sin(ang).astype(np.float32).reshape(16, 128, 1, 8).transpose(1, 0, 2, 3), (128, 16, 4, 8)))
    def fm(v, nch):
        return np.ascontiguousarray(v.reshape(4, nch, 128).transpose(2, 0, 1))
    c["bg"] = fm(inp["b_gate"], 16)
    c["g1"] = fm(inp["ln1_g"], 8)
    c["b1"] = fm(inp["ln1_b"], 8)
    c["g2"] = fm(inp["ln2_g"], 8)
    c["b2"] = fm(inp["ln2_b"], 8)
    c["bpgb"] = np.ascontiguousarray(np.broadcast_to(inp["b_ple_gate"][:, None, :], (4, 128, 1024)))
    c["sink"] = np.ascontiguousarray(np.broadcast_to(inp["a_sink"][:, A_PERM].reshape(1, 32), (128, 32)))
    c["lnfg"] = np.ascontiguousarray(np.broadcast_to(inp["ln2_g"][lastl][None, :], (128, 1024)))
    c["lnfb"] = np.ascontiguousarray(np.broadcast_to(inp["ln2_b"][lastl][None, :], (128, 1024)))
    return c


CONST_SHAPES = {
    "identf": ([128, 128], F32), "masks": ([128, 17 * 128], BF), "cos": ([128, 16, 4, 8], F32),
    "sin": ([128, 16, 4, 8], F32), "bg": ([128, 4, 16], F32), "g1": ([128, 4, 8], F32),
    "b1": ([128, 4, 8], F32), "g2": ([128, 4, 8], F32), "b2": ([128, 4, 8], F32),
    "bpgb": ([4, 128, 1024], F32), "sink": ([128, 32], F32), "lnfg": ([128, 1024], F32),
    "lnfb": ([128, 1024], F32),
}


def build(nlayers=4, nseq=2):
    import os as _os
    nc = bass.Bass("TRN2", target_bir_lowering=False)
    S = Sched(nc)
    x_d = nc.dram_tensor("x", [nseq, S_LEN, D], F32, kind="ExternalInput").ap()
    p_d = nc.dram_tensor("p", [4, nseq, S_LEN, 256], F32, kind="ExternalInput").ap()
    w_d = nc.dram_tensor("w", [4, 128, WTOT], F32, kind="ExternalInput").ap()
    out_d = nc.dram_tensor("out", [nseq, S_LEN, D], F32, kind="ExternalOutput").ap()
    cd = {k: nc.dram_tensor("c_" + k, sh, dt, kind="ExternalInput").ap() for k, (sh, dt) in CONST_SHAPES.items()}
    KDBG = _os.environ.get("KDBG", "")
    dbg_d = nc.dram_tensor("dbg", [S_LEN, D], F32, kind="ExternalOutput").ap() if KDBG else None

    def dbg_dump(tag, t, ap, key, ncols=1024):
        if KDBG == tag:
            keys = key if isinstance(key, list) else [key]
            S.op("sp", I("dma_start", out=dbg_d[t * 128:(t + 1) * 128, 0:ncols], in_=ap), r=keys, w=[("dbg", t)], dma=("dbg", t % 4))

    def sb(name, shape, dt):
        return nc.alloc_sbuf_tensor(name, shape, dt).ap()

    xh = sb("xh", [128, 8, S_LEN], BF)
    xl = sb("xl", [128, 8, S_LEN], BF)
    RSZ = 49408
    R = sb("R", [128, RSZ], BF)
    wbuf = [sb("wbuf0", [128, 8, 512], BF),
            R[:, 40960:45056].rearrange("p (k c) -> p k c", c=512), R[:, 45056:49152].rearrange("p (k c) -> p k c", c=512)]
    NSTG = 3
    stg = [sb("stg%d" % i, [128, 512], F32) for i in range(NSTG)]
    wscr = nc.dram_tensor("wscr", [4, 128, WTOT], BF, kind="Internal").ap()
    T = sb("T", [128, 9216], BF)
    cs = {k: sb("k_" + k, sh, dt) for k, (sh, dt) in CONST_SHAPES.items() if k not in ("lnfg", "lnfb", "bpgb")}
    identb = sb("identb", [128, 128], BF)
    bgh = sb("bgh", [128, 4, 16], F32)
    esink = sb("esink", [128, 32], F32)
    mhalf = sb("mhalf", [128, 1], F32)
    onesc = sb("onesc", [128, 2], BF)
    tq = [T[:, i * 512:(i + 1) * 512] for i in range(2)] + [T[:, 7168:7680]]
    rp = [T[:, o_:o_ + 512].bitcast(F32).rearrange("p (a h d) -> p a h d", a=4, h=8) for o_ in (1024, 1536, 7680)]
    NPT = 6
    PT = [T[:, 2048 + i * 512:2048 + (i + 1) * 512] for i in range(3)] + [T[:, 5120 + i * 512:5120 + (i + 1) * 512] for i in range(3)]
    oab = [T[:, 3584 + i * 768:3584 + (i + 1) * 768].rearrange("p (h d) -> p h d", d=64) for i in range(2)]
    f512 = [T[:, i * 1024:(i + 1) * 1024].bitcast(F32) for i in range(3)]
    xf = [T[:, 3072:5120].bitcast(F32).rearrange("p (c t) -> p c t", t=128)]
    zt = [T[:, 5120 + i * 2048:5120 + (i + 1) * 2048].bitcast(F32) for i in range(2)]
    tfr = [T[:, o_:o_ + 256].bitcast(F32).rearrange("p (h d) -> p h d", d=16) for o_ in (6656, 6912, 8192)]
    TA = [("tfr", i) for i in range(3)] + [("tq", i) for i in range(3)] + [("rp", i, j) for i in range(3) for j in range(4)] + [("PT", i) for i in range(6)] + [("oab", i, b) for i in range(2) for b in range(3)]
    TB = [("f", i) for i in range(3)] + [("xf", 0, h) for h in range(2)] + [("zt", i) for i in range(2)]
    sm = [sb("sm%d" % i, [128, 32], F32) for i in range(4)]
    pin = [sb("pin%d" % i, [128, 256], F32) for i in range(2)]
    pT = [sb("pT%d" % i, [128, 2, 128], BF) for i in range(4)]
    psb = [nc.alloc_psum_tensor("ps%d" % i, [128, 512], F32).ap() for i in range(8)]

    qT = R[:, 0:10 * 2048].rearrange("p (c t) -> p c t", t=2048)
    kT = R[:, 20480:20480 + 7 * 2048].rearrange("p (c t) -> p c t", t=2048)
    vv = R[:, 34816:34816 + 16 * 14 * 65].rearrange("p (t h d) -> p t h d", h=14, d=65)
    vflat = R[:, 34816:34816 + 16 * 14 * 65].rearrange("p (n d) -> p n d", d=65)
    o2 = 12288
    wg = R[:, o2:o2 + 8 * 2048].rearrange("p (k c) -> p k c", c=2048)
    wa = R[:, o2 + 16384:o2 + 16384 + 4 * 1024].rearrange("p (k c) -> p k c", c=1024)
    wb = R[:, o2 + 20480:o2 + 20480 + 2 * 1024].rearrange("p (k c) -> p k c", c=1024)
    wo = R[:, o2 + 22528:o2 + 22528 + 8 * 1024].rearrange("p (k c) -> p k c", c=1024)
    mg = R[:, o2 + 30720:o2 + 30720 + 8 * 512].rearrange("p (k c) -> p k c", c=512)
    uT = R[:, 0:32 * 512].rearrange("p (f t) -> p f t", t=512)
    yb = R[:, 16384:16384 + 8192].bitcast(F32).rearrange("p (t c) -> p t c", c=1024)
    wpg = R[:, 24576:24576 + 8 * 1024].rearrange("p (k c) -> p k c", c=1024)
    wpl = R[:, 32768:32768 + 2 * 1024].rearrange("p (k c) -> p k c", c=1024)
    lnfg = R[:, 34816:34816 + 2048].bitcast(F32)
    lnfb = R[:, 36864:36864 + 2048].bitcast(F32)
    bpgbv = R[:, 38912:38912 + 2048].bitcast(F32)
    ztb = [zt[0], zt[1], R[:, 47104:49152].bitcast(F32)]
    RKEYS_A = [("qT", t) for t in range(NT)] + [("kT", t) for t in range(NT)] + [("v", t) for t in range(NT)]
    RKEYS_B = ([(("wg", i), k) for i in range(4) for k in range(8)] + [(("wa", i), k) for i in range(2) for k in range(4)]
               + [(("wo", i), k) for i in range(2) for k in range(8)] + [("wb", k) for k in range(2)] + ["mg", ("zt", 2)])
    RKEYS_C = ["uT", "lnfg", "lnfb", "bpgb"] + [(("wpg", i), k) for i in range(2) for k in range(8)] + [("wpl", k) for k in range(2)] + [(("wbuf", i), k) for i in (1, 2) for k in range(8)] + [("yb", i, j) for i in range(4) for j in range(2)]

    state = {"zb": 0, "ce": 0, "bank": 0, "stg": 0, "wb": 0, "f": 0, "tq": 0, "PT": 0, "z": 0, "xf": 0, "sm": 0, "pin": 0, "oab": 0}

    def rot(name, n):
        i = state[name]
        state[name] = (i + 1) % n
        return i

    pool_ = {"l": list(range(8)), "i": 0}

    def set_pool(lst):
        pool_["l"] = list(lst)
        pool_["i"] = 0

    def bank():
        b = pool_["l"][pool_["i"] % len(pool_["l"])]
        pool_["i"] += 1
        return b

    def I(meth, *a, **kw):
        return lambda e: getattr(e, meth)(*a, **kw)

    def PS(i):
        return psb[i], ("ps", i)

    def mm(out, lhsT, rhs, start, stop, r, w):
        S.op("pe", I("matmul", out, lhsT=lhsT, rhs=rhs, start=start, stop=stop, skip_group_check=True), r=r, w=w)

    def tr(out, in_, ident, r, w):
        S.op("pe", I("transpose", out=out, in_=in_, identity=ident), r=r, w=w)

    def fence(old, new):
        if _os.environ.get("KOFF", "").find("fence") < 0:
            S.op("pool", I("nop", ), w=list(old) + list(new))

    def load_piece(l, name, dst, dkey):
        off, kc, ncol = PIECES[name]
        for k in range(kc):
            for c0 in range(0, ncol, 512):
                n = min(512, ncol - c0)
                si = rot("stg", 2)
                o = off + k * ncol + c0
                S.op("sp", I("dma_start", out=stg[si][:, 0:n], in_=w_d[l, :, o:o + n]),
                     w=[("stg", si)], dma=("stg", si))
                S.op("pool", I("tensor_copy", out=dst[:, k, c0:c0 + n], in_=stg[si][:, 0:n]),
                     r=[("stg", si)], w=[dkey])

    def load_block(l, pieces, dst, dkey, first):
        off0, _, ncol = PIECES[pieces[0]]
        K = sum(PIECES[p_][1] for p_ in pieces)
        skey = ("scr", l, pieces[0])
        sview = wscr[l, :, off0:off0 + K * ncol].rearrange("p (k c) -> p k c", c=ncol)
        if not first:
            S.op("sp", I("dma_start", out=dst, in_=sview), r=[skey], w=[(dkey, k) for k in range(K)], dma=("ld", dkey))
            return
        for k in range(K):
            for c0 in range(0, ncol, 512):
                n = min(512, ncol - c0)
                si = rot("stg", NSTG)
                o = off0 + k * ncol + c0
                S.op("sp", I("dma_start", out=stg[si][:, 0:n], in_=w_d[l, :, o:o + n]), w=[("stg", si)], dma=("stg", si))
                if rot("ce", 2) == 0:
                    S.op("dve", I("tensor_copy", out=dst[:, k, c0:c0 + n], in_=stg[si][:, 0:n]), r=[("stg", si)], w=[(dkey, k)])
                else:
                    S.op("act", I("activation", out=dst[:, k, c0:c0 + n], in_=stg[si][:, 0:n], func=AF.Copy), r=[("stg", si)], w=[(dkey, k)])
        S.op("sp", I("dma_start", out=sview, in_=dst), r=[(dkey, k) for k in range(K)], w=[skey], dma=("scrw", dkey))

    for k in cs:
        S.op("sp", I("dma_start", out=cs[k], in_=cd[k]), w=["c_" + k], dma="c_" + k)
    S.op("dve", I("tensor_copy", out=identb, in_=cs["identf"]), r=["c_identf"], w=["identb"])
    S.op("dve", I("memset", mhalf, -0.5), w=["mhalf"])
    S.op("dve", I("memset", onesc, 1.0), w=["onesc"])
    S.op("dve", I("tensor_scalar", out=bgh, in0=cs["bg"], scalar1=0.5, scalar2=None, op0=ALU.mult), r=["c_bg"], w=["bgh"])
    S.op("act", I("activation", out=esink, in_=cs["sink"], func=AF.Exp), r=["c_sink"], w=["esink"])
    CK = ["identb", "c_identf"]

    def to_hilo(t, src, skey, gb):
        xi = rot("xf", 1)
        skeys = skey if isinstance(skey, list) else [skey]
        for half in range(2):
            b = bank()
            pb, pk = PS(b)
            for c in range(4):
                tr(pb[:, c * 128:(c + 1) * 128], src[:, (half * 4 + c) * 128:(half * 4 + c + 1) * 128], cs["identf"],
                   r=skeys + ["c_identf"], w=[pk])
            if gb is None:
                S.op("act", I("activation",
                    out=xf[xi][:, half * 4:(half + 1) * 4, :], in_=pb.rearrange("p (c t) -> p c t", t=128), func=AF.Copy),
                    r=[pk], w=[("xf", xi, half)])
            else:
                g, bb, l = gb
                for c in range(4):
                    cc = half * 4 + c
                    S.op("dve", I("tensor_scalar",
                        out=xf[xi][:, cc, :], in0=pb[:, c * 128:(c + 1) * 128], scalar1=cs[g][:, l, cc:cc + 1],
                        scalar2=cs[bb][:, l, cc:cc + 1], op0=ALU.mult, op1=ALU.add),
                        r=[pk, "c_" + g, "c_" + bb], w=[("xf", xi, half)])
        ts = slice(t * 128, (t + 1) * 128)
        S.op("act", I("activation", out=xh[:, :, ts], in_=xf[xi], func=AF.Copy),
             r=[("xf", xi, 0), ("xf", xi, 1)], w=[("xh", t)])
        S.op("pool", I("tensor_tensor", out=xl[:, :, ts], in0=xf[xi], in1=xh[:, :, ts], op=ALU.subtract),
             r=[("xf", xi, 0), ("xf", xi, 1), ("xh", t)], w=[("xl", t)])

    def layer_norm_stats(src_aps, skeys):
        si = rot("sm", 4)
        st = sm[si]
        for i, (a, k) in enumerate(zip(src_aps, skeys)):
            S.op("dve", I("bn_stats", out=st[:, i * 6:(i + 1) * 6], in_=a), r=[k], w=[("sm", si, i)])
        S.op("dve", I("bn_aggr", out=st[:, 12:14], in_=st[:, 0:12]), r=[("sm", si, 0), ("sm", si, 1)], w=[("sm", si, 2)])
        S.op("dve", I("tensor_scalar", out=st[:, 14:15], in0=st[:, 13:14], scalar1=EPS_P, scalar2=None, op0=ALU.add),
             r=[("sm", si, 2)], w=[("sm", si, 3)])
        S.op("pool", I("tensor_tensor", out=st[:, 15:16], in0=st[:, 14:15], in1=mhalf, op=ALU.pow),
             r=[("sm", si, 3), "mhalf"], w=[("sm", si, 4)])
        S.op("dve", I("tensor_scalar", out=st[:, 16:17], in0=st[:, 12:13], scalar1=-1.0, scalar2=st[:, 15:16],
                                              op0=ALU.mult, op1=ALU.mult), r=[("sm", si, 2), ("sm", si, 4)], w=[("sm", si, 5)])
        return st[:, 15:16], st[:, 16:17], [("sm", si, 4), ("sm", si, 5)]

    import os as _os
    STOP = _os.environ.get("KSTOP", "")

    class _Stop(Exception):
        pass

    KOFF = _os.environ.get("KOFF", "").split(",")

    def on(name):
        return name not in KOFF

    def stop_at(name):
        if STOP == name:
            raise _Stop()
    try:
      for sq in range(nseq):
          for t in range(NT):
              zi = rot("z", 2)
              S.op("sp", I("dma_start", out=zt[zi], in_=x_d[sq, t * 128:(t + 1) * 128, :]),
                   w=[("zt", zi)], dma=("zt", zi))
              to_hilo(t, zt[zi], ("zt", zi), None)

          for l in range(nlayers):
              last = (l == nlayers - 1)
              stop_at('load')
              fence(RKEYS_C + TB, RKEYS_A + TA)
              set_pool(range(8))
              if on("ones"):
                  S.op("dve", I("tensor_copy", out=vflat[:, :, 64:65], in_=onesc[:, 0:1].unsqueeze(1).to_broadcast([128, 224, 1])),
                       r=["onesc"], w=[("v", t) for t in range(NT)])
              p1pend = []
              for b in range(7):
                  ncol = len(QKV_BLOCKS[b])
                  wi = 0
                  load_block(l, [("qkv", b, 0), ("qkv", b, 1)], wbuf[wi][:, :, 0:ncol], ("wbuf", wi), sq == 0)
                  for t in range(NT):
                      ts = slice(t * 128, (t + 1) * 128)
                      bk = bank()
                      pb, pk = PS(bk)
                      for k in range(8):
                          mm(pb[:, 0:ncol], xh[:, k, ts], wbuf[wi][:, k, 0:ncol], k == 0, k == 7,
                             r=[("xh", t), (("wbuf", wi), k)], w=[pk])
                      if not on("evac"):
                          continue
                      if b >= 5:
                          h0, nh = (0, 8) if b == 5 else (8, 6)
                          S.op("act", I("activation",
                              out=vv[:, t, h0:h0 + nh, 0:64], in_=pb[:, 0:ncol].rearrange("p (h d) -> p h d", d=64), func=AF.Copy),
                              r=[pk], w=[("v", t)])
                          continue
                      nh = ncol // 64
                      qi = rot("tq", 3)
                      tqv = tq[qi][:, 0:ncol].rearrange("p (h d) -> p h d", d=64)
                      p3 = pb[:, 0:ncol].rearrange("p (h d) -> p h d", d=64)
                      S.op("act", I("activation", out=tqv, in_=p3, func=AF.Copy), r=[pk], w=[("tq", qi)])
                      tf = tfr[qi]
                      S.op("act", I("activation", out=tf[:, 0:nh, :], in_=p3[:, :, 0:16], func=AF.Copy), r=[pk], w=[("tfr", qi)])
                      if not on("rope"):
                          continue
                      rr = rp[qi]
                      for h0_ in range(0, nh, 4):
                          hn = min(4, nh - h0_)
                          for j, (lo, tabn) in enumerate([(0, "cos"), (8, "sin"), (8, "cos"), (0, "sin")]):
                              S.op("dve", I("tensor_tensor",
                                  out=rr[:, j, h0_:h0_ + hn, :], in0=tf[:, h0_:h0_ + hn, lo:lo + 8], in1=cs[tabn][:, t, 0:hn, :], op=ALU.mult),
                                  r=[("tfr", qi), "c_cos", "c_sin"], w=[("rp", qi, j)])
                      S.op("dve", I("tensor_tensor",
                          out=tqv[:, :, 0:8], in0=rr[:, 0, 0:nh, :], in1=rr[:, 1, 0:nh, :], op=ALU.subtract),
                          r=[("rp", qi, 0), ("rp", qi, 1)], w=[("tq", qi)])
                      S.op("dve", I("tensor_tensor",
                          out=tqv[:, :, 8:16], in0=rr[:, 2, 0:nh, :], in1=rr[:, 3, 0:nh, :], op=ALU.add),
                          r=[("rp", qi, 2), ("rp", qi, 3)], w=[("tq", qi)])
                      def p1_tail(b=b, t=t, ts=ts, qi=qi):
                          c0, c1 = QK_CHUNK_RANGES[b]
                          bt = bank()
                          pt_, ptk = PS(bt)
                          ptb = pt_.bitcast(BF)
                          for i in range(c1 - c0):
                              tr(ptb[:, i * 128:(i + 1) * 128], tq[qi][:, i * 128:(i + 1) * 128], identb, r=[("tq", qi), "identb"], w=[ptk])
                          for (a, bnd, dst, dk, base) in [(c0, min(c1, 10), qT, "qT", 0), (max(c0, 10), c1, kT, "kT", 10)]:
                              if bnd > a:
                                  S.op("act", I("activation",
                                      out=dst[:, a - base:bnd - base, ts],
                                      in_=ptb[:, (a - c0) * 128:(bnd - c0) * 128].rearrange("p (c t) -> p c t", t=128), func=AF.Copy),
                                      r=[ptk], w=[(dk, t)])
                      p1pend.append(p1_tail)
                      while len(p1pend) > 2:
                          p1pend.pop(0)()
              while p1pend:
                  p1pend.pop(0)()

              stop_at('p1')
              moff = {}
              o_ = 0
              for (name, W, r_) in GROUPS:
                  moff[name] = o_
                  o_ += (2 * GD[name] + 1) if name != "g2" else 6
              set_pool(range(4))
              p2pend = []
              lagq = []
              started = set()

              def stage(f2, lag):
                  lagq.append(f2)
                  while len(lagq) > lag:
                      lagq.pop(0)()
              o2T = [tq[0], tq[1], T[:, 1024:1536], T[:, 1536:2048]]
              o2k = [[("tq", 0)], [("tq", 1)], [("rp", 0, q_) for q_ in range(4)], [("rp", 1, q_) for q_ in range(4)]]
              for j in range(NT):
                  js = slice(j * 128, (j + 1) * 128)
                  oi = rot("oab", 2)
                  accs = [5, 6, 7]
                  si = rot("sm", 4)
                  st = sm[si]
                  if j % 4 == 0:
                      for s_ in range(4):
                          i_ = 8 + s_
                          qc2, kc2, hf2, vh2 = 4 + i_ // 2, 1 + i_ // 2, i_ % 2, 2 + i_
                          rows = slice(hf2 * 64, (hf2 + 1) * 64)
                          pa2, pa2k = PS(4)
                          first2 = True
                          for kt in range(max(0, j - 8), min(NT - 1, j + 3 + 8) + 1):
                              qa, qb = max(j, kt - 8), min(j + 3, kt + 8) + 1
                              n = qb - qa
                              pb, pk = PS(bank())
                              mm(pb[:, 0:n * 128], kT[rows, kc2, kt * 128:(kt + 1) * 128], qT[rows, qc2, qa * 128:qb * 128],
                                 True, False, r=[("kT", kt)] + [("qT", q_) for q_ in range(qa, qb)], w=[pk])
                              c_ = 0
                              while c_ < n:
                                  dl = kt - (qa + c_)
                                  if dl == 8:
                                      m0, ln = (moff["g2"] + 5) * 128, 1
                                  elif dl == -8:
                                      m0, ln = moff["g2"] * 128, 1
                                  else:
                                      ln = min(n - c_, 4, dl + 8)
                                      m0 = (moff["g2"] + 1) * 128
                                  mm(pb[:, c_ * 128:(c_ + ln) * 128], identb, cs["masks"][:, m0:m0 + ln * 128], False, c_ + ln == n,
                                     r=["identb", "c_masks"], w=[pk])
                                  c_ += ln
                              pi = rot("PT", NPT)
                              S.op("act", I("activation", out=PT[pi][:, 0:n * 128], in_=pb[:, 0:n * 128], func=AF.Exp, scale=0.125),
                                   r=[pk], w=[("PT", pi)])
                              stage(lambda pa2=pa2, pa2k=pa2k, qa=qa, qb=qb, kt=kt, vh2=vh2, pi=pi, n=n, first2=first2, j=j: mm(
                                  pa2[0:65, (qa - j) * 128:(qb - j) * 128], vv[:, kt, vh2, :], PT[pi][:, 0:n * 128], first2, False,
                                  r=[("PT", pi), ("v", kt)], w=[pa2k]), 2)
                              first2 = False
                          stage(lambda pa2=pa2, pa2k=pa2k, s_=s_: S.op("act", I("activation", out=o2T[s_][0:65, :], in_=pa2[0:65, :], func=AF.Copy),
                                                                       r=[pa2k], w=o2k[s_]), 2)

                  def pair_blocks(gname, qc, kc_, vhs, accb, slots):
                      Dm = GD[gname]
                      lo, hi = max(-Dm, -j), min(Dm, NT - 1 - j)
                      dls = list(range(lo, hi + 1))
                      pa, pak = PS(accb)
                      for b0 in range(0, len(dls), 4):
                          batch = dls[b0:b0 + 4]
                          n = len(batch)
                          pbs = [PS(bank()), PS(bank())]
                          for i, dl in enumerate(batch):
                              for hf in range(2):
                                  rows = slice(hf * 64, (hf + 1) * 64)
                                  pb, pk = pbs[hf]
                                  mm(pb[:, i * 128:(i + 1) * 128], kT[rows, kc_, (j + dl) * 128:(j + dl + 1) * 128], qT[rows, qc, js],
                                     i == 0, False, r=[("kT", j + dl), ("qT", j)], w=[pk])
                          if gname != "g2":
                              m0 = (moff[gname] + batch[0] + Dm) * 128
                          elif batch[0] == -8:
                              m0 = moff[gname] * 128
                          elif batch[-1] == 8:
                              m0 = (moff[gname] + 6 - n) * 128
                          else:
                              m0 = (moff[gname] + 1) * 128
                          pis = []
                          for hf in range(2):
                              pb, pk = pbs[hf]
                              mm(pb[:, 0:n * 128], identb, cs["masks"][:, m0:m0 + n * 128], False, True, r=["identb", "c_masks"], w=[pk])
                          for hf in range(2):
                              pb, pk = pbs[hf]
                              pi = rot("PT", NPT)
                              pis.append(pi)
                              S.op("act", I("activation", out=PT[pi][:, 0:n * 128], in_=pb[:, 0:n * 128], func=AF.Exp, scale=0.125),
                                   r=[pk], w=[("PT", pi)])
                          def pvs(pis=pis, batch=batch, pa=pa, pak=pak, j=j):
                              for hf in range(2):
                                  pi = pis[hf]
                                  for i, dl in enumerate(batch):
                                      st_ = (j, accb) not in started
                                      started.add((j, accb))
                                      mm(pa[:, slots[hf] * 128:slots[hf] * 128 + 65], PT[pi][:, i * 128:(i + 1) * 128], vv[:, j + dl, vhs[hf], :],
                                         st_, False, r=[("PT", pi), ("v", j + dl)], w=[pak])
                          stage(pvs, 1)

                  def normalise(bi, st=st, si=si, oi=oi, l=l):
                      pa, pak = PS(accs[bi])
                      pv = pa.rearrange("p (s c) -> p s c", c=128)
                      if bi < 2:
                          S.op("dve", I("tensor_tensor",
                              out=st[:, 20 + bi * 4:24 + bi * 4], in0=pv[:, :, 64], in1=esink[:, l * 8 + bi * 4:l * 8 + bi * 4 + 4], op=ALU.add),
                              r=[pak, "esink"], w=[("smd", si, bi)])
                      else:
                          S.op("dve", I("tensor_copy", out=st[:, 20 + bi * 4:24 + bi * 4], in_=pv[:, :, 64]),
                               r=[pak], w=[("smd", si, bi)])
                      S.op("dve", I("reciprocal", out=st[:, 20 + bi * 4:24 + bi * 4], in_=st[:, 20 + bi * 4:24 + bi * 4]),
                           r=[("smd", si, bi)], w=[("smr", si, bi)])
                      for s_ in range(4):
                          S.op("dve", I("tensor_scalar",
                              out=oab[oi][:, bi * 4 + s_, :], in0=pv[:, s_, 0:64], scalar1=st[:, 20 + bi * 4 + s_:21 + bi * 4 + s_],
                              scalar2=None, op0=ALU.mult), r=[pak, ("smr", si, bi)], w=[("oab", oi, bi)])

                  for s0 in (0, 2):
                      for g, gname in enumerate(["g0", "g1"]):
                          i_ = g * 4 + s0
                          pair_blocks(gname, 4 + i_ // 2, 1 + i_ // 2, (2 + i_, 3 + i_), accs[2], (s0, s0 + 1))
                      def g2acc(s0=s0, j=j):
                          for s_ in (s0, s0 + 1):
                              pa7, pa7k = PS(accs[2])
                              mm(pa7[:, s_ * 128:s_ * 128 + 65], o2T[s_][0:65, (j % 4) * 128:(j % 4 + 1) * 128], identb[0:65, 0:65],
                                 False, False, r=o2k[s_] + ["identb"], w=[pa7k])
                      stage(g2acc, 1)
                  stage(lambda: normalise(2), 1)
                  for c in range(4):
                      pair_blocks("A", c, 0, (0, 1), accs[c // 2], ((2 * c) % 4, (2 * c + 1) % 4))
                      if c == 1:
                          stage(lambda: normalise(0), 1)
                  stage(lambda: normalise(1), 1)
                  def p2_tail(oi=oi, js=js, j=j):
                      bt = bank()
                      pt_, ptk = PS(bt)
                      ptb = pt_.bitcast(BF)
                      of = oab[oi].rearrange("p h d -> p (h d)")
                      for i in range(6):
                          tr(ptb[:, i * 128:(i + 1) * 128], of[:, i * 128:(i + 1) * 128], identb,
                             r=[("oab", oi, 0), ("oab", oi, 1), ("oab", oi, 2), "identb"], w=[ptk])
                      S.op("act", I("activation",
                          out=qT[:, 0:6, js], in_=ptb[:, 0:768].rearrange("p (c t) -> p c t", t=128), func=AF.Copy), r=[ptk], w=[("qT", j)])
                  stage(p2_tail, 1)
              while lagq:
                  lagq.pop(0)()

              stop_at('p2a')
              fence(RKEYS_A + TA, RKEYS_B + TB)
              set_pool(range(8))
              f0 = (sq == 0)

              def ld_wg(cb):
                  load_block(l, [("wg", cb, 0), ("wg", cb, 1)], wg[:, :, cb * 512:(cb + 1) * 512], ("wg", cb), f0)

              def ld_wa(cb):
                  load_block(l, [("wa", cb)], wa[:, :, cb * 512:(cb + 1) * 512], ("wa", cb), f0)

              def ld_wo(cb):
                  load_block(l, [("wo", cb, 0), ("wo", cb, 1)], wo[:, :, cb * 512:(cb + 1) * 512], ("wo", cb), f0)
              ld_wg(0)
              ld_wg(2)
              ld_wa(0)
              load_block(l, [("wb",)], wb, "wb", f0)
              ld_wg(1)
              ld_wg(3)
              ld_wa(1)
              ld_wo(0)
              ld_wo(1)
              pend = []

              def flush():
                  while pend:
                      pend.pop(0)()
              for tc in range(4):
                  cs_ = slice(tc * 512, (tc + 1) * 512)
                  tiles = list(range(tc * 4, tc * 4 + 4))
                  xk = [("xh", t) for t in tiles]
                  qk_ = [("qT", t) for t in tiles]
                  for dc in range(8):
                      if dc == 2:
                          flush()
                      tg_ = []
                      for gi in range(2):
                          b_ = bank()
                          pb, pk = PS(b_)
                          col = gi * 1024 + dc * 128
                          for k in range(8):
                              mm(pb, wg[:, k, col:col + 128], xh[:, k, cs_], k == 0, k == 7, r=[(("wg", col // 512), k)] + xk, w=[pk])
                          fi = rot("f", 3)
                          S.op("act", I("activation",
                              out=f512[fi], in_=pb, func=AF.Tanh, scale=0.5, bias=bgh[:, l, gi * 8 + dc:gi * 8 + dc + 1]),
                              r=[pk, "bgh"], w=[("f", fi)])
                          tg_.append(fi)
                      ba = bank()
                      pa, pak = PS(ba)
                      for k in range(4):
                          mm(pa, wa[:, k, dc * 128:(dc + 1) * 128], qT[:, k, cs_], k == 0, k == 3, r=[(("wa", dc // 4), k)] + qk_, w=[pak])
                      bb_ = bank()
                      pb2, pbk = PS(bb_)
                      for k in range(2):
                          mm(pb2, wb[:, k, dc * 128:(dc + 1) * 128], qT[:, 4 + k, cs_], k == 0, k == 1, r=[("wb", k)] + qk_, w=[pbk])
                      fa, fb_ = tg_
                      S.op("dve", I("scalar_tensor_tensor",
                          out=f512[fa], in0=f512[fa], scalar=1.0, in1=pa, op0=ALU.add, op1=ALU.mult), r=[("f", fa), pak], w=[("f", fa)])
                      S.op("dve", I("scalar_tensor_tensor",
                          out=f512[fb_], in0=f512[fb_], scalar=1.0, in1=pb2, op0=ALU.add, op1=ALU.mult), r=[("f", fb_), pbk], w=[("f", fb_)])
                      S.op("dve", I("tensor_tensor", out=f512[fa], in0=f512[fa], in1=f512[fb_], op=ALU.add),
                           r=[("f", fa), ("f", fb_)], w=[("f", fa)])
                      S.op("act", I("activation", out=mg[:, dc, :], in_=f512[fa], func=AF.Copy, scale=C_HALF), r=[("f", fa)], w=["mg"])
                  for ti, t in enumerate(tiles):
                      ts = slice(t * 128, (t + 1) * 128)
                      bks = [bank(), bank()]
                      for cb in range(2):
                          pb, pk = PS(bks[cb])
                          for k in range(8):
                              mm(pb, mg[:, k, ti * 128:(ti + 1) * 128], wo[:, k, cb * 512:(cb + 1) * 512], k == 0, False,
                                 r=["mg", (("wo", cb), k)], w=[pk])
                          for c in range(4):
                              mm(pb[:, c * 128:(c + 1) * 128], xh[:, cb * 4 + c, ts], identb, False, False, r=[("xh", t), "identb"], w=[pk])
                              mm(pb[:, c * 128:(c + 1) * 128], xl[:, cb * 4 + c, ts], identb, False, c == 3, r=[("xl", t), "identb"], w=[pk])
                      rstd, nmr, smk = layer_norm_stats([psb[bks[0]], psb[bks[1]]], [("ps", bks[0]), ("ps", bks[1])])
                      dbg_dump("sml", t, sm[(state["sm"] + 1) % 2], smk, 32)
                      zi = rot("zb", 3)
                      for cb in range(2):
                          S.op("act", I("activation",
                              out=ztb[zi][:, cb * 512:(cb + 1) * 512], in_=psb[bks[cb]], func=AF.Identity, scale=rstd, bias=nmr),
                              r=[("ps", bks[cb])] + smk, w=[("zt", zi)])
                      pend.append(lambda t=t, zi=zi, l=l: to_hilo(t, ztb[zi], ("zt", zi), ("g1", "b1", l)))
                      while len(pend) > 2:
                          pend.pop(0)()
              flush()

              stop_at('p2b')
              fence(RKEYS_B + [("qT", t) for t in range(NT)], RKEYS_C)
              set_pool(range(4))
              for cb in range(2):
                  load_block(l, [("wpg", cb, 0), ("wpg", cb, 1)], wpg[:, :, cb * 512:(cb + 1) * 512], ("wpg", cb), sq == 0)
              load_block(l, [("wp",)], wpl, "wpl", sq == 0)
              jobs = []
              for tc_ in range(4):
                  for fb in range(8):
                      jobs.append(([("wu", fb, 0), ("wu", fb, 1)], sq == 0 and tc_ == 0))
                  for cb in range(2):
                      for fgp in range(4):
                          jobs.append(([("wd", cb, fgp * 2), ("wd", cb, fgp * 2 + 1)], sq == 0 and tc_ == 0))
              jst = {"issued": 0, "used": 0}

              def next_block():
                  n = jst["used"]
                  while jst["issued"] < min(n + 3, len(jobs)):
                      m = jst["issued"]
                      load_block(l, jobs[m][0], wbuf[m % 3], ("wbuf", m % 3), jobs[m][1])
                      jst["issued"] += 1
                  jst["used"] += 1
                  return n % 3
              S.op("sp", I("dma_start", out=bpgbv, in_=cd["bpgb"][l]), w=["bpgb"], dma="bpgb")
              if last and sq == 0:
                  pass
              if last:
                  S.op("sp", I("dma_start", out=lnfg, in_=cd["lnfg"]), w=["lnfg"], dma="lnfg")
                  S.op("sp", I("dma_start", out=lnfb, in_=cd["lnfb"]), w=["lnfb"], dma="lnfb")
              for tc in range(4):
                  cs_ = slice(tc * 512, (tc + 1) * 512)
                  tiles = list(range(tc * 4, tc * 4 + 4))
                  xk = [("xh", t) for t in tiles]
                  pTi = {}
                  for fb in range(8):
                      if fb == 4:
                          flush()
                      wi = next_block()
                      for fc in range(4):
                          b_ = bank()
                          pb, pk = PS(b_)
                          for k in range(8):
                              mm(pb, wbuf[wi][:, k, fc * 128:(fc + 1) * 128], xh[:, k, cs_], k == 0, k == 7, r=[(("wbuf", wi), k)] + xk, w=[pk])
                          fi = rot("f", 3)
                          S.op("act", I("activation", out=f512[fi], in_=pb, func=AF.Relu, scale=RELU_S),
                               r=[pk], w=[("f", fi)])
                          S.op("pool", I("tensor_tensor", out=uT[:, fb * 4 + fc, :], in0=f512[fi], in1=f512[fi],
                                                                                    op=ALU.mult), r=[("f", fi)], w=["uT"])
                  for cb in range(2):
                      ccs = slice(cb * 512, (cb + 1) * 512)
                      accb = [4, 5, 6, 7]
                      for fgp in range(4):
                          wi = next_block()
                          for f in range(8):
                              for ti in range(4):
                                  pb, pk = PS(accb[ti])
                                  mm(pb, uT[:, fgp * 8 + f, ti * 128:(ti + 1) * 128], wbuf[wi][:, f, :], fgp == 0 and f == 0, False,
                                     r=["uT", (("wbuf", wi), f)], w=[pk])
                      for ti, t in enumerate(tiles):
                          ts = slice(t * 128, (t + 1) * 128)
                          pb, pk = PS(accb[ti])
                          for c in range(4):
                              mm(pb[:, c * 128:(c + 1) * 128], xh[:, cb * 4 + c, ts], identb, False, False, r=[("xh", t), "identb"], w=[pk])
                              mm(pb[:, c * 128:(c + 1) * 128], xl[:, cb * 4 + c, ts], identb, False, c == 3, r=[("xl", t), "identb"], w=[pk])
                          bg_ = bank()
                          pg, pgk = PS(bg_)
                          for k in range(8):
                              mm(pg, xh[:, k, ts], wpg[:, k, ccs], k == 0, k == 7, r=[("xh", t), (("wpg", cb), k)], w=[pgk])
                          fi = rot("f", 3)
                          S.op("dve", I("tensor_tensor", out=f512[fi], in0=pg, in1=bpgbv[:, ccs], op=ALU.add),
                               r=[pgk, "bpgb"], w=[("f", fi)])
                          S.op("act", I("activation", out=f512[fi], in_=f512[fi], func=AF.Tanh, scale=0.5), r=[("f", fi)], w=[("f", fi)])
                          if cb == 0:
                              pi_ = rot("pin", 2)
                              S.op("sp", I("dma_start", out=pin[pi_], in_=p_d[l, sq, t * 128:(t + 1) * 128, :]),
                                   w=[("pin", pi_)], dma=("pin", pi_))
                              bt = bank()
                              pt_, ptk = PS(bt)
                              for k2 in range(2):
                                  tr(pt_[:, k2 * 128:(k2 + 1) * 128], pin[pi_][:, k2 * 128:(k2 + 1) * 128], cs["identf"],
                                     r=[("pin", pi_), "c_identf"], w=[ptk])
                              S.op("act", I("activation",
                                  out=pT[ti], in_=pt_[:, 0:256].rearrange("p (k t) -> p k t", t=128), func=AF.Copy), r=[ptk], w=[("pT", ti)])
                          pi_ = ti
                          bw = bank()
                          pw_, pwk = PS(bw)
                          for k2 in range(2):
                              mm(pw_, pT[pi_][:, k2, :], wpl[:, k2, ccs], k2 == 0, k2 == 1, r=[("pT", pi_), ("wpl", k2)], w=[pwk])
                          S.op("dve", I("scalar_tensor_tensor",
                              out=f512[fi], in0=f512[fi], scalar=1.0, in1=pw_, op0=ALU.add, op1=ALU.mult), r=[("f", fi), pwk], w=[("f", fi)])
                          S.op("dve", I("scalar_tensor_tensor",
                              out=yb[:, ti, ccs], in0=f512[fi], scalar=C_HALF, in1=pb, op0=ALU.mult, op1=ALU.add),
                              r=[("f", fi), pk], w=[("yb", ti, cb)])
                  def tail(tiles=tiles, l=l, sq=sq, last=last):
                      for ti, t in enumerate(tiles):
                          yk = [("yb", ti, 0), ("yb", ti, 1)]
                          rstd, nmr, smk = layer_norm_stats([yb[:, ti, 0:512], yb[:, ti, 512:1024]], yk)
                          S.op("act", I("activation",
                              out=yb[:, ti, :], in_=yb[:, ti, :], func=AF.Identity, scale=rstd, bias=nmr), r=yk + smk, w=yk)
                      for ti, t in enumerate(tiles):
                          yk = [("yb", ti, 0), ("yb", ti, 1)]
                          if not last:
                              to_hilo(t, yb[:, ti, :], yk, ("g2", "b2", l))
                          else:
                              S.op("dve", I("tensor_tensor", out=yb[:, ti, :], in0=yb[:, ti, :], in1=lnfg, op=ALU.mult),
                                   r=yk + ["lnfg"], w=yk)
                              S.op("dve", I("tensor_tensor", out=yb[:, ti, :], in0=yb[:, ti, :], in1=lnfb, op=ALU.add),
                                   r=yk + ["lnfb"], w=yk)
                              S.op("sp", I("dma_start", out=out_d[sq, t * 128:(t + 1) * 128, :], in_=yb[:, ti, :]),
                                   r=yk, w=[("out", ti)], dma=("out", ti))
                  pend.append(tail)
              flush()

    except _Stop:
        pass
    S.emit(final_slots=[s for s in S.slots if isinstance(s, tuple) and s[0] == "out"])
    return nc


_CACHE = {}


def kernel(**inputs):
    inp = {k: np.asarray(v) for k, v in inputs.items()}
    wall = pack_weights(inp)
    consts = make_consts(inp)
    if "nc" not in _CACHE:
        _CACHE["nc"] = build()
    nc = _CACHE["nc"]
    in_maps = []
    for c in range(8):
        m = {"x": np.ascontiguousarray(inp["x"][2 * c:2 * c + 2]),
             "p": np.ascontiguousarray(inp["p"][:, 2 * c:2 * c + 2]),
             "w": wall}
        for k, v in consts.items():
            m["c_" + k] = v
        in_maps.append(m)
    res = run_bass_kernel_spmd(nc, in_maps, core_ids=list(range(8)))
    return np.concatenate([r["out"] for r in res.results], axis=0).astype(np.float32)
```

```python
import contextlib
import numpy as np
import ml_dtypes
import concourse.bass as bass
import concourse.mybir as mybir
from concourse.bass_utils import run_bass_kernel_spmd

F32 = mybir.dt.float32
BF = mybir.dt.bfloat16
AF = mybir.ActivationFunctionType
ALU = mybir.AluOpType

ENGS = ("pe", "act", "dve", "pool", "sp")
S_LEN = 2048
NT = 16
D = 1024
ALPHA = 8.0 ** 0.25
EPS_P = 1e-5 / (ALPHA * ALPHA)
C_HALF = 0.5 / ALPHA
RELU_S = ALPHA ** -0.5
A_PERM = [0, 4, 1, 5, 2, 6, 3, 7]
GROUPS = [("A", 128, 1), ("g0", 64, 1), ("g1", 256, 4), ("g2", 1024, 16)]
GD = {"A": 1, "g0": 1, "g1": 2, "g2": 8}


class Sched:
    def __init__(self, nc):
        self.nc = nc
        self.ops = {e: [] for e in ENGS}
        self.lastw = {}
        self.readers = {}
        self.slot_cnt = {}
        self.slots = []

    def op(self, eng, fn, r=(), w=(), dma=None):
        idx = len(self.ops[eng])
        w = list(w) + [("psr",) + tuple(k[1:]) for k in r if isinstance(k, tuple) and k[0] == "ps"]
        deps = set()
        for k in r:
            t = self.lastw.get(k)
            if t is not None:
                deps.add((t, "raw"))
        for k in w:
            t = self.lastw.get(k)
            if t is not None:
                deps.add((t, "waw"))
            for t in self.readers.get(k, ()):
                deps.add((t, "war"))
        if dma is not None:
            if dma not in self.slot_cnt:
                self.slot_cnt[dma] = 0
                self.slots.append(dma)
            self.slot_cnt[dma] += 1
            tok = ("d", dma, self.slot_cnt[dma])
        else:
            tok = ("c", eng, idx)
        for k in w:
            self.lastw[k] = tok
            self.readers[k] = []
        for k in r:
            self.readers.setdefault(k, []).append(tok)
        self.ops[eng].append(dict(fn=fn, deps=deps, dma=dma, sig=False))
        return tok

    def emit(self, final_slots=()):
        nc = self.nc
        for e in ENGS:
            for i, o in enumerate(self.ops[e]):
                lst = {}
                for (t, kind) in o["deps"]:
                    if t[0] == "c":
                        _, pe_, pi = t
                        if pe_ == e and o["dma"] is None:
                            if pe_ == "pe":
                                continue
                            if kind != "raw" or pi < i - 2:
                                continue
                        k = ("c", pe_)
                        lst[k] = max(lst.get(k, -1), pi)
                    else:
                        _, slot, cnt = t
                        k = ("d", slot)
                        lst[k] = max(lst.get(k, -1), cnt)
                o["need"] = lst
                for k, v in lst.items():
                    if k[0] == "c":
                        self.ops[k[1]][v]["sig"] = True
        cum = {}
        for e in ENGS:
            c = 0
            arr = []
            for o in self.ops[e]:
                if o["sig"] and o["dma"] is None:
                    c += 1
                arr.append(c)
            cum[e] = arr
        with contextlib.ExitStack() as st:
            esem = {e: st.enter_context(nc.semaphore("s_" + e)) for e in ENGS}
            dsem = {s: st.enter_context(nc.semaphore("d_%d" % i)) for i, s in enumerate(self.slots)}
            block = st.enter_context(nc.Block())

            def run(e, eng):
                waited = {}
                for o in self.ops[e]:
                    for k, v in o["need"].items():
                        if k[0] == "c":
                            val = cum[k[1]][v]
                            sem = esem[k[1]]
                        else:
                            val = 16 * v
                            sem = dsem[k[1]]
                        if waited.get(k, 0) < val:
                            eng.wait_ge(sem, val)
                            waited[k] = val
                    ins = o["fn"](eng)
                    if o["dma"] is not None:
                        ins.then_inc(dsem[o["dma"]], 16)
                    elif o["sig"]:
                        ins.then_inc(esem[e], 1)
                if e == "sp":
                    for s in final_slots:
                        eng.wait_ge(dsem[s], 16 * self.slot_cnt[s])

            @block.tensor
            def _(eng):
                run("pe", eng)

            @block.scalar
            def _(eng):
                run("act", eng)

            @block.vector
            def _(eng):
                run("dve", eng)

            @block.gpsimd
            def _(eng):
                run("pool", eng)

            @block.sync
            def _(eng):
                run("sp", eng)


def _qkv_cols():
    def head(base, h):
        return list(range(base + h * 64, base + (h + 1) * 64))
    chunks = []
    for (h0, h1) in [(0, 4), (1, 5), (2, 6), (3, 7)]:
        chunks.append(head(0, h0) + head(0, h1))
    for i in range(6):
        chunks.append(head(768, 2 * i) + head(768, 2 * i + 1))
    chunks.append(head(512, 0) + head(512, 1))
    for i in range(6):
        chunks.append(head(1536, 2 * i) + head(1536, 2 * i + 1))
    blocks = []
    for (a, b) in [(0, 4), (4, 8), (8, 11), (11, 15), (15, 17)]:
        blocks.append(sum(chunks[a:b], []))
    vheads = [head(640, 0), head(640, 1)] + [head(2304, i) for i in range(12)]
    blocks.append(sum(vheads[0:8], []))
    blocks.append(sum(vheads[8:14], []))
    return blocks


QKV_BLOCKS = _qkv_cols()
QK_CHUNK_RANGES = [(0, 4), (4, 8), (8, 11), (11, 15), (15, 17)]


def piece_table():
    tab = {}
    off = 0

    def add(name, kc, ncols):
        nonlocal off
        tab[name] = (off, kc, ncols)
        off += kc * ncols
    for b in range(7):
        for kh in range(2):
            add(("qkv", b, kh), 4, len(QKV_BLOCKS[b]))
    for cb in range(4):
        for kh in range(2):
            add(("wg", cb, kh), 4, 512)
    for cb in range(2):
        add(("wa", cb), 4, 512)
    add(("wb",), 2, 1024)
    for cb in range(2):
        for kh in range(2):
            add(("wo", cb, kh), 4, 512)
    for fb in range(8):
        for kh in range(2):
            add(("wu", fb, kh), 4, 512)
    for cb in range(2):
        for fg in range(8):
            add(("wd", cb, fg), 4, 512)
    for cb in range(2):
        for kh in range(2):
            add(("wpg", cb, kh), 4, 512)
    add(("wp",), 2, 1024)
    return tab, off


PIECES, WTOT = piece_table()


def _pm(W):
    K, C = W.shape
    return np.ascontiguousarray(W.reshape(K // 128, 128, C).transpose(1, 0, 2))


def pack_weights(inp):
    out = np.empty((4, 128, WTOT), np.float32)
    for l in range(4):
        def put(name, arr):
            off, kc, nc_ = PIECES[name]
            out[l, :, off:off + kc * nc_] = arr.reshape(128, kc * nc_)
        w_in = inp["w_in"][l]
        for b in range(7):
            wp = _pm(w_in[:, QKV_BLOCKS[b]])
            for kh in range(2):
                put(("qkv", b, kh), wp[:, kh * 4:(kh + 1) * 4, :])
        wg = _pm(w_in[:, 3072:5120])
        for cb in range(4):
            for kh in range(2):
                put(("wg", cb, kh), wg[:, kh * 4:(kh + 1) * 4, cb * 512:(cb + 1) * 512])
        rows = sum([list(range(h * 64, (h + 1) * 64)) for h in A_PERM], [])
        wa = _pm(inp["w_branch_a"][l][rows, :])
        for cb in range(2):
            put(("wa", cb), wa[:, :, cb * 512:(cb + 1) * 512])
        put(("wb",), _pm(inp["w_branch_b"][l]))
        wo = _pm(inp["w_out"][l])
        for cb in range(2):
            for kh in range(2):
                put(("wo", cb, kh), wo[:, kh * 4:(kh + 1) * 4, cb * 512:(cb + 1) * 512])
        wu = _pm(inp["w_up"][l])
        for fb in range(8):
            for kh in range(2):
                put(("wu", fb, kh), wu[:, kh * 4:(kh + 1) * 4, fb * 512:(fb + 1) * 512])
        wd = _pm(inp["w_down"][l])
        for cb in range(2):
            for fg in range(8):
                put(("wd", cb, fg), wd[:, fg * 4:(fg + 1) * 4, cb * 512:(cb + 1) * 512])
        wpg = _pm(inp["w_ple_gate"][l])
        for cb in range(2):
            for kh in range(2):
                put(("wpg", cb, kh), wpg[:, kh * 4:(kh + 1) * 4, cb * 512:(cb + 1) * 512])
        put(("wp",), _pm(inp["w_ple"][l]))
    return out


def make_consts(inp, lastl=3):
    c = {}
    c["identf"] = np.eye(128, dtype=np.float32)
    kl = np.arange(128)[:, None]
    ql = np.arange(128)[None, :]
    cols = []
    for (name, W, r) in GROUPS:
        Dm = GD[name]
        dls = range(-Dm, Dm + 1) if name != "g2" else [-8, 0, 0, 0, 0, 8]
        for dl in dls:
            diff = 128 * dl + kl - ql
            cols.append((((np.abs(diff) <= W) & ((kl - ql) % r == 0)).astype(np.float32) - 1.0) * 30000.0)
    c["masks"] = np.concatenate(cols, axis=1).astype(ml_dtypes.bfloat16)
    pos = np.arange(S_LEN, dtype=np.float32)
    inv = (np.float32(500000.0) ** (-np.arange(0, 16, 2, dtype=np.float32) / np.float32(16))).astype(np.float32)
    ang = (pos[:, None] * inv[None, :]).astype(np.float32)
    c["cos"] = np.ascontiguousarray(np.broadcast_to(np.cos(ang).astype(np.float32).reshape(16, 128, 1, 8).transpose(1, 0, 2, 3), (128, 16, 4, 8)))
    c["sin"] = np.ascontiguousarray(np.broadcast_to(np.sin(ang).astype(np.float32).reshape(16, 128, 1, 8).transpose(1, 0, 2, 3), (128, 16, 4, 8)))
    def fm(v, nch):
        return np.ascontiguousarray(v.reshape(4, nch, 128).transpose(2, 0, 1))
    c["bg"] = fm(inp["b_gate"], 16)
    c["g1"] = fm(inp["ln1_g"], 8)
    c["b1"] = fm(inp["ln1_b"], 8)
    c["g2"] = fm(inp["ln2_g"], 8)
    c["b2"] = fm(inp["ln2_b"], 8)
    c["bpgb"] = np.ascontiguousarray(np.broadcast_to(inp["b_ple_gate"][:, None, :], (4, 128, 1024)))
    c["sink"] = np.ascontiguousarray(np.broadcast_to(inp["a_sink"][:, A_PERM].reshape(1, 32), (128, 32)))
    c["lnfg"] = np.ascontiguousarray(np.broadcast_to(inp["ln2_g"][lastl][None, :], (128, 1024)))
    c["lnfb"] = np.ascontiguousarray(np.broadcast_to(inp["ln2_b"][lastl][None, :], (128, 1024)))
    return c


CONST_SHAPES = {
    "identf": ([128, 128], F32), "masks": ([128, 17 * 128], BF), "cos": ([128, 16, 4, 8], F32),
    "sin": ([128, 16, 4, 8], F32), "bg": ([128, 4, 16], F32), "g1": ([128, 4, 8], F32),
    "b1": ([128, 4, 8], F32), "g2": ([128, 4, 8], F32), "b2": ([128, 4, 8], F32),
    "bpgb": ([4, 128, 1024], F32), "sink": ([128, 32], F32), "lnfg": ([128, 1024], F32),
    "lnfb": ([128, 1024], F32),
}


def build(nlayers=4, nseq=2):
    import os as _os
    nc = bass.Bass("TRN2", target_bir_lowering=False)
    S = Sched(nc)
    x_d = nc.dram_tensor("x", [nseq, S_LEN, D], F32, kind="ExternalInput").ap()
    p_d = nc.dram_tensor("p", [4, nseq, S_LEN, 256], F32, kind="ExternalInput").ap()
    w_d = nc.dram_tensor("w", [4, 128, WTOT], F32, kind="ExternalInput").ap()
    out_d = nc.dram_tensor("out", [nseq, S_LEN, D], F32, kind="ExternalOutput").ap()
    cd = {k: nc.dram_tensor("c_" + k, sh, dt, kind="ExternalInput").ap() for k, (sh, dt) in CONST_SHAPES.items()}
    KDBG = _os.environ.get("KDBG", "")
    dbg_d = nc.dram_tensor("dbg", [S_LEN, D], F32, kind="ExternalOutput").ap() if KDBG else None

    def dbg_dump(tag, t, ap, key, ncols=1024):
        if KDBG == tag:
            keys = key if isinstance(key, list) else [key]
            S.op("sp", I("dma_start", out=dbg_d[t * 128:(t + 1) * 128, 0:ncols], in_=ap), r=keys, w=[("dbg", t)], dma=("dbg", t % 4))

    def sb(name, shape, dt):
        return nc.alloc_sbuf_tensor(name, shape, dt).ap()

    xh = sb("xh", [128, 8, S_LEN], BF)
    xl = sb("xl", [128, 8, S_LEN], BF)
    RSZ = 49408
    R = sb("R", [128, RSZ], BF)
    wbuf = [sb("wbuf0", [128, 8, 512], BF),
            R[:, 40960:45056].rearrange("p (k c) -> p k c", c=512), R[:, 45056:49152].rearrange("p (k c) -> p k c", c=512)]
    NSTG = 3
    stg = [sb("stg%d" % i, [128, 512], F32) for i in range(NSTG)]
    wscr = nc.dram_tensor("wscr", [4, 128, WTOT], BF, kind="Internal").ap()
    T = sb("T", [128, 9216], BF)
    cs = {k: sb("k_" + k, sh, dt) for k, (sh, dt) in CONST_SHAPES.items() if k not in ("lnfg", "lnfb", "bpgb")}
    identb = sb("identb", [128, 128], BF)
    bgh = sb("bgh", [128, 4, 16], F32)
    esink = sb("esink", [128, 32], F32)
    mhalf = sb("mhalf", [128, 1], F32)
    onesc = sb("onesc", [128, 2], BF)
    tq = [T[:, i * 512:(i + 1) * 512] for i in range(2)] + [T[:, 7168:7680]]
    rp = [T[:, o_:o_ + 512].bitcast(F32).rearrange("p (a h d) -> p a h d", a=4, h=8) for o_ in (1024, 1536, 7680)]
    NPT = 6
    PT = [T[:, 2048 + i * 512:2048 + (i + 1) * 512] for i in range(3)] + [T[:, 5120 + i * 512:5120 + (i + 1) * 512] for i in range(3)]
    oab = [T[:, 3584 + i * 768:3584 + (i + 1) * 768].rearrange("p (h d) -> p h d", d=64) for i in range(2)]
    f512 = [T[:, i * 1024:(i + 1) * 1024].bitcast(F32) for i in range(3)]
    xf = [T[:, 3072:5120].bitcast(F32).rearrange("p (c t) -> p c t", t=128)]
    zt = [T[:, 5120 + i * 2048:5120 + (i + 1) * 2048].bitcast(F32) for i in range(2)]
    tfr = [T[:, o_:o_ + 256].bitcast(F32).rearrange("p (h d) -> p h d", d=16) for o_ in (6656, 6912, 8192)]
    TA = [("pcb",)] + [("tfr", i) for i in range(3)] + [("tq", i) for i in range(3)] + [("rp", i, j) for i in range(3) for j in range(4)] + [("PT", i) for i in range(6)] + [("oab", i, b) for i in range(2) for b in range(3)]
    TB = [("f", i) for i in range(3)] + [("xf", 0, h) for h in range(2)] + [("zt", i) for i in range(2)]
    sm = [sb("sm%d" % i, [128, 32], F32) for i in range(4)]
    pin = [sb("pin%d" % i, [128, 256], F32) for i in range(2)]
    pT = [sb("pT%d" % i, [128, 2, 128], BF) for i in range(4)]
    psb = [nc.alloc_psum_tensor("ps%d" % i, [128, 512], F32).ap() for i in range(8)]

    qT = R[:, 0:10 * 2048].rearrange("p (c t) -> p c t", t=2048)
    kT = R[:, 20480:20480 + 7 * 2048].rearrange("p (c t) -> p c t", t=2048)
    vv = R[:, 34816:34816 + 16 * 14 * 65].rearrange("p (t h d) -> p t h d", h=14, d=65)
    vflat = R[:, 34816:34816 + 16 * 14 * 65].rearrange("p (n d) -> p n d", d=65)
    o2 = 12288
    wg = R[:, o2:o2 + 8 * 2048].rearrange("p (k c) -> p k c", c=2048)
    wa = R[:, o2 + 16384:o2 + 16384 + 4 * 1024].rearrange("p (k c) -> p k c", c=1024)
    wb = R[:, o2 + 20480:o2 + 20480 + 2 * 1024].rearrange("p (k c) -> p k c", c=1024)
    wo = R[:, o2 + 22528:o2 + 22528 + 8 * 1024].rearrange("p (k c) -> p k c", c=1024)
    mg = R[:, o2 + 30720:o2 + 30720 + 8 * 512].rearrange("p (k c) -> p k c", c=512)
    uT = R[:, 0:32 * 512].rearrange("p (f t) -> p f t", t=512)
    yb = R[:, 16384:16384 + 8192].bitcast(F32).rearrange("p (t c) -> p t c", c=1024)
    wpg = R[:, 24576:24576 + 8 * 1024].rearrange("p (k c) -> p k c", c=1024)
    wpl = R[:, 32768:32768 + 2 * 1024].rearrange("p (k c) -> p k c", c=1024)
    lnfg = R[:, 34816:34816 + 2048].bitcast(F32)
    lnfb = R[:, 36864:36864 + 2048].bitcast(F32)
    bpgbv = R[:, 38912:38912 + 2048].bitcast(F32)
    ztb = [zt[0], zt[1], R[:, 47104:49152].bitcast(F32)]
    RKEYS_A = [("qT", t) for t in range(NT)] + [("kT", t) for t in range(NT)] + [("v", t) for t in range(NT)]
    RKEYS_B = ([(("wg", i), k) for i in range(4) for k in range(8)] + [(("wa", i), k) for i in range(2) for k in range(4)]
               + [(("wo", i), k) for i in range(2) for k in range(8)] + [("wb", k) for k in range(2)] + ["mg", ("zt", 2)])
    RKEYS_C = ["uT", "lnfg", "lnfb", "bpgb"] + [(("wpg", i), k) for i in range(2) for k in range(8)] + [("wpl", k) for k in range(2)] + [(("wbuf", i), k) for i in (1, 2) for k in range(8)] + [("yb", i, j) for i in range(4) for j in range(2)]

    state = {"zb": 0, "ce": 0, "bank": 0, "stg": 0, "wb": 0, "f": 0, "tq": 0, "PT": 0, "z": 0, "xf": 0, "sm": 0, "pin": 0, "oab": 0}

    def rot(name, n):
        i = state[name]
        state[name] = (i + 1) % n
        return i

    pool_ = {"l": list(range(8)), "i": 0}

    def set_pool(lst):
        pool_["l"] = list(lst)
        pool_["i"] = 0

    def bank():
        b = pool_["l"][pool_["i"] % len(pool_["l"])]
        pool_["i"] += 1
        return b

    def I(meth, *a, **kw):
        return lambda e: getattr(e, meth)(*a, **kw)

    def PS(i):
        return psb[i], ("ps", i)

    def mm(out, lhsT, rhs, start, stop, r, w):
        S.op("pe", I("matmul", out, lhsT=lhsT, rhs=rhs, start=start, stop=stop, skip_group_check=True), r=r, w=w)

    def tr(out, in_, ident, r, w):
        S.op("pe", I("transpose", out=out, in_=in_, identity=ident), r=r, w=w)

    def fence(old, new):
        if _os.environ.get("KOFF", "").find("fence") < 0:
            S.op("pool", I("nop", ), w=list(old) + list(new))

    def load_piece(l, name, dst, dkey):
        off, kc, ncol = PIECES[name]
        for k in range(kc):
            for c0 in range(0, ncol, 512):
                n = min(512, ncol - c0)
                si = rot("stg", 2)
                o = off + k * ncol + c0
                S.op("sp", I("dma_start", out=stg[si][:, 0:n], in_=w_d[l, :, o:o + n]),
                     w=[("stg", si)], dma=("stg", si))
                S.op("pool", I("tensor_copy", out=dst[:, k, c0:c0 + n], in_=stg[si][:, 0:n]),
                     r=[("stg", si)], w=[dkey])

    def load_block(l, pieces, dst, dkey, first):
        off0, _, ncol = PIECES[pieces[0]]
        K = sum(PIECES[p_][1] for p_ in pieces)
        skeys = [("scr", l, off0 + k * ncol + c0) for k in range(K) for c0 in range(0, ncol, 512)]
        sview = wscr[l, :, off0:off0 + K * ncol].rearrange("p (k c) -> p k c", c=ncol)
        if not first:
            S.op("sp", I("dma_start", out=dst, in_=sview), r=skeys, w=[(dkey, k) for k in range(K)], dma=("ld", dkey))
            return
        for k in range(K):
            for c0 in range(0, ncol, 512):
                n = min(512, ncol - c0)
                si = rot("stg", NSTG)
                o = off0 + k * ncol + c0
                S.op("sp", I("dma_start", out=stg[si][:, 0:n], in_=w_d[l, :, o:o + n]), w=[("stg", si)], dma=("stg", si))
                if rot("ce", 2) == 0:
                    S.op("dve", I("tensor_copy", out=dst[:, k, c0:c0 + n], in_=stg[si][:, 0:n]), r=[("stg", si)], w=[(dkey, k)])
                else:
                    S.op("act", I("activation", out=dst[:, k, c0:c0 + n], in_=stg[si][:, 0:n], func=AF.Copy), r=[("stg", si)], w=[(dkey, k)])
        S.op("sp", I("dma_start", out=sview, in_=dst), r=[(dkey, k) for k in range(K)], w=skeys, dma=("scrw", dkey))

    pc_out = [T[:, 8448:8960], T[:, 7168:7680]]
    pc_okey = [("pcb",), ("tq", 2)]
    pc = {"jobs": [], "n": 0, "in": 0}

    def precast_begin(l):
        names = []
        for cb in range(4):
            names += [("wg", cb, 0), ("wg", cb, 1)]
        names += [("wa", 0), ("wa", 1), ("wb",)]
        for cb in range(2):
            names += [("wo", cb, 0), ("wo", cb, 1)]
        for cb in range(2):
            names += [("wpg", cb, 0), ("wpg", cb, 1)]
        names += [("wp",)]
        for fb in range(8):
            names += [("wu", fb, 0), ("wu", fb, 1)]
        for cb in range(2):
            for fg in range(8):
                names += [("wd", cb, fg)]
        jobs = []
        for nm in names:
            off, kc, ncol = PIECES[nm]
            for k in range(kc):
                for c0 in range(0, ncol, 512):
                    jobs.append((l, off + k * ncol + c0, min(512, ncol - c0)))
        pc["jobs"], pc["n"], pc["in"] = jobs, 0, 0

    def precast_issue_in():
        m = pc["in"]
        if m < len(pc["jobs"]):
            l_, o, n_ = pc["jobs"][m]
            si = m % NSTG
            S.op("sp", I("dma_start", out=stg[si][:, 0:n_], in_=w_d[l_, :, o:o + n_]), w=[("stg", si)], dma=("stg", si))
            pc["in"] += 1

    def precast_tick():
        pc["tick"] = pc.get("tick", 0) + 1
        if pc["tick"] % 2 == 0:
            precast_step()

    def precast_step():
        n = pc["n"]
        if n >= len(pc["jobs"]):
            return
        while pc["in"] < min(n + NSTG, len(pc["jobs"])):
            precast_issue_in()
        l_, o, n_ = pc["jobs"][n]
        si, oi_ = n % NSTG, n % 2
        S.op("dve", I("tensor_copy", out=pc_out[oi_][:, 0:n_], in_=stg[si][:, 0:n_]), r=[("stg", si)], w=[pc_okey[oi_]])
        S.op("sp", I("dma_start", out=wscr[l_, :, o:o + n_], in_=pc_out[oi_][:, 0:n_]), r=[pc_okey[oi_]], w=[("scr", l_, o)],
             dma=("pco", oi_))
        pc["n"] += 1

    for k in cs:
        S.op("sp", I("dma_start", out=cs[k], in_=cd[k]), w=["c_" + k], dma="c_" + k)
    S.op("dve", I("tensor_copy", out=identb, in_=cs["identf"]), r=["c_identf"], w=["identb"])
    S.op("dve", I("memset", mhalf, -0.5), w=["mhalf"])
    S.op("dve", I("memset", onesc, 1.0), w=["onesc"])
    S.op("dve", I("tensor_scalar", out=bgh, in0=cs["bg"], scalar1=0.5, scalar2=None, op0=ALU.mult), r=["c_bg"], w=["bgh"])
    S.op("act", I("activation", out=esink, in_=cs["sink"], func=AF.Exp), r=["c_sink"], w=["esink"])
    CK = ["identb", "c_identf"]

    def to_hilo(t, src, skey, gb):
        xi = rot("xf", 1)
        skeys = skey if isinstance(skey, list) else [skey]
        for half in range(2):
            b = bank()
            pb, pk = PS(b)
            for c in range(4):
                tr(pb[:, c * 128:(c + 1) * 128], src[:, (half * 4 + c) * 128:(half * 4 + c + 1) * 128], cs["identf"],
                   r=skeys + ["c_identf"], w=[pk])
            if gb is None:
                S.op("act", I("activation",
                    out=xf[xi][:, half * 4:(half + 1) * 4, :], in_=pb.rearrange("p (c t) -> p c t", t=128), func=AF.Copy),
                    r=[pk], w=[("xf", xi, half)])
            else:
                g, bb, l = gb
                for c in range(4):
                    cc = half * 4 + c
                    S.op("dve", I("tensor_scalar",
                        out=xf[xi][:, cc, :], in0=pb[:, c * 128:(c + 1) * 128], scalar1=cs[g][:, l, cc:cc + 1],
                        scalar2=cs[bb][:, l, cc:cc + 1], op0=ALU.mult, op1=ALU.add),
                        r=[pk, "c_" + g, "c_" + bb], w=[("xf", xi, half)])
        ts = slice(t * 128, (t + 1) * 128)
        S.op("act", I("activation", out=xh[:, :, ts], in_=xf[xi], func=AF.Copy),
             r=[("xf", xi, 0), ("xf", xi, 1)], w=[("xh", t)])
        S.op("pool", I("tensor_tensor", out=xl[:, :, ts], in0=xf[xi], in1=xh[:, :, ts], op=ALU.subtract),
             r=[("xf", xi, 0), ("xf", xi, 1), ("xh", t)], w=[("xl", t)])

    def layer_norm_stats(src_aps, skeys):
        si = rot("sm", 4)
        st = sm[si]
        for i, (a, k) in enumerate(zip(src_aps, skeys)):
            S.op("dve", I("bn_stats", out=st[:, i * 6:(i + 1) * 6], in_=a), r=[k], w=[("sm", si, i)])
        S.op("dve", I("bn_aggr", out=st[:, 12:14], in_=st[:, 0:12]), r=[("sm", si, 0), ("sm", si, 1)], w=[("sm", si, 2)])
        S.op("dve", I("tensor_scalar", out=st[:, 14:15], in0=st[:, 13:14], scalar1=EPS_P, scalar2=None, op0=ALU.add),
             r=[("sm", si, 2)], w=[("sm", si, 3)])
        S.op("pool", I("tensor_tensor", out=st[:, 15:16], in0=st[:, 14:15], in1=mhalf, op=ALU.pow),
             r=[("sm", si, 3), "mhalf"], w=[("sm", si, 4)])
        S.op("dve", I("tensor_scalar", out=st[:, 16:17], in0=st[:, 12:13], scalar1=-1.0, scalar2=st[:, 15:16],
                                              op0=ALU.mult, op1=ALU.mult), r=[("sm", si, 2), ("sm", si, 4)], w=[("sm", si, 5)])
        return st[:, 15:16], st[:, 16:17], [("sm", si, 4), ("sm", si, 5)]

    import os as _os
    STOP = _os.environ.get("KSTOP", "")

    class _Stop(Exception):
        pass

    KOFF = _os.environ.get("KOFF", "").split(",")

    def on(name):
        return name not in KOFF

    def stop_at(name):
        if STOP == name:
            raise _Stop()
    try:
      for sq in range(nseq):
          for t in range(NT):
              zi = rot("z", 2)
              S.op("sp", I("dma_start", out=zt[zi], in_=x_d[sq, t * 128:(t + 1) * 128, :]),
                   w=[("zt", zi)], dma=("zt", zi))
              to_hilo(t, zt[zi], ("zt", zi), None)

          for l in range(nlayers):
              last = (l == nlayers - 1)
              stop_at('load')
              fence(RKEYS_C + TB, RKEYS_A + TA)
              set_pool(range(8))
              if on("ones"):
                  S.op("dve", I("tensor_copy", out=vflat[:, :, 64:65], in_=onesc[:, 0:1].unsqueeze(1).to_broadcast([128, 224, 1])),
                       r=["onesc"], w=[("v", t) for t in range(NT)])
              p1pend = []
              for b in range(7):
                  ncol = len(QKV_BLOCKS[b])
                  wi = 0
                  load_block(l, [("qkv", b, 0), ("qkv", b, 1)], wbuf[wi][:, :, 0:ncol], ("wbuf", wi), sq == 0)
                  for t in range(NT):
                      ts = slice(t * 128, (t + 1) * 128)
                      bk = bank()
                      pb, pk = PS(bk)
                      for k in range(8):
                          mm(pb[:, 0:ncol], xh[:, k, ts], wbuf[wi][:, k, 0:ncol], k == 0, k == 7,
                             r=[("xh", t), (("wbuf", wi), k)], w=[pk])
                      if not on("evac"):
                          continue
                      if b >= 5:
                          h0, nh = (0, 8) if b == 5 else (8, 6)
                          S.op("act", I("activation",
                              out=vv[:, t, h0:h0 + nh, 0:64], in_=pb[:, 0:ncol].rearrange("p (h d) -> p h d", d=64), func=AF.Copy),
                              r=[pk], w=[("v", t)])
                          continue
                      nh = ncol // 64
                      qi = rot("tq", 3)
                      tqv = tq[qi][:, 0:ncol].rearrange("p (h d) -> p h d", d=64)
                      p3 = pb[:, 0:ncol].rearrange("p (h d) -> p h d", d=64)
                      S.op("act", I("activation", out=tqv, in_=p3, func=AF.Copy), r=[pk], w=[("tq", qi)])
                      tf = tfr[qi]
                      S.op("act", I("activation", out=tf[:, 0:nh, :], in_=p3[:, :, 0:16], func=AF.Copy), r=[pk], w=[("tfr", qi)])
                      if not on("rope"):
                          continue
                      rr = rp[qi]
                      for h0_ in range(0, nh, 4):
                          hn = min(4, nh - h0_)
                          for j, (lo, tabn) in enumerate([(0, "cos"), (8, "sin"), (8, "cos"), (0, "sin")]):
                              S.op("dve", I("tensor_tensor",
                                  out=rr[:, j, h0_:h0_ + hn, :], in0=tf[:, h0_:h0_ + hn, lo:lo + 8], in1=cs[tabn][:, t, 0:hn, :], op=ALU.mult),
                                  r=[("tfr", qi), "c_cos", "c_sin"], w=[("rp", qi, j)])
                      S.op("dve", I("tensor_tensor",
                          out=tqv[:, :, 0:8], in0=rr[:, 0, 0:nh, :], in1=rr[:, 1, 0:nh, :], op=ALU.subtract),
                          r=[("rp", qi, 0), ("rp", qi, 1)], w=[("tq", qi)])
                      S.op("dve", I("tensor_tensor",
                          out=tqv[:, :, 8:16], in0=rr[:, 2, 0:nh, :], in1=rr[:, 3, 0:nh, :], op=ALU.add),
                          r=[("rp", qi, 2), ("rp", qi, 3)], w=[("tq", qi)])
                      def p1_tail(b=b, t=t, ts=ts, qi=qi):
                          c0, c1 = QK_CHUNK_RANGES[b]
                          bt = bank()
                          pt_, ptk = PS(bt)
                          ptb = pt_.bitcast(BF)
                          for i in range(c1 - c0):
                              tr(ptb[:, i * 128:(i + 1) * 128], tq[qi][:, i * 128:(i + 1) * 128], identb, r=[("tq", qi), "identb"], w=[ptk])
                          for (a, bnd, dst, dk, base) in [(c0, min(c1, 10), qT, "qT", 0), (max(c0, 10), c1, kT, "kT", 10)]:
                              if bnd > a:
                                  S.op("act", I("activation",
                                      out=dst[:, a - base:bnd - base, ts],
                                      in_=ptb[:, (a - c0) * 128:(bnd - c0) * 128].rearrange("p (c t) -> p c t", t=128), func=AF.Copy),
                                      r=[ptk], w=[(dk, t)])
                      p1pend.append(p1_tail)
                      while len(p1pend) > 2:
                          p1pend.pop(0)()
              while p1pend:
                  p1pend.pop(0)()

              stop_at('p1')
              moff = {}
              o_ = 0
              for (name, W, r_) in GROUPS:
                  moff[name] = o_
                  o_ += (2 * GD[name] + 1) if name != "g2" else 6
              set_pool(range(4))
              if sq == 0:
                  precast_begin(l)
              p2pend = []
              lagq = []
              started = set()

              def stage(f2, lag):
                  lagq.append(f2)
                  while len(lagq) > lag:
                      lagq.pop(0)()
              o2T = [tq[0], tq[1], T[:, 1024:1536], T[:, 1536:2048]]
              o2k = [[("tq", 0)], [("tq", 1)], [("rp", 0, q_) for q_ in range(4)], [("rp", 1, q_) for q_ in range(4)]]
              for j in range(NT):
                  js = slice(j * 128, (j + 1) * 128)
                  oi = rot("oab", 2)
                  accs = [5, 6, 7]
                  si = rot("sm", 4)
                  st = sm[si]
                  if j % 4 == 0:
                      for s_ in range(4):
                          i_ = 8 + s_
                          qc2, kc2, hf2, vh2 = 4 + i_ // 2, 1 + i_ // 2, i_ % 2, 2 + i_
                          rows = slice(hf2 * 64, (hf2 + 1) * 64)
                          pa2, pa2k = PS(4)
                          first2 = True
                          for kt in range(max(0, j - 8), min(NT - 1, j + 3 + 8) + 1):
                              qa, qb = max(j, kt - 8), min(j + 3, kt + 8) + 1
                              n = qb - qa
                              pb, pk = PS(bank())
                              mm(pb[:, 0:n * 128], kT[rows, kc2, kt * 128:(kt + 1) * 128], qT[rows, qc2, qa * 128:qb * 128],
                                 True, False, r=[("kT", kt)] + [("qT", q_) for q_ in range(qa, qb)], w=[pk])
                              c_ = 0
                              while c_ < n:
                                  dl = kt - (qa + c_)
                                  if dl == 8:
                                      m0, ln = (moff["g2"] + 5) * 128, 1
                                  elif dl == -8:
                                      m0, ln = moff["g2"] * 128, 1
                                  else:
                                      ln = min(n - c_, 4, dl + 8)
                                      m0 = (moff["g2"] + 1) * 128
                                  mm(pb[:, c_ * 128:(c_ + ln) * 128], identb, cs["masks"][:, m0:m0 + ln * 128], False, c_ + ln == n,
                                     r=["identb", "c_masks"], w=[pk])
                                  c_ += ln
                              pi = rot("PT", NPT)
                              S.op("act", I("activation", out=PT[pi][:, 0:n * 128], in_=pb[:, 0:n * 128], func=AF.Exp, scale=0.125),
                                   r=[pk], w=[("PT", pi)])
                              if sq == 0:
                                  precast_tick()
                              stage(lambda pa2=pa2, pa2k=pa2k, qa=qa, qb=qb, kt=kt, vh2=vh2, pi=pi, n=n, first2=first2, j=j: mm(
                                  pa2[0:65, (qa - j) * 128:(qb - j) * 128], vv[:, kt, vh2, :], PT[pi][:, 0:n * 128], first2, False,
                                  r=[("PT", pi), ("v", kt)], w=[pa2k]), 2)
                              first2 = False
                          stage(lambda pa2=pa2, pa2k=pa2k, s_=s_: S.op("act", I("activation", out=o2T[s_][0:65, :], in_=pa2[0:65, :], func=AF.Copy),
                                                                       r=[pa2k], w=o2k[s_]), 2)

                  def pair_blocks(gname, qc, kc_, vhs, accb, slots):
                      Dm = GD[gname]
                      lo, hi = max(-Dm, -j), min(Dm, NT - 1 - j)
                      dls = list(range(lo, hi + 1))
                      pa, pak = PS(accb)
                      for b0 in range(0, len(dls), 4):
                          batch = dls[b0:b0 + 4]
                          n = len(batch)
                          pbs = [PS(bank()), PS(bank())]
                          for i, dl in enumerate(batch):
                              for hf in range(2):
                                  rows = slice(hf * 64, (hf + 1) * 64)
                                  pb, pk = pbs[hf]
                                  mm(pb[:, i * 128:(i + 1) * 128], kT[rows, kc_, (j + dl) * 128:(j + dl + 1) * 128], qT[rows, qc, js],
                                     i == 0, False, r=[("kT", j + dl), ("qT", j)], w=[pk])
                          if gname != "g2":
                              m0 = (moff[gname] + batch[0] + Dm) * 128
                          elif batch[0] == -8:
                              m0 = moff[gname] * 128
                          elif batch[-1] == 8:
                              m0 = (moff[gname] + 6 - n) * 128
                          else:
                              m0 = (moff[gname] + 1) * 128
                          pis = []
                          for hf in range(2):
                              pb, pk = pbs[hf]
                              mm(pb[:, 0:n * 128], identb, cs["masks"][:, m0:m0 + n * 128], False, True, r=["identb", "c_masks"], w=[pk])
                          for hf in range(2):
                              pb, pk = pbs[hf]
                              pi = rot("PT", NPT)
                              pis.append(pi)
                              S.op("act", I("activation", out=PT[pi][:, 0:n * 128], in_=pb[:, 0:n * 128], func=AF.Exp, scale=0.125),
                                   r=[pk], w=[("PT", pi)])
                          if sq == 0:
                              precast_tick()

                          def pvs(pis=pis, batch=batch, pa=pa, pak=pak, j=j):
                              for hf in range(2):
                                  pi = pis[hf]
                                  for i, dl in enumerate(batch):
                                      st_ = (j, accb) not in started
                                      started.add((j, accb))
                                      mm(pa[:, slots[hf] * 128:slots[hf] * 128 + 65], PT[pi][:, i * 128:(i + 1) * 128], vv[:, j + dl, vhs[hf], :],
                                         st_, False, r=[("PT", pi), ("v", j + dl)], w=[pak])
                          stage(pvs, 1)

                  def normalise(bi, st=st, si=si, oi=oi, l=l):
                      pa, pak = PS(accs[bi])
                      pv = pa.rearrange("p (s c) -> p s c", c=128)
                      if bi < 2:
                          S.op("dve", I("tensor_tensor",
                              out=st[:, 20 + bi * 4:24 + bi * 4], in0=pv[:, :, 64], in1=esink[:, l * 8 + bi * 4:l * 8 + bi * 4 + 4], op=ALU.add),
                              r=[pak, "esink"], w=[("smd", si, bi)])
                      else:
                          S.op("dve", I("tensor_copy", out=st[:, 20 + bi * 4:24 + bi * 4], in_=pv[:, :, 64]),
                               r=[pak], w=[("smd", si, bi)])
                      S.op("dve", I("reciprocal", out=st[:, 20 + bi * 4:24 + bi * 4], in_=st[:, 20 + bi * 4:24 + bi * 4]),
                           r=[("smd", si, bi)], w=[("smr", si, bi)])
                      for s_ in range(4):
                          S.op("dve", I("tensor_scalar",
                              out=oab[oi][:, bi * 4 + s_, :], in0=pv[:, s_, 0:64], scalar1=st[:, 20 + bi * 4 + s_:21 + bi * 4 + s_],
                              scalar2=None, op0=ALU.mult), r=[pak, ("smr", si, bi)], w=[("oab", oi, bi)])

                  for s0 in (0, 2):
                      for g, gname in enumerate(["g0", "g1"]):
                          i_ = g * 4 + s0
                          pair_blocks(gname, 4 + i_ // 2, 1 + i_ // 2, (2 + i_, 3 + i_), accs[2], (s0, s0 + 1))
                      def g2acc(s0=s0, j=j):
                          for s_ in (s0, s0 + 1):
                              pa7, pa7k = PS(accs[2])
                              mm(pa7[:, s_ * 128:s_ * 128 + 65], o2T[s_][0:65, (j % 4) * 128:(j % 4 + 1) * 128], identb[0:65, 0:65],
                                 False, False, r=o2k[s_] + ["identb"], w=[pa7k])
                      stage(g2acc, 1)
                  stage(lambda: normalise(2), 1)
                  for c in range(4):
                      pair_blocks("A", c, 0, (0, 1), accs[c // 2], ((2 * c) % 4, (2 * c + 1) % 4))
                      if c == 1:
                          stage(lambda: normalise(0), 1)
                  stage(lambda: normalise(1), 1)
                  def p2_tail(oi=oi, js=js, j=j):
                      bt = bank()
                      pt_, ptk = PS(bt)
                      ptb = pt_.bitcast(BF)
                      of = oab[oi].rearrange("p h d -> p (h d)")
                      for i in range(6):
                          tr(ptb[:, i * 128:(i + 1) * 128], of[:, i * 128:(i + 1) * 128], identb,
                             r=[("oab", oi, 0), ("oab", oi, 1), ("oab", oi, 2), "identb"], w=[ptk])
                      S.op("act", I("activation",
                          out=qT[:, 0:6, js], in_=ptb[:, 0:768].rearrange("p (c t) -> p c t", t=128), func=AF.Copy), r=[ptk], w=[("qT", j)])
                  stage(p2_tail, 1)
              while lagq:
                  lagq.pop(0)()
              if sq == 0:
                  while pc["n"] < len(pc["jobs"]):
                      precast_step()

              stop_at('p2a')
              fence(RKEYS_A + TA, RKEYS_B + TB)
              set_pool(range(8))
              f0 = False

              def ld_wg(cb):
                  load_block(l, [("wg", cb, 0), ("wg", cb, 1)], wg[:, :, cb * 512:(cb + 1) * 512], ("wg", cb), f0)

              def ld_wa(cb):
                  load_block(l, [("wa", cb)], wa[:, :, cb * 512:(cb + 1) * 512], ("wa", cb), f0)

              def ld_wo(cb):
                  load_block(l, [("wo", cb, 0), ("wo", cb, 1)], wo[:, :, cb * 512:(cb + 1) * 512], ("wo", cb), f0)
              ld_wg(0)
              ld_wg(2)
              ld_wa(0)
              load_block(l, [("wb",)], wb, "wb", f0)
              ld_wg(1)
              ld_wg(3)
              ld_wa(1)
              ld_wo(0)
              ld_wo(1)
              pend = []

              def flush():
                  while pend:
                      pend.pop(0)()
              for tc in range(4):
                  cs_ = slice(tc * 512, (tc + 1) * 512)
                  tiles = list(range(tc * 4, tc * 4 + 4))
                  xk = [("xh", t) for t in tiles]
                  qk_ = [("qT", t) for t in tiles]
                  for dc in range(8):
                      if dc == 2:
                          flush()
                      tg_ = []
                      for gi in range(2):
                          b_ = bank()
                          pb, pk = PS(b_)
                          col = gi * 1024 + dc * 128
                          for k in range(8):
                              mm(pb, wg[:, k, col:col + 128], xh[:, k, cs_], k == 0, k == 7, r=[(("wg", col // 512), k)] + xk, w=[pk])
                          fi = rot("f", 3)
                          S.op("act", I("activation",
                              out=f512[fi], in_=pb, func=AF.Tanh, scale=0.5, bias=bgh[:, l, gi * 8 + dc:gi * 8 + dc + 1]),
                              r=[pk, "bgh"], w=[("f", fi)])
                          tg_.append(fi)
                      ba = bank()
                      pa, pak = PS(ba)
                      for k in range(4):
                          mm(pa, wa[:, k, dc * 128:(dc + 1) * 128], qT[:, k, cs_], k == 0, k == 3, r=[(("wa", dc // 4), k)] + qk_, w=[pak])
                      bb_ = bank()
                      pb2, pbk = PS(bb_)
                      for k in range(2):
                          mm(pb2, wb[:, k, dc * 128:(dc + 1) * 128], qT[:, 4 + k, cs_], k == 0, k == 1, r=[("wb", k)] + qk_, w=[pbk])
                      fa, fb_ = tg_
                      S.op("dve", I("scalar_tensor_tensor",
                          out=f512[fa], in0=f512[fa], scalar=1.0, in1=pa, op0=ALU.add, op1=ALU.mult), r=[("f", fa), pak], w=[("f", fa)])
                      S.op("dve", I("scalar_tensor_tensor",
                          out=f512[fb_], in0=f512[fb_], scalar=1.0, in1=pb2, op0=ALU.add, op1=ALU.mult), r=[("f", fb_), pbk], w=[("f", fb_)])
                      S.op("dve", I("tensor_tensor", out=f512[fa], in0=f512[fa], in1=f512[fb_], op=ALU.add),
                           r=[("f", fa), ("f", fb_)], w=[("f", fa)])
                      S.op("act", I("activation", out=mg[:, dc, :], in_=f512[fa], func=AF.Copy, scale=C_HALF), r=[("f", fa)], w=["mg"])
                  for ti, t in enumerate(tiles):
                      ts = slice(t * 128, (t + 1) * 128)
                      bks = [bank(), bank()]
                      for cb in range(2):
                          pb, pk = PS(bks[cb])
                          for k in range(8):
                              mm(pb, mg[:, k, ti * 128:(ti + 1) * 128], wo[:, k, cb * 512:(cb + 1) * 512], k == 0, False,
                                 r=["mg", (("wo", cb), k)], w=[pk])
                          for c in range(4):
                              mm(pb[:, c * 128:(c + 1) * 128], xh[:, cb * 4 + c, ts], identb, False, False, r=[("xh", t), "identb"], w=[pk])
                              mm(pb[:, c * 128:(c + 1) * 128], xl[:, cb * 4 + c, ts], identb, False, c == 3, r=[("xl", t), "identb"], w=[pk])
                      rstd, nmr, smk = layer_norm_stats([psb[bks[0]], psb[bks[1]]], [("ps", bks[0]), ("ps", bks[1])])
                      dbg_dump("sml", t, sm[(state["sm"] + 1) % 2], smk, 32)
                      zi = rot("zb", 3)
                      for cb in range(2):
                          S.op("act", I("activation",
                              out=ztb[zi][:, cb * 512:(cb + 1) * 512], in_=psb[bks[cb]], func=AF.Identity, scale=rstd, bias=nmr),
                              r=[("ps", bks[cb])] + smk, w=[("zt", zi)])
                      pend.append(lambda t=t, zi=zi, l=l: to_hilo(t, ztb[zi], ("zt", zi), ("g1", "b1", l)))
                      while len(pend) > 2:
                          pend.pop(0)()
              flush()

              stop_at('p2b')
              fence(RKEYS_B + [("qT", t) for t in range(NT)], RKEYS_C)
              set_pool(range(4))
              for cb in range(2):
                  load_block(l, [("wpg", cb, 0), ("wpg", cb, 1)], wpg[:, :, cb * 512:(cb + 1) * 512], ("wpg", cb), False)
              load_block(l, [("wp",)], wpl, "wpl", False)
              jobs = []
              for tc_ in range(4):
                  for fb in range(8):
                      jobs.append(([("wu", fb, 0), ("wu", fb, 1)], False))
                  for cb in range(2):
                      for fgp in range(4):
                          jobs.append(([("wd", cb, fgp * 2), ("wd", cb, fgp * 2 + 1)], False))
              jst = {"issued": 0, "used": 0}

              def next_block():
                  n = jst["used"]
                  while jst["issued"] < min(n + 3, len(jobs)):
                      m = jst["issued"]
                      load_block(l, jobs[m][0], wbuf[m % 3], ("wbuf", m % 3), jobs[m][1])
                      jst["issued"] += 1
                  jst["used"] += 1
                  return n % 3
              S.op("sp", I("dma_start", out=bpgbv, in_=cd["bpgb"][l]), w=["bpgb"], dma="bpgb")
              if last and sq == 0:
                  pass
              if last:
                  S.op("sp", I("dma_start", out=lnfg, in_=cd["lnfg"]), w=["lnfg"], dma="lnfg")
                  S.op("sp", I("dma_start", out=lnfb, in_=cd["lnfb"]), w=["lnfb"], dma="lnfb")
              for tc in range(4):
                  cs_ = slice(tc * 512, (tc + 1) * 512)
                  tiles = list(range(tc * 4, tc * 4 + 4))
                  xk = [("xh", t) for t in tiles]
                  pTi = {}
                  for fb in range(8):
                      if fb == 4:
                          flush()
                      wi = next_block()
                      for fc in range(4):
                          b_ = bank()
                          pb, pk = PS(b_)
                          for k in range(8):
                              mm(pb, wbuf[wi][:, k, fc * 128:(fc + 1) * 128], xh[:, k, cs_], k == 0, k == 7, r=[(("wbuf", wi), k)] + xk, w=[pk])
                          fi = rot("f", 3)
                          S.op("act", I("activation", out=f512[fi], in_=pb, func=AF.Relu, scale=RELU_S),
                               r=[pk], w=[("f", fi)])
                          S.op("pool", I("tensor_tensor", out=uT[:, fb * 4 + fc, :], in0=f512[fi], in1=f512[fi],
                                                                                    op=ALU.mult), r=[("f", fi)], w=["uT"])
                  for cb in range(2):
                      ccs = slice(cb * 512, (cb + 1) * 512)
                      accb = [4, 5, 6, 7]
                      for fgp in range(4):
                          wi = next_block()
                          for f in range(8):
                              for ti in range(4):
                                  pb, pk = PS(accb[ti])
                                  mm(pb, uT[:, fgp * 8 + f, ti * 128:(ti + 1) * 128], wbuf[wi][:, f, :], fgp == 0 and f == 0, False,
                                     r=["uT", (("wbuf", wi), f)], w=[pk])
                      for ti, t in enumerate(tiles):
                          ts = slice(t * 128, (t + 1) * 128)
                          pb, pk = PS(accb[ti])
                          for c in range(4):
                              mm(pb[:, c * 128:(c + 1) * 128], xh[:, cb * 4 + c, ts], identb, False, False, r=[("xh", t), "identb"], w=[pk])
                              mm(pb[:, c * 128:(c + 1) * 128], xl[:, cb * 4 + c, ts], identb, False, c == 3, r=[("xl", t), "identb"], w=[pk])
                          bg_ = bank()
                          pg, pgk = PS(bg_)
                          for k in range(8):
                              mm(pg, xh[:, k, ts], wpg[:, k, ccs], k == 0, k == 7, r=[("xh", t), (("wpg", cb), k)], w=[pgk])
                          fi = rot("f", 3)
                          S.op("dve", I("tensor_tensor", out=f512[fi], in0=pg, in1=bpgbv[:, ccs], op=ALU.add),
                               r=[pgk, "bpgb"], w=[("f", fi)])
                          S.op("act", I("activation", out=f512[fi], in_=f512[fi], func=AF.Tanh, scale=0.5), r=[("f", fi)], w=[("f", fi)])
                          if cb == 0:
                              pi_ = rot("pin", 2)
                              S.op("sp", I("dma_start", out=pin[pi_], in_=p_d[l, sq, t * 128:(t + 1) * 128, :]),
                                   w=[("pin", pi_)], dma=("pin", pi_))
                              bt = bank()
                              pt_, ptk = PS(bt)
                              for k2 in range(2):
                                  tr(pt_[:, k2 * 128:(k2 + 1) * 128], pin[pi_][:, k2 * 128:(k2 + 1) * 128], cs["identf"],
                                     r=[("pin", pi_), "c_identf"], w=[ptk])
                              S.op("act", I("activation",
                                  out=pT[ti], in_=pt_[:, 0:256].rearrange("p (k t) -> p k t", t=128), func=AF.Copy), r=[ptk], w=[("pT", ti)])
                          pi_ = ti
                          bw = bank()
                          pw_, pwk = PS(bw)
                          for k2 in range(2):
                              mm(pw_, pT[pi_][:, k2, :], wpl[:, k2, ccs], k2 == 0, k2 == 1, r=[("pT", pi_), ("wpl", k2)], w=[pwk])
                          S.op("dve", I("scalar_tensor_tensor",
                              out=f512[fi], in0=f512[fi], scalar=1.0, in1=pw_, op0=ALU.add, op1=ALU.mult), r=[("f", fi), pwk], w=[("f", fi)])
                          S.op("dve", I("scalar_tensor_tensor",
                              out=yb[:, ti, ccs], in0=f512[fi], scalar=C_HALF, in1=pb, op0=ALU.mult, op1=ALU.add),
                              r=[("f", fi), pk], w=[("yb", ti, cb)])
                  def tail(tiles=tiles, l=l, sq=sq, last=last):
                      for ti, t in enumerate(tiles):
                          yk = [("yb", ti, 0), ("yb", ti, 1)]
                          rstd, nmr, smk = layer_norm_stats([yb[:, ti, 0:512], yb[:, ti, 512:1024]], yk)
                          S.op("act", I("activation",
                              out=yb[:, ti, :], in_=yb[:, ti, :], func=AF.Identity, scale=rstd, bias=nmr), r=yk + smk, w=yk)
                      for ti, t in enumerate(tiles):
                          yk = [("yb", ti, 0), ("yb", ti, 1)]
                          if not last:
                              to_hilo(t, yb[:, ti, :], yk, ("g2", "b2", l))
                          else:
                              S.op("dve", I("tensor_tensor", out=yb[:, ti, :], in0=yb[:, ti, :], in1=lnfg, op=ALU.mult),
                                   r=yk + ["lnfg"], w=yk)
                              S.op("dve", I("tensor_tensor", out=yb[:, ti, :], in0=yb[:, ti, :], in1=lnfb, op=ALU.add),
                                   r=yk + ["lnfb"], w=yk)
                              S.op("sp", I("dma_start", out=out_d[sq, t * 128:(t + 1) * 128, :], in_=yb[:, ti, :]),
                                   r=yk, w=[("out", ti)], dma=("out", ti))
                  pend.append(tail)
              flush()

    except _Stop:
        pass
    S.emit(final_slots=[s for s in S.slots if isinstance(s, tuple) and s[0] == "out"])
    return nc


_CACHE = {}


def kernel(**inputs):
    inp = {k: np.asarray(v) for k, v in inputs.items()}
    wall = pack_weights(inp)
    consts = make_consts(inp)
    if "nc" not in _CACHE:
        _CACHE["nc"] = build()
    nc = _CACHE["nc"]
    in_maps = []
    for c in range(8):
        m = {"x": np.ascontiguousarray(inp["x"][2 * c:2 * c + 2]),
             "p": np.ascontiguousarray(inp["p"][:, 2 * c:2 * c + 2]),
             "w": wall}
        for k, v in consts.items():
            m["c_" + k] = v
        in_maps.append(m)
    res = run_bass_kernel_spmd(nc, in_maps, core_ids=list(range(8)))
    return np.concatenate([r["out"] for r in res.results], axis=0).astype(np.float32)
```

```python
import contextlib
import numpy as np
import ml_dtypes
import concourse.bass as bass
import concourse.mybir as mybir
from concourse.bass_utils import run_bass_kernel_spmd

F32 = mybir.dt.float32
BF = mybir.dt.bfloat16
AF = mybir.ActivationFunctionType
ALU = mybir.AluOpType

ENGS = ("pe", "act", "dve", "pool", "sp")
S_LEN = 2048
NT = 16
D = 1024
ALPHA = 8.0 ** 0.25
EPS_P = 1e-5 / (ALPHA * ALPHA)
C_HALF = 0.5 / ALPHA
RELU_S = ALPHA ** -0.5
A_PERM = [0, 4, 1, 5, 2, 6, 3, 7]
GROUPS = [("A", 128, 1), ("g0", 64, 1), ("g1", 256, 4), ("g2", 1024, 16)]
GD = {"A": 1, "g0": 1, "g1": 2, "g2": 8}


class Sched:
    def __init__(self, nc):
        self.nc = nc
        self.ops = {e: [] for e in ENGS}
        self.lastw = {}
        self.readers = {}
        self.slot_cnt = {}
        self.slots = []

    def op(self, eng, fn, r=(), w=(), dma=None):
        idx = len(self.ops[eng])
        w = list(w) + [("psr",) + tuple(k[1:]) for k in r if isinstance(k, tuple) and k[0] == "ps"]
        deps = set()
        for k in r:
            t = self.lastw.get(k)
            if t is not None:
                deps.add((t, "raw"))
        for k in w:
            t = self.lastw.get(k)
            if t is not None:
                deps.add((t, "waw"))
            for t in self.readers.get(k, ()):
                deps.add((t, "war"))
        if dma is not None:
            if dma not in self.slot_cnt:
                self.slot_cnt[dma] = 0
                self.slots.append(dma)
            self.slot_cnt[dma] += 1
            tok = ("d", dma, self.slot_cnt[dma])
        else:
            tok = ("c", eng, idx)
        for k in w:
            self.lastw[k] = tok
            self.readers[k] = []
        for k in r:
            self.readers.setdefault(k, []).append(tok)
        self.ops[eng].append(dict(fn=fn, deps=deps, dma=dma, sig=False))
        return tok

    def emit(self, final_slots=()):
        nc = self.nc
        for e in ENGS:
            for i, o in enumerate(self.ops[e]):
                lst = {}
                for (t, kind) in o["deps"]:
                    if t[0] == "c":
                        _, pe_, pi = t
                        if pe_ == e and o["dma"] is None:
                            if pe_ == "pe":
                                continue
                            if kind != "raw" or pi < i - 2:
                                continue
                        k = ("c", pe_)
                        lst[k] = max(lst.get(k, -1), pi)
                    else:
                        _, slot, cnt = t
                        k = ("d", slot)
                        lst[k] = max(lst.get(k, -1), cnt)
                o["need"] = lst
                for k, v in lst.items():
                    if k[0] == "c":
                        self.ops[k[1]][v]["sig"] = True
        cum = {}
        for e in ENGS:
            c = 0
            arr = []
            for o in self.ops[e]:
                if o["sig"] and o["dma"] is None:
                    c += 1
                arr.append(c)
            cum[e] = arr
        with contextlib.ExitStack() as st:
            esem = {e: st.enter_context(nc.semaphore("s_" + e)) for e in ENGS}
            dsem = {s: st.enter_context(nc.semaphore("d_%d" % i)) for i, s in enumerate(self.slots)}
            block = st.enter_context(nc.Block())

            def run(e, eng):
                waited = {}
                for o in self.ops[e]:
                    for k, v in o["need"].items():
                        if k[0] == "c":
                            val = cum[k[1]][v]
                            sem = esem[k[1]]
                        else:
                            val = 16 * v
                            sem = dsem[k[1]]
                        if waited.get(k, 0) < val:
                            eng.wait_ge(sem, val)
                            waited[k] = val
                    ins = o["fn"](eng)
                    if o["dma"] is not None:
                        ins.then_inc(dsem[o["dma"]], 16)
                    elif o["sig"]:
                        ins.then_inc(esem[e], 1)
                if e == "sp":
                    for s in final_slots:
                        eng.wait_ge(dsem[s], 16 * self.slot_cnt[s])

            @block.tensor
            def _(eng):
                run("pe", eng)

            @block.scalar
            def _(eng):
                run("act", eng)

            @block.vector
            def _(eng):
                run("dve", eng)

            @block.gpsimd
            def _(eng):
                run("pool", eng)

            @block.sync
            def _(eng):
                run("sp", eng)


def _qkv_cols():
    def head(base, h):
        return list(range(base + h * 64, base + (h + 1) * 64))
    chunks = []
    for (h0, h1) in [(0, 4), (1, 5), (2, 6), (3, 7)]:
        chunks.append(head(0, h0) + head(0, h1))
    for i in range(6):
        chunks.append(head(768, 2 * i) + head(768, 2 * i + 1))
    chunks.append(head(512, 0) + head(512, 1))
    for i in range(6):
        chunks.append(head(1536, 2 * i) + head(1536, 2 * i + 1))
    blocks = []
    for (a, b) in [(0, 4), (4, 8), (8, 11), (11, 15), (15, 17)]:
        blocks.append(sum(chunks[a:b], []))
    vheads = [head(640, 0), head(640, 1)] + [head(2304, i) for i in range(12)]
    blocks.append(sum(vheads[0:8], []))
    blocks.append(sum(vheads[8:14], []))
    return blocks


QKV_BLOCKS = _qkv_cols()
QK_CHUNK_RANGES = [(0, 4), (4, 8), (8, 11), (11, 15), (15, 17)]


def piece_table():
    tab = {}
    off = 0

    def add(name, kc, ncols):
        nonlocal off
        tab[name] = (off, kc, ncols)
        off += kc * ncols
    for b in range(7):
        for kh in range(2):
            add(("qkv", b, kh), 4, len(QKV_BLOCKS[b]))
    for cb in range(4):
        for kh in range(2):
            add(("wg", cb, kh), 4, 512)
    for cb in range(2):
        add(("wa", cb), 4, 512)
    add(("wb",), 2, 1024)
    for cb in range(2):
        for kh in range(2):
            add(("wo", cb, kh), 4, 512)
    for fb in range(8):
        for kh in range(2):
            add(("wu", fb, kh), 4, 512)
    for cb in range(2):
        for fg in range(8):
            add(("wd", cb, fg), 4, 512)
    for cb in range(2):
        for kh in range(2):
            add(("wpg", cb, kh), 4, 512)
    add(("wp",), 2, 1024)
    return tab, off


PIECES, WTOT = piece_table()


def _pm(W):
    K, C = W.shape
    return np.ascontiguousarray(W.reshape(K // 128, 128, C).transpose(1, 0, 2))


def pack_weights(inp):
    out = np.empty((4, 128, WTOT), np.float32)
    for l in range(4):
        def put(name, arr):
            off, kc, nc_ = PIECES[name]
            out[l, :, off:off + kc * nc_] = arr.reshape(128, kc * nc_)
        w_in = inp["w_in"][l]
        for b in range(7):
            wp = _pm(w_in[:, QKV_BLOCKS[b]])
            for kh in range(2):
                put(("qkv", b, kh), wp[:, kh * 4:(kh + 1) * 4, :])
        wg = _pm(w_in[:, 3072:5120])
        for cb in range(4):
            for kh in range(2):
                put(("wg", cb, kh), wg[:, kh * 4:(kh + 1) * 4, cb * 512:(cb + 1) * 512])
        rows = sum([list(range(h * 64, (h + 1) * 64)) for h in A_PERM], [])
        wa = _pm(inp["w_branch_a"][l][rows, :])
        for cb in range(2):
            put(("wa", cb), wa[:, :, cb * 512:(cb + 1) * 512])
        put(("wb",), _pm(inp["w_branch_b"][l]))
        wo = _pm(inp["w_out"][l])
        for cb in range(2):
            for kh in range(2):
                put(("wo", cb, kh), wo[:, kh * 4:(kh + 1) * 4, cb * 512:(cb + 1) * 512])
        wu = _pm(inp["w_up"][l])
        for fb in range(8):
            for kh in range(2):
                put(("wu", fb, kh), wu[:, kh * 4:(kh + 1) * 4, fb * 512:(fb + 1) * 512])
        wd = _pm(inp["w_down"][l])
        for cb in range(2):
            for fg in range(8):
                put(("wd", cb, fg), wd[:, fg * 4:(fg + 1) * 4, cb * 512:(cb + 1) * 512])
        wpg = _pm(inp["w_ple_gate"][l])
        for cb in range(2):
            for kh in range(2):
                put(("wpg", cb, kh), wpg[:, kh * 4:(kh + 1) * 4, cb * 512:(cb + 1) * 512])
        put(("wp",), _pm(inp["w_ple"][l]))
    return out


def make_consts(inp, lastl=3):
    c = {}
    c["identf"] = np.eye(128, dtype=np.float32)
    kl = np.arange(128)[:, None]
    ql = np.arange(128)[None, :]
    cols = []
    for (name, W, r) in GROUPS:
        Dm = GD[name]
        dls = range(-Dm, Dm + 1) if name != "g2" else [-8, 0, 0, 0, 0, 8]
        for dl in dls:
            diff = 128 * dl + kl - ql
            cols.append((((np.abs(diff) <= W) & ((kl - ql) % r == 0)).astype(np.float32) - 1.0) * 30000.0)
    c["masks"] = np.concatenate(cols, axis=1).astype(ml_dtypes.bfloat16)
    pos = np.arange(S_LEN, dtype=np.float32)
    inv = (np.float32(500000.0) ** (-np.arange(0, 16, 2, dtype=np.float32) / np.float32(16))).astype(np.float32)
    ang = (pos[:, None] * inv[None, :]).astype(np.float32)
    c["cos"] = np.ascontiguousarray(np.broadcast_to(np.cos(ang).astype(np.float32).reshape(16, 128, 1, 8).transpose(1, 0, 2, 3), (128, 16, 4, 8)))
    c["sin"] = np.ascontiguousarray(np.broadcast_to(np.sin(ang).astype(np.float32).reshape(16, 128, 1, 8).transpose(1, 0, 2, 3), (128, 16, 4, 8)))
    def fm(v, nch):
        return np.ascontiguousarray(v.reshape(4, nch, 128).transpose(2, 0, 1))
    c["bg"] = fm(inp["b_gate"], 16)
    c["g1"] = fm(inp["ln1_g"], 8)
    c["b1"] = fm(inp["ln1_b"], 8)
    c["g2"] = fm(inp["ln2_g"], 8)
    c["b2"] = fm(inp["ln2_b"], 8)
    c["bpgb"] = np.ascontiguousarray(np.broadcast_to(inp["b_ple_gate"][:, None, :], (4, 128, 1024)))
    c["sink"] = np.ascontiguousarray(np.broadcast_to(inp["a_sink"][:, A_PERM].reshape(1, 32), (128, 32)))
    c["lnfg"] = np.ascontiguousarray(np.broadcast_to(inp["ln2_g"][lastl][None, :], (128, 1024)))
    c["lnfb"] = np.ascontiguousarray(np.broadcast_to(inp["ln2_b"][lastl][None, :], (128, 1024)))
    return c


CONST_SHAPES = {
    "identf": ([128, 128], F32), "masks": ([128, 17 * 128], BF), "cos": ([128, 16, 4, 8], F32),
    "sin": ([128, 16, 4, 8], F32), "bg": ([128, 4, 16], F32), "g1": ([128, 4, 8], F32),
    "b1": ([128, 4, 8], F32), "g2": ([128, 4, 8], F32), "b2": ([128, 4, 8], F32),
    "bpgb": ([4, 128, 1024], F32), "sink": ([128, 32], F32), "lnfg": ([128, 1024], F32),
    "lnfb": ([128, 1024], F32),
}


def build(nlayers=4, nseq=2):
    import os as _os
    nc = bass.Bass("TRN2", target_bir_lowering=False)
    S = Sched(nc)
    x_d = nc.dram_tensor("x", [nseq, S_LEN, D], F32, kind="ExternalInput").ap()
    p_d = nc.dram_tensor("p", [4, nseq, S_LEN, 256], F32, kind="ExternalInput").ap()
    w_d = nc.dram_tensor("w", [4, 128, WTOT], F32, kind="ExternalInput").ap()
    out_d = nc.dram_tensor("out", [nseq, S_LEN, D], F32, kind="ExternalOutput").ap()
    cd = {k: nc.dram_tensor("c_" + k, sh, dt, kind="ExternalInput").ap() for k, (sh, dt) in CONST_SHAPES.items()}
    KDBG = _os.environ.get("KDBG", "")
    dbg_d = nc.dram_tensor("dbg", [S_LEN, D], F32, kind="ExternalOutput").ap() if KDBG else None

    def dbg_dump(tag, t, ap, key, ncols=1024):
        if KDBG == tag:
            keys = key if isinstance(key, list) else [key]
            S.op("sp", I("dma_start", out=dbg_d[t * 128:(t + 1) * 128, 0:ncols], in_=ap), r=keys, w=[("dbg", t)], dma=("dbg", t % 4))

    def sb(name, shape, dt):
        return nc.alloc_sbuf_tensor(name, shape, dt).ap()

    xh = sb("xh", [128, 8, S_LEN], BF)
    xl = sb("xl", [128, 8, S_LEN], BF)
    RSZ = 49408
    R = sb("R", [128, RSZ], BF)
    wbuf = [sb("wbuf0", [128, 8, 512], BF),
            R[:, 40960:45056].rearrange("p (k c) -> p k c", c=512), R[:, 45056:49152].rearrange("p (k c) -> p k c", c=512)]
    NSTG = 3
    stg = [sb("stg%d" % i, [128, 512], F32) for i in range(NSTG)]
    wscr = nc.dram_tensor("wscr", [4, 128, WTOT], BF, kind="Internal").ap()
    T = sb("T", [128, 9216], BF)
    cs = {k: sb("k_" + k, sh, dt) for k, (sh, dt) in CONST_SHAPES.items() if k not in ("lnfg", "lnfb", "bpgb")}
    identb = sb("identb", [128, 128], BF)
    bgh = sb("bgh", [128, 4, 16], F32)
    esink = sb("esink", [128, 32], F32)
    mhalf = sb("mhalf", [128, 1], F32)
    onesc = sb("onesc", [128, 2], BF)
    tq = [T[:, i * 512:(i + 1) * 512] for i in range(2)] + [T[:, 7168:7680]]
    rp = [T[:, o_:o_ + 512].bitcast(F32).rearrange("p (a h d) -> p a h d", a=4, h=8) for o_ in (1024, 1536, 7680)]
    NPT = 6
    PT = [T[:, 2048 + i * 512:2048 + (i + 1) * 512] for i in range(3)] + [T[:, 5120 + i * 512:5120 + (i + 1) * 512] for i in range(3)]
    oab = [T[:, 3584 + i * 768:3584 + (i + 1) * 768].rearrange("p (h d) -> p h d", d=64) for i in range(2)]
    f512 = [T[:, i * 1024:(i + 1) * 1024].bitcast(F32) for i in range(3)]
    xf = [T[:, 3072:5120].bitcast(F32).rearrange("p (c t) -> p c t", t=128)]
    zt = [T[:, 5120 + i * 2048:5120 + (i + 1) * 2048].bitcast(F32) for i in range(2)]
    tfr = [T[:, o_:o_ + 256].bitcast(F32).rearrange("p (h d) -> p h d", d=16) for o_ in (6656, 6912, 8192)]
    TA = [("pcb",)] + [("tfr", i) for i in range(3)] + [("tq", i) for i in range(3)] + [("rp", i, j) for i in range(3) for j in range(4)] + [("PT", i) for i in range(6)] + [("oab", i, b) for i in range(2) for b in range(3)]
    TB = [("f", i) for i in range(3)] + [("xf", 0, h) for h in range(2)] + [("zt", i) for i in range(2)]
    sm = [sb("sm%d" % i, [128, 32], F32) for i in range(4)]
    pin = [sb("pin%d" % i, [128, 256], F32) for i in range(2)]
    pT = [sb("pT%d" % i, [128, 2, 128], BF) for i in range(4)]
    psb = [nc.alloc_psum_tensor("ps%d" % i, [128, 512], F32).ap() for i in range(8)]

    qT = R[:, 0:10 * 2048].rearrange("p (c t) -> p c t", t=2048)
    kT = R[:, 20480:20480 + 7 * 2048].rearrange("p (c t) -> p c t", t=2048)
    vv = R[:, 34816:34816 + 16 * 14 * 65].rearrange("p (t h d) -> p t h d", h=14, d=65)
    vflat = R[:, 34816:34816 + 16 * 14 * 65].rearrange("p (n d) -> p n d", d=65)
    o2 = 12288
    wg = R[:, o2:o2 + 8 * 2048].rearrange("p (k c) -> p k c", c=2048)
    wa = R[:, o2 + 16384:o2 + 16384 + 4 * 1024].rearrange("p (k c) -> p k c", c=1024)
    wb = R[:, o2 + 20480:o2 + 20480 + 2 * 1024].rearrange("p (k c) -> p k c", c=1024)
    wo = R[:, o2 + 22528:o2 + 22528 + 8 * 1024].rearrange("p (k c) -> p k c", c=1024)
    mg = R[:, o2 + 30720:o2 + 30720 + 8 * 512].rearrange("p (k c) -> p k c", c=512)
    uT = R[:, 0:32 * 512].rearrange("p (f t) -> p f t", t=512)
    yb = R[:, 16384:16384 + 8192].bitcast(F32).rearrange("p (t c) -> p t c", c=1024)
    wpg = R[:, 24576:24576 + 8 * 1024].rearrange("p (k c) -> p k c", c=1024)
    wpl = R[:, 32768:32768 + 2 * 1024].rearrange("p (k c) -> p k c", c=1024)
    lnfg = R[:, 34816:34816 + 2048].bitcast(F32)
    lnfb = R[:, 36864:36864 + 2048].bitcast(F32)
    bpgbv = R[:, 38912:38912 + 2048].bitcast(F32)
    ztb = [zt[0], zt[1], R[:, 47104:49152].bitcast(F32)]
    RKEYS_A = [("qT", t) for t in range(NT)] + [("kT", t) for t in range(NT)] + [("v", t) for t in range(NT)]
    RKEYS_B = ([(("wg", i), k) for i in range(4) for k in range(8)] + [(("wa", i), k) for i in range(2) for k in range(4)]
               + [(("wo", i), k) for i in range(2) for k in range(8)] + [("wb", k) for k in range(2)] + ["mg", ("zt", 2)])
    RKEYS_C = ["uT", "lnfg", "lnfb", "bpgb"] + [(("wpg", i), k) for i in range(2) for k in range(8)] + [("wpl", k) for k in range(2)] + [(("wbuf", i), k) for i in (1, 2) for k in range(8)] + [("yb", i, j) for i in range(4) for j in range(2)]

    state = {"zb": 0, "ce": 0, "bank": 0, "stg": 0, "wb": 0, "f": 0, "tq": 0, "PT": 0, "z": 0, "xf": 0, "sm": 0, "pin": 0, "oab": 0}

    def rot(name, n):
        i = state[name]
        state[name] = (i + 1) % n
        return i

    pool_ = {"l": list(range(8)), "i": 0}

    def set_pool(lst):
        pool_["l"] = list(lst)
        pool_["i"] = 0

    def bank():
        b = pool_["l"][pool_["i"] % len(pool_["l"])]
        pool_["i"] += 1
        return b

    def I(meth, *a, **kw):
        return lambda e: getattr(e, meth)(*a, **kw)

    def PS(i):
        return psb[i], ("ps", i)

    def mm(out, lhsT, rhs, start, stop, r, w):
        S.op("pe", I("matmul", out, lhsT=lhsT, rhs=rhs, start=start, stop=stop, skip_group_check=True), r=r, w=w)

    def tr(out, in_, ident, r, w):
        S.op("pe", I("transpose", out=out, in_=in_, identity=ident), r=r, w=w)

    def fence(old, new):
        if _os.environ.get("KOFF", "").find("fence") < 0:
            S.op("pool", I("nop", ), w=list(old) + list(new))

    def load_piece(l, name, dst, dkey):
        off, kc, ncol = PIECES[name]
        for k in range(kc):
            for c0 in range(0, ncol, 512):
                n = min(512, ncol - c0)
                si = rot("stg", 2)
                o = off + k * ncol + c0
                S.op("sp", I("dma_start", out=stg[si][:, 0:n], in_=w_d[l, :, o:o + n]),
                     w=[("stg", si)], dma=("stg", si))
                S.op("pool", I("tensor_copy", out=dst[:, k, c0:c0 + n], in_=stg[si][:, 0:n]),
                     r=[("stg", si)], w=[dkey])

    def load_block(l, pieces, dst, dkey, first):
        off0, _, ncol = PIECES[pieces[0]]
        K = sum(PIECES[p_][1] for p_ in pieces)
        skeys = [("scr", l, off0 + k * ncol + c0) for k in range(K) for c0 in range(0, ncol, 512)]
        sview = wscr[l, :, off0:off0 + K * ncol].rearrange("p (k c) -> p k c", c=ncol)
        if not first:
            S.op("sp", I("dma_start", out=dst, in_=sview), r=skeys, w=[(dkey, k) for k in range(K)], dma=("ld", dkey))
            return
        for k in range(K):
            for c0 in range(0, ncol, 512):
                n = min(512, ncol - c0)
                si = rot("stg", NSTG)
                o = off0 + k * ncol + c0
                S.op("sp", I("dma_start", out=stg[si][:, 0:n], in_=w_d[l, :, o:o + n]), w=[("stg", si)], dma=("stg", si))
                if rot("ce", 2) == 0:
                    S.op("dve", I("tensor_copy", out=dst[:, k, c0:c0 + n], in_=stg[si][:, 0:n]), r=[("stg", si)], w=[(dkey, k)])
                else:
                    S.op("act", I("activation", out=dst[:, k, c0:c0 + n], in_=stg[si][:, 0:n], func=AF.Copy), r=[("stg", si)], w=[(dkey, k)])
        S.op("sp", I("dma_start", out=sview, in_=dst), r=[(dkey, k) for k in range(K)], w=skeys, dma=("scrw", dkey))

    pc_out = [T[:, 8448:8960], T[:, 7168:7680]]
    pc_okey = [("pcb",), ("tq", 2)]
    pc = {"jobs": [], "n": 0, "in": 0}

    def precast_begin(l):
        names = []
        for cb in range(4):
            names += [("wg", cb, 0), ("wg", cb, 1)]
        names += [("wa", 0), ("wa", 1), ("wb",)]
        for cb in range(2):
            names += [("wo", cb, 0), ("wo", cb, 1)]
        for cb in range(2):
            names += [("wpg", cb, 0), ("wpg", cb, 1)]
        names += [("wp",)]
        for fb in range(8):
            names += [("wu", fb, 0), ("wu", fb, 1)]
        for cb in range(2):
            for fg in range(8):
                names += [("wd", cb, fg)]
        jobs = []
        for nm in names:
            off, kc, ncol = PIECES[nm]
            for k in range(kc):
                for c0 in range(0, ncol, 512):
                    jobs.append((l, off + k * ncol + c0, min(512, ncol - c0)))
        pc["jobs"], pc["n"], pc["in"] = jobs, 0, 0

    def precast_issue_in():
        m = pc["in"]
        if m < len(pc["jobs"]):
            l_, o, n_ = pc["jobs"][m]
            si = m % NSTG
            S.op("sp", I("dma_start", out=stg[si][:, 0:n_], in_=w_d[l_, :, o:o + n_]), w=[("stg", si)], dma=("stg", si))
            pc["in"] += 1

    def precast_tick():
        pc["tick"] = pc.get("tick", 0) + 1
        if pc["tick"] % 2 == 0:
            precast_step()

    def precast_step():
        n = pc["n"]
        if n >= len(pc["jobs"]):
            return
        while pc["in"] < min(n + NSTG, len(pc["jobs"])):
            precast_issue_in()
        l_, o, n_ = pc["jobs"][n]
        si, oi_ = n % NSTG, n % 2
        S.op("dve", I("tensor_copy", out=pc_out[oi_][:, 0:n_], in_=stg[si][:, 0:n_]), r=[("stg", si)], w=[pc_okey[oi_]])
        S.op("sp", I("dma_start", out=wscr[l_, :, o:o + n_], in_=pc_out[oi_][:, 0:n_]), r=[pc_okey[oi_]], w=[("scr", l_, o)],
             dma=("pco", oi_))
        pc["n"] += 1

    for k in cs:
        S.op("sp", I("dma_start", out=cs[k], in_=cd[k]), w=["c_" + k], dma="c_" + k)
    S.op("dve", I("tensor_copy", out=identb, in_=cs["identf"]), r=["c_identf"], w=["identb"])
    S.op("dve", I("memset", mhalf, -0.5), w=["mhalf"])
    S.op("dve", I("memset", onesc, 1.0), w=["onesc"])
    S.op("dve", I("tensor_scalar", out=bgh, in0=cs["bg"], scalar1=0.5, scalar2=None, op0=ALU.mult), r=["c_bg"], w=["bgh"])
    S.op("act", I("activation", out=esink, in_=cs["sink"], func=AF.Exp), r=["c_sink"], w=["esink"])
    CK = ["identb", "c_identf"]

    def to_hilo(t, src, skey, gb):
        xi = rot("xf", 1)
        skeys = skey if isinstance(skey, list) else [skey]
        for half in range(2):
            b = bank()
            pb, pk = PS(b)
            for c in range(4):
                tr(pb[:, c * 128:(c + 1) * 128], src[:, (half * 4 + c) * 128:(half * 4 + c + 1) * 128], cs["identf"],
                   r=skeys + ["c_identf"], w=[pk])
            if gb is None:
                S.op("act", I("activation",
                    out=xf[xi][:, half * 4:(half + 1) * 4, :], in_=pb.rearrange("p (c t) -> p c t", t=128), func=AF.Copy),
                    r=[pk], w=[("xf", xi, half)])
            else:
                g, bb, l = gb
                for c in range(4):
                    cc = half * 4 + c
                    S.op("dve", I("tensor_scalar",
                        out=xf[xi][:, cc, :], in0=pb[:, c * 128:(c + 1) * 128], scalar1=cs[g][:, l, cc:cc + 1],
                        scalar2=cs[bb][:, l, cc:cc + 1], op0=ALU.mult, op1=ALU.add),
                        r=[pk, "c_" + g, "c_" + bb], w=[("xf", xi, half)])
        ts = slice(t * 128, (t + 1) * 128)
        S.op("act", I("activation", out=xh[:, :, ts], in_=xf[xi], func=AF.Copy),
             r=[("xf", xi, 0), ("xf", xi, 1)], w=[("xh", t)])
        S.op("pool", I("tensor_tensor", out=xl[:, :, ts], in0=xf[xi], in1=xh[:, :, ts], op=ALU.subtract),
             r=[("xf", xi, 0), ("xf", xi, 1), ("xh", t)], w=[("xl", t)])

    def layer_norm_stats(src_aps, skeys):
        si = rot("sm", 4)
        st = sm[si]
        for i, (a, k) in enumerate(zip(src_aps, skeys)):
            S.op("dve", I("bn_stats", out=st[:, i * 6:(i + 1) * 6], in_=a), r=[k], w=[("sm", si, i)])
        S.op("dve", I("bn_aggr", out=st[:, 12:14], in_=st[:, 0:12]), r=[("sm", si, 0), ("sm", si, 1)], w=[("sm", si, 2)])
        S.op("dve", I("tensor_scalar", out=st[:, 14:15], in0=st[:, 13:14], scalar1=EPS_P, scalar2=None, op0=ALU.add),
             r=[("sm", si, 2)], w=[("sm", si, 3)])
        S.op("pool", I("tensor_tensor", out=st[:, 15:16], in0=st[:, 14:15], in1=mhalf, op=ALU.pow),
             r=[("sm", si, 3), "mhalf"], w=[("sm", si, 4)])
        S.op("dve", I("tensor_scalar", out=st[:, 16:17], in0=st[:, 12:13], scalar1=-1.0, scalar2=st[:, 15:16],
                                              op0=ALU.mult, op1=ALU.mult), r=[("sm", si, 2), ("sm", si, 4)], w=[("sm", si, 5)])
        return st[:, 15:16], st[:, 16:17], [("sm", si, 4), ("sm", si, 5)]

    import os as _os
    STOP = _os.environ.get("KSTOP", "")

    class _Stop(Exception):
        pass

    KOFF = _os.environ.get("KOFF", "").split(",")

    def on(name):
        return name not in KOFF

    def stop_at(name):
        if STOP == name:
            raise _Stop()
    try:
      for sq in range(nseq):
          for t in range(NT):
              zi = rot("z", 2)
              S.op("sp", I("dma_start", out=zt[zi], in_=x_d[sq, t * 128:(t + 1) * 128, :]),
                   w=[("zt", zi)], dma=("zt", zi))
              to_hilo(t, zt[zi], ("zt", zi), None)

          for l in range(nlayers):
              last = (l == nlayers - 1)
              stop_at('load')
              fence(RKEYS_C + TB, RKEYS_A + TA)
              set_pool(range(8))
              if on("ones"):
                  S.op("dve", I("tensor_copy", out=vflat[:, :, 64:65], in_=onesc[:, 0:1].unsqueeze(1).to_broadcast([128, 224, 1])),
                       r=["onesc"], w=[("v", t) for t in range(NT)])
              p1pend = []
              for b in range(7):
                  ncol = len(QKV_BLOCKS[b])
                  wi = 0
                  load_block(l, [("qkv", b, 0), ("qkv", b, 1)], wbuf[wi][:, :, 0:ncol], ("wbuf", wi), sq == 0)
                  for t in range(NT):
                      ts = slice(t * 128, (t + 1) * 128)
                      bk = bank()
                      pb, pk = PS(bk)
                      for k in range(8):
                          mm(pb[:, 0:ncol], xh[:, k, ts], wbuf[wi][:, k, 0:ncol], k == 0, k == 7,
                             r=[("xh", t), (("wbuf", wi), k)], w=[pk])
                      if not on("evac"):
                          continue
                      if b >= 5:
                          h0, nh = (0, 8) if b == 5 else (8, 6)
                          S.op("act", I("activation",
                              out=vv[:, t, h0:h0 + nh, 0:64], in_=pb[:, 0:ncol].rearrange("p (h d) -> p h d", d=64), func=AF.Copy),
                              r=[pk], w=[("v", t)])
                          continue
                      nh = ncol // 64
                      qi = rot("tq", 3)
                      tqv = tq[qi][:, 0:ncol].rearrange("p (h d) -> p h d", d=64)
                      p3 = pb[:, 0:ncol].rearrange("p (h d) -> p h d", d=64)
                      S.op("act", I("activation", out=tqv, in_=p3, func=AF.Copy), r=[pk], w=[("tq", qi)])
                      tf = tfr[qi]
                      S.op("act", I("activation", out=tf[:, 0:nh, :], in_=p3[:, :, 0:16], func=AF.Copy), r=[pk], w=[("tfr", qi)])
                      if not on("rope"):
                          continue
                      rr = rp[qi]
                      for h0_ in range(0, nh, 4):
                          hn = min(4, nh - h0_)
                          for j, (lo, tabn) in enumerate([(0, "cos"), (8, "sin"), (8, "cos"), (0, "sin")]):
                              S.op("dve", I("tensor_tensor",
                                  out=rr[:, j, h0_:h0_ + hn, :], in0=tf[:, h0_:h0_ + hn, lo:lo + 8], in1=cs[tabn][:, t, 0:hn, :], op=ALU.mult),
                                  r=[("tfr", qi), "c_cos", "c_sin"], w=[("rp", qi, j)])
                      S.op("dve", I("tensor_tensor",
                          out=tqv[:, :, 0:8], in0=rr[:, 0, 0:nh, :], in1=rr[:, 1, 0:nh, :], op=ALU.subtract),
                          r=[("rp", qi, 0), ("rp", qi, 1)], w=[("tq", qi)])
                      S.op("dve", I("tensor_tensor",
                          out=tqv[:, :, 8:16], in0=rr[:, 2, 0:nh, :], in1=rr[:, 3, 0:nh, :], op=ALU.add),
                          r=[("rp", qi, 2), ("rp", qi, 3)], w=[("tq", qi)])
                      def p1_tail(b=b, t=t, ts=ts, qi=qi):
                          c0, c1 = QK_CHUNK_RANGES[b]
                          bt = bank()
                          pt_, ptk = PS(bt)
                          ptb = pt_.bitcast(BF)
                          for i in range(c1 - c0):
                              tr(ptb[:, i * 128:(i + 1) * 128], tq[qi][:, i * 128:(i + 1) * 128], identb, r=[("tq", qi), "identb"], w=[ptk])
                          for (a, bnd, dst, dk, base) in [(c0, min(c1, 10), qT, "qT", 0), (max(c0, 10), c1, kT, "kT", 10)]:
                              if bnd > a:
                                  S.op("act", I("activation",
                                      out=dst[:, a - base:bnd - base, ts],
                                      in_=ptb[:, (a - c0) * 128:(bnd - c0) * 128].rearrange("p (c t) -> p c t", t=128), func=AF.Copy),
                                      r=[ptk], w=[(dk, t)])
                      p1pend.append(p1_tail)
                      while len(p1pend) > 2:
                          p1pend.pop(0)()
              while p1pend:
                  p1pend.pop(0)()

              stop_at('p1')
              moff = {}
              o_ = 0
              for (name, W, r_) in GROUPS:
                  moff[name] = o_
                  o_ += (2 * GD[name] + 1) if name != "g2" else 6
              set_pool(range(4))
              if sq == 0:
                  precast_begin(l)
              p2pend = []
              lagq = []
              started = set()

              def stage(f2, lag):
                  lagq.append(f2)
                  while len(lagq) > lag:
                      lagq.pop(0)()
              o2T = [tq[0], tq[1], T[:, 1024:1536], T[:, 1536:2048]]
              o2k = [[("tq", 0)], [("tq", 1)], [("rp", 0, q_) for q_ in range(4)], [("rp", 1, q_) for q_ in range(4)]]
              for j in range(NT):
                  js = slice(j * 128, (j + 1) * 128)
                  oi = rot("oab", 2)
                  accs = [5, 6, 7]
                  si = rot("sm", 4)
                  st = sm[si]
                  if j % 4 == 0:
                      for s_ in range(4):
                          i_ = 8 + s_
                          qc2, kc2, hf2, vh2 = 4 + i_ // 2, 1 + i_ // 2, i_ % 2, 2 + i_
                          rows = slice(hf2 * 64, (hf2 + 1) * 64)
                          pa2, pa2k = PS(4)
                          first2 = True
                          for kt in range(max(0, j - 8), min(NT - 1, j + 3 + 8) + 1):
                              qa, qb = max(j, kt - 8), min(j + 3, kt + 8) + 1
                              n = qb - qa
                              pb, pk = PS(bank())
                              mm(pb[:, 0:n * 128], kT[rows, kc2, kt * 128:(kt + 1) * 128], qT[rows, qc2, qa * 128:qb * 128],
                                 True, False, r=[("kT", kt)] + [("qT", q_) for q_ in range(qa, qb)], w=[pk])
                              c_ = 0
                              while c_ < n:
                                  dl = kt - (qa + c_)
                                  if dl == 8:
                                      m0, ln = (moff["g2"] + 5) * 128, 1
                                  elif dl == -8:
                                      m0, ln = moff["g2"] * 128, 1
                                  else:
                                      ln = min(n - c_, 4, dl + 8)
                                      m0 = (moff["g2"] + 1) * 128
                                  mm(pb[:, c_ * 128:(c_ + ln) * 128], identb, cs["masks"][:, m0:m0 + ln * 128], False, c_ + ln == n,
                                     r=["identb", "c_masks"], w=[pk])
                                  c_ += ln
                              pi = rot("PT", NPT)
                              S.op("act", I("activation", out=PT[pi][:, 0:n * 128], in_=pb[:, 0:n * 128], func=AF.Exp, scale=0.125),
                                   r=[pk], w=[("PT", pi)])
                              if sq == 0:
                                  precast_tick()
                              stage(lambda pa2=pa2, pa2k=pa2k, qa=qa, qb=qb, kt=kt, vh2=vh2, pi=pi, n=n, first2=first2, j=j: mm(
                                  pa2[0:65, (qa - j) * 128:(qb - j) * 128], vv[:, kt, vh2, :], PT[pi][:, 0:n * 128], first2, False,
                                  r=[("PT", pi), ("v", kt)], w=[pa2k]), 2)
                              first2 = False
                          stage(lambda pa2=pa2, pa2k=pa2k, s_=s_: S.op("act", I("activation", out=o2T[s_][0:65, :], in_=pa2[0:65, :], func=AF.Copy),
                                                                       r=[pa2k], w=o2k[s_]), 2)

                  def pair_blocks(gname, qc, kc_, vhs, accb, slots):
                      Dm = GD[gname]
                      lo, hi = max(-Dm, -j), min(Dm, NT - 1 - j)
                      dls = list(range(lo, hi + 1))
                      pa, pak = PS(accb)
                      for b0 in range(0, len(dls), 4):
                          batch = dls[b0:b0 + 4]
                          n = len(batch)
                          pbs = [PS(bank()), PS(bank())]
                          for i, dl in enumerate(batch):
                              for hf in range(2):
                                  rows = slice(hf * 64, (hf + 1) * 64)
                                  pb, pk = pbs[hf]
                                  mm(pb[:, i * 128:(i + 1) * 128], kT[rows, kc_, (j + dl) * 128:(j + dl + 1) * 128], qT[rows, qc, js],
                                     i == 0, False, r=[("kT", j + dl), ("qT", j)], w=[pk])
                          if gname != "g2":
                              m0 = (moff[gname] + batch[0] + Dm) * 128
                          elif batch[0] == -8:
                              m0 = moff[gname] * 128
                          elif batch[-1] == 8:
                              m0 = (moff[gname] + 6 - n) * 128
                          else:
                              m0 = (moff[gname] + 1) * 128
                          pis = []
                          for hf in range(2):
                              pb, pk = pbs[hf]
                              mm(pb[:, 0:n * 128], identb, cs["masks"][:, m0:m0 + n * 128], False, True, r=["identb", "c_masks"], w=[pk])
                          for hf in range(2):
                              pb, pk = pbs[hf]
                              pi = rot("PT", NPT)
                              pis.append(pi)
                              S.op("act", I("activation", out=PT[pi][:, 0:n * 128], in_=pb[:, 0:n * 128], func=AF.Exp, scale=0.125),
                                   r=[pk], w=[("PT", pi)])
                          if sq == 0:
                              precast_tick()

                          def pvs(pis=pis, batch=batch, pa=pa, pak=pak, j=j):
                              for hf in range(2):
                                  pi = pis[hf]
                                  for i, dl in enumerate(batch):
                                      st_ = (j, accb) not in started
                                      started.add((j, accb))
                                      mm(pa[:, slots[hf] * 128:slots[hf] * 128 + 65], PT[pi][:, i * 128:(i + 1) * 128], vv[:, j + dl, vhs[hf], :],
                                         st_, False, r=[("PT", pi), ("v", j + dl)], w=[pak])
                          stage(pvs, 1)

                  def normalise(bi, st=st, si=si, oi=oi, l=l):
                      pa, pak = PS(accs[bi])
                      pv = pa.rearrange("p (s c) -> p s c", c=128)
                      if bi < 2:
                          S.op("dve", I("tensor_tensor",
                              out=st[:, 20 + bi * 4:24 + bi * 4], in0=pv[:, :, 64], in1=esink[:, l * 8 + bi * 4:l * 8 + bi * 4 + 4], op=ALU.add),
                              r=[pak, "esink"], w=[("smd", si, bi)])
                      else:
                          S.op("dve", I("tensor_copy", out=st[:, 20 + bi * 4:24 + bi * 4], in_=pv[:, :, 64]),
                               r=[pak], w=[("smd", si, bi)])
                      S.op("dve", I("reciprocal", out=st[:, 20 + bi * 4:24 + bi * 4], in_=st[:, 20 + bi * 4:24 + bi * 4]),
                           r=[("smd", si, bi)], w=[("smr", si, bi)])
                      for s_ in range(4):
                          S.op("dve", I("tensor_scalar",
                              out=oab[oi][:, bi * 4 + s_, :], in0=pv[:, s_, 0:64], scalar1=st[:, 20 + bi * 4 + s_:21 + bi * 4 + s_],
                              scalar2=None, op0=ALU.mult), r=[pak, ("smr", si, bi)], w=[("oab", oi, bi)])

                  for s0 in (0, 2):
                      for g, gname in enumerate(["g0", "g1"]):
                          i_ = g * 4 + s0
                          pair_blocks(gname, 4 + i_ // 2, 1 + i_ // 2, (2 + i_, 3 + i_), accs[2], (s0, s0 + 1))
                      def g2acc(s0=s0, j=j):
                          for s_ in (s0, s0 + 1):
                              pa7, pa7k = PS(accs[2])
                              mm(pa7[:, s_ * 128:s_ * 128 + 65], o2T[s_][0:65, (j % 4) * 128:(j % 4 + 1) * 128], identb[0:65, 0:65],
                                 False, False, r=o2k[s_] + ["identb"], w=[pa7k])
                      stage(g2acc, 1)
                  stage(lambda: normalise(2), 1)
                  for c in range(4):
                      pair_blocks("A", c, 0, (0, 1), accs[c // 2], ((2 * c) % 4, (2 * c + 1) % 4))
                      if c == 1:
                          stage(lambda: normalise(0), 1)
                  stage(lambda: normalise(1), 1)
                  def p2_tail(oi=oi, js=js, j=j):
                      bt = bank()
                      pt_, ptk = PS(bt)
                      ptb = pt_.bitcast(BF)
                      of = oab[oi].rearrange("p h d -> p (h d)")
                      for i in range(6):
                          tr(ptb[:, i * 128:(i + 1) * 128], of[:, i * 128:(i + 1) * 128], identb,
                             r=[("oab", oi, 0), ("oab", oi, 1), ("oab", oi, 2), "identb"], w=[ptk])
                      S.op("act", I("activation",
                          out=qT[:, 0:6, js], in_=ptb[:, 0:768].rearrange("p (c t) -> p c t", t=128), func=AF.Copy), r=[ptk], w=[("qT", j)])
                  stage(p2_tail, 1)
              while lagq:
                  lagq.pop(0)()
              if sq == 0:
                  while pc["n"] < len(pc["jobs"]):
                      precast_step()

              stop_at('p2a')
              fence(RKEYS_A + TA, RKEYS_B + TB)
              set_pool(range(8))
              f0 = False

              def ld_wg(cb):
                  load_block(l, [("wg", cb, 0), ("wg", cb, 1)], wg[:, :, cb * 512:(cb + 1) * 512], ("wg", cb), f0)

              def ld_wa(cb):
                  load_block(l, [("wa", cb)], wa[:, :, cb * 512:(cb + 1) * 512], ("wa", cb), f0)

              def ld_wo(cb):
                  load_block(l, [("wo", cb, 0), ("wo", cb, 1)], wo[:, :, cb * 512:(cb + 1) * 512], ("wo", cb), f0)
              ld_wg(0)
              ld_wg(2)
              ld_wa(0)
              load_block(l, [("wb",)], wb, "wb", f0)
              ld_wg(1)
              ld_wg(3)
              ld_wa(1)
              ld_wo(0)
              ld_wo(1)
              pend = []

              def flush():
                  while pend:
                      pend.pop(0)()
              for tc in range(4):
                  cs_ = slice(tc * 512, (tc + 1) * 512)
                  tiles = list(range(tc * 4, tc * 4 + 4))
                  xk = [("xh", t) for t in tiles]
                  qk_ = [("qT", t) for t in tiles]
                  for dc in range(8):
                      if dc == 2:
                          flush()
                      tg_ = []
                      for gi in range(2):
                          b_ = bank()
                          pb, pk = PS(b_)
                          col = gi * 1024 + dc * 128
                          for k in range(8):
                              mm(pb, wg[:, k, col:col + 128], xh[:, k, cs_], k == 0, k == 7, r=[(("wg", col // 512), k)] + xk, w=[pk])
                          fi = rot("f", 3)
                          S.op("act", I("activation",
                              out=f512[fi], in_=pb, func=AF.Tanh, scale=0.5, bias=bgh[:, l, gi * 8 + dc:gi * 8 + dc + 1]),
                              r=[pk, "bgh"], w=[("f", fi)])
                          tg_.append(fi)
                      ba = bank()
                      pa, pak = PS(ba)
                      for k in range(4):
                          mm(pa, wa[:, k, dc * 128:(dc + 1) * 128], qT[:, k, cs_], k == 0, k == 3, r=[(("wa", dc // 4), k)] + qk_, w=[pak])
                      bb_ = bank()
                      pb2, pbk = PS(bb_)
                      for k in range(2):
                          mm(pb2, wb[:, k, dc * 128:(dc + 1) * 128], qT[:, 4 + k, cs_], k == 0, k == 1, r=[("wb", k)] + qk_, w=[pbk])
                      fa, fb_ = tg_
                      S.op("dve", I("scalar_tensor_tensor",
                          out=f512[fa], in0=f512[fa], scalar=1.0, in1=pa, op0=ALU.add, op1=ALU.mult), r=[("f", fa), pak], w=[("f", fa)])
                      S.op("dve", I("scalar_tensor_tensor",
                          out=f512[fb_], in0=f512[fb_], scalar=1.0, in1=pb2, op0=ALU.add, op1=ALU.mult), r=[("f", fb_), pbk], w=[("f", fb_)])
                      S.op("dve", I("tensor_tensor", out=f512[fa], in0=f512[fa], in1=f512[fb_], op=ALU.add),
                           r=[("f", fa), ("f", fb_)], w=[("f", fa)])
                      S.op("act", I("activation", out=mg[:, dc, :], in_=f512[fa], func=AF.Copy, scale=C_HALF), r=[("f", fa)], w=["mg"])
                  for ti, t in enumerate(tiles):
                      ts = slice(t * 128, (t + 1) * 128)
                      bks = [bank(), bank()]
                      for cb in range(2):
                          pb, pk = PS(bks[cb])
                          for k in range(8):
                              mm(pb, mg[:, k, ti * 128:(ti + 1) * 128], wo[:, k, cb * 512:(cb + 1) * 512], k == 0, False,
                                 r=["mg", (("wo", cb), k)], w=[pk])
                          for c in range(4):
                              mm(pb[:, c * 128:(c + 1) * 128], xh[:, cb * 4 + c, ts], identb, False, False, r=[("xh", t), "identb"], w=[pk])
                              mm(pb[:, c * 128:(c + 1) * 128], xl[:, cb * 4 + c, ts], identb, False, c == 3, r=[("xl", t), "identb"], w=[pk])
                      rstd, nmr, smk = layer_norm_stats([psb[bks[0]], psb[bks[1]]], [("ps", bks[0]), ("ps", bks[1])])
                      dbg_dump("sml", t, sm[(state["sm"] + 1) % 2], smk, 32)
                      zi = rot("zb", 3)
                      for cb in range(2):
                          S.op("act", I("activation",
                              out=ztb[zi][:, cb * 512:(cb + 1) * 512], in_=psb[bks[cb]], func=AF.Identity, scale=rstd, bias=nmr),
                              r=[("ps", bks[cb])] + smk, w=[("zt", zi)])
                      pend.append(lambda t=t, zi=zi, l=l: to_hilo(t, ztb[zi], ("zt", zi), ("g1", "b1", l)))
                      while len(pend) > 2:
                          pend.pop(0)()
              flush()

              stop_at('p2b')
              fence(RKEYS_B + [("qT", t) for t in range(NT)], RKEYS_C)
              set_pool(range(4))
              for cb in range(2):
                  load_block(l, [("wpg", cb, 0), ("wpg", cb, 1)], wpg[:, :, cb * 512:(cb + 1) * 512], ("wpg", cb), False)
              load_block(l, [("wp",)], wpl, "wpl", False)
              jobs = []
              for tc_ in range(4):
                  for fb in range(8):
                      jobs.append(([("wu", fb, 0), ("wu", fb, 1)], False))
                  for cb in range(2):
                      for fgp in range(4):
                          jobs.append(([("wd", cb, fgp * 2), ("wd", cb, fgp * 2 + 1)], False))
              jst = {"issued": 0, "used": 0}

              def next_block():
                  n = jst["used"]
                  while jst["issued"] < min(n + 3, len(jobs)):
                      m = jst["issued"]
                      load_block(l, jobs[m][0], wbuf[m % 3], ("wbuf", m % 3), jobs[m][1])
                      jst["issued"] += 1
                  jst["used"] += 1
                  return n % 3
              S.op("sp", I("dma_start", out=bpgbv, in_=cd["bpgb"][l]), w=["bpgb"], dma="bpgb")
              if last and sq == 0:
                  pass
              if last:
                  S.op("sp", I("dma_start", out=lnfg, in_=cd["lnfg"]), w=["lnfg"], dma="lnfg")
                  S.op("sp", I("dma_start", out=lnfb, in_=cd["lnfb"]), w=["lnfb"], dma="lnfb")
              for tc in range(4):
                  cs_ = slice(tc * 512, (tc + 1) * 512)
                  tiles = list(range(tc * 4, tc * 4 + 4))
                  xk = [("xh", t) for t in tiles]
                  pTi = {}
                  for fb in range(8):
                      if fb == 4:
                          flush()
                      wi = next_block()
                      for fc in range(4):
                          b_ = bank()
                          pb, pk = PS(b_)
                          for k in range(8):
                              mm(pb, wbuf[wi][:, k, fc * 128:(fc + 1) * 128], xh[:, k, cs_], k == 0, k == 7, r=[(("wbuf", wi), k)] + xk, w=[pk])
                          fi = rot("f", 3)
                          S.op("act", I("activation", out=f512[fi], in_=pb, func=AF.Relu, scale=RELU_S),
                               r=[pk], w=[("f", fi)])
                          S.op("pool", I("tensor_tensor", out=uT[:, fb * 4 + fc, :], in0=f512[fi], in1=f512[fi],
                                                                                    op=ALU.mult), r=[("f", fi)], w=["uT"])
                  for cb in range(2):
                      ccs = slice(cb * 512, (cb + 1) * 512)
                      accb = [4, 5, 6, 7]
                      for fgp in range(4):
                          wi = next_block()
                          for f in range(8):
                              for ti in range(4):
                                  pb, pk = PS(accb[ti])
                                  mm(pb, uT[:, fgp * 8 + f, ti * 128:(ti + 1) * 128], wbuf[wi][:, f, :], fgp == 0 and f == 0, False,
                                     r=["uT", (("wbuf", wi), f)], w=[pk])
                      for ti, t in enumerate(tiles):
                          ts = slice(t * 128, (t + 1) * 128)
                          pb, pk = PS(accb[ti])
                          for c in range(4):
                              mm(pb[:, c * 128:(c + 1) * 128], xh[:, cb * 4 + c, ts], identb, False, False, r=[("xh", t), "identb"], w=[pk])
                              mm(pb[:, c * 128:(c + 1) * 128], xl[:, cb * 4 + c, ts], identb, False, c == 3, r=[("xl", t), "identb"], w=[pk])
                          bg_ = bank()
                          pg, pgk = PS(bg_)
                          for k in range(8):
                              mm(pg, xh[:, k, ts], wpg[:, k, ccs], k == 0, k == 7, r=[("xh", t), (("wpg", cb), k)], w=[pgk])
                          fi = rot("f", 3)
                          S.op("dve", I("tensor_tensor", out=f512[fi], in0=pg, in1=bpgbv[:, ccs], op=ALU.add),
                               r=[pgk, "bpgb"], w=[("f", fi)])
                          S.op("act", I("activation", out=f512[fi], in_=f512[fi], func=AF.Tanh, scale=0.5), r=[("f", fi)], w=[("f", fi)])
                          if cb == 0:
                              pi_ = rot("pin", 2)
                              S.op("sp", I("dma_start", out=pin[pi_], in_=p_d[l, sq, t * 128:(t + 1) * 128, :]),
                                   w=[("pin", pi_)], dma=("pin", pi_))
                              bt = bank()
                              pt_, ptk = PS(bt)
                              for k2 in range(2):
                                  tr(pt_[:, k2 * 128:(k2 + 1) * 128], pin[pi_][:, k2 * 128:(k2 + 1) * 128], cs["identf"],
                                     r=[("pin", pi_), "c_identf"], w=[ptk])
                              S.op("act", I("activation",
                                  out=pT[ti], in_=pt_[:, 0:256].rearrange("p (k t) -> p k t", t=128), func=AF.Copy), r=[ptk], w=[("pT", ti)])
                          pi_ = ti
                          bw = bank()
                          pw_, pwk = PS(bw)
                          for k2 in range(2):
                              mm(pw_, pT[pi_][:, k2, :], wpl[:, k2, ccs], k2 == 0, k2 == 1, r=[("pT", pi_), ("wpl", k2)], w=[pwk])
                          S.op("dve", I("scalar_tensor_tensor",
                              out=f512[fi], in0=f512[fi], scalar=1.0, in1=pw_, op0=ALU.add, op1=ALU.mult), r=[("f", fi), pwk], w=[("f", fi)])
                          S.op("dve", I("scalar_tensor_tensor",
                              out=yb[:, ti, ccs], in0=f512[fi], scalar=C_HALF, in1=pb, op0=ALU.mult, op1=ALU.add),
                              r=[("f", fi), pk], w=[("yb", ti, cb)])
                  for ti, t in enumerate(tiles):
                      yk = [("yb", ti, 0), ("yb", ti, 1)]
                      rstd, nmr, smk = layer_norm_stats([yb[:, ti, 0:512], yb[:, ti, 512:1024]], yk)
                      S.op("act", I("activation",
                          out=yb[:, ti, :], in_=yb[:, ti, :], func=AF.Identity, scale=rstd, bias=nmr), r=yk + smk, w=yk)

                  def tail(tiles=tiles, l=l, sq=sq, last=last):
                      for ti, t in enumerate(tiles):
                          yk = [("yb", ti, 0), ("yb", ti, 1)]
                          if not last:
                              to_hilo(t, yb[:, ti, :], yk, ("g2", "b2", l))
                          else:
                              S.op("dve", I("tensor_tensor", out=yb[:, ti, :], in0=yb[:, ti, :], in1=lnfg, op=ALU.mult),
                                   r=yk + ["lnfg"], w=yk)
                              S.op("dve", I("tensor_tensor", out=yb[:, ti, :], in0=yb[:, ti, :], in1=lnfb, op=ALU.add),
                                   r=yk + ["lnfb"], w=yk)
                              S.op("sp", I("dma_start", out=out_d[sq, t * 128:(t + 1) * 128, :], in_=yb[:, ti, :]),
                                   r=yk, w=[("out", ti)], dma=("out", ti))
                  pend.append(tail)
              flush()

    except _Stop:
        pass
    S.emit(final_slots=[s for s in S.slots if isinstance(s, tuple) and s[0] == "out"])
    return nc


_CACHE = {}


def kernel(**inputs):
    inp = {k: np.asarray(v) for k, v in inputs.items()}
    wall = pack_weights(inp)
    consts = make_consts(inp)
    if "nc" not in _CACHE:
        _CACHE["nc"] = build()
    nc = _CACHE["nc"]
    in_maps = []
    for c in range(8):
        m = {"x": np.ascontiguousarray(inp["x"][2 * c:2 * c + 2]),
             "p": np.ascontiguousarray(inp["p"][:, 2 * c:2 * c + 2]),
             "w": wall}
        for k, v in consts.items():
            m["c_" + k] = v
        in_maps.append(m)
    res = run_bass_kernel_spmd(nc, in_maps, core_ids=list(range(8)))
    return np.concatenate([r["out"] for r in res.results], axis=0).astype(np.float32)
```

```python
import contextlib
import numpy as np
import ml_dtypes
import concourse.bass as bass
import concourse.mybir as mybir
from concourse.bass_utils import run_bass_kernel_spmd

F32 = mybir.dt.float32
BF = mybir.dt.bfloat16
AF = mybir.ActivationFunctionType
ALU = mybir.AluOpType

ENGS = ("pe", "act", "dve", "pool", "sp")
S_LEN = 2048
NT = 16
D = 1024
ALPHA = 8.0 ** 0.25
EPS_P = 1e-5 / (ALPHA * ALPHA)
C_HALF = 0.5 / ALPHA
RELU_S = ALPHA ** -0.5
A_PERM = [0, 4, 1, 5, 2, 6, 3, 7]
GROUPS = [("A", 128, 1), ("g0", 64, 1), ("g1", 256, 4), ("g2", 1024, 16)]
GD = {"A": 1, "g0": 1, "g1": 2, "g2": 8}


class Sched:
    def __init__(self, nc):
        self.nc = nc
        self.ops = {e: [] for e in ENGS}
        self.lastw = {}
        self.readers = {}
        self.slot_cnt = {}
        self.slots = []

    def op(self, eng, fn, r=(), w=(), dma=None):
        idx = len(self.ops[eng])
        w = list(w) + [("psr",) + tuple(k[1:]) for k in r if isinstance(k, tuple) and k[0] == "ps"]
        deps = set()
        for k in r:
            t = self.lastw.get(k)
            if t is not None:
                deps.add((t, "raw"))
        for k in w:
            t = self.lastw.get(k)
            if t is not None:
                deps.add((t, "waw"))
            for t in self.readers.get(k, ()):
                deps.add((t, "war"))
        if dma is not None:
            if dma not in self.slot_cnt:
                self.slot_cnt[dma] = 0
                self.slots.append(dma)
            self.slot_cnt[dma] += 1
            tok = ("d", dma, self.slot_cnt[dma])
        else:
            tok = ("c", eng, idx)
        for k in w:
            self.lastw[k] = tok
            self.readers[k] = []
        for k in r:
            self.readers.setdefault(k, []).append(tok)
        self.ops[eng].append(dict(fn=fn, deps=deps, dma=dma, sig=False))
        return tok

    def emit(self, final_slots=()):
        nc = self.nc
        for e in ENGS:
            for i, o in enumerate(self.ops[e]):
                lst = {}
                for (t, kind) in o["deps"]:
                    if t[0] == "c":
                        _, pe_, pi = t
                        if pe_ == e and o["dma"] is None:
                            if pe_ == "pe":
                                continue
                            if kind != "raw" or pi < i - 2:
                                continue
                        k = ("c", pe_)
                        lst[k] = max(lst.get(k, -1), pi)
                    else:
                        _, slot, cnt = t
                        k = ("d", slot)
                        lst[k] = max(lst.get(k, -1), cnt)
                o["need"] = lst
                for k, v in lst.items():
                    if k[0] == "c":
                        self.ops[k[1]][v]["sig"] = True
        cum = {}
        for e in ENGS:
            c = 0
            arr = []
            for o in self.ops[e]:
                if o["sig"] and o["dma"] is None:
                    c += 1
                arr.append(c)
            cum[e] = arr
        with contextlib.ExitStack() as st:
            esem = {e: st.enter_context(nc.semaphore("s_" + e)) for e in ENGS}
            dsem = {s: st.enter_context(nc.semaphore("d_%d" % i)) for i, s in enumerate(self.slots)}
            block = st.enter_context(nc.Block())

            def run(e, eng):
                waited = {}
                for o in self.ops[e]:
                    for k, v in o["need"].items():
                        if k[0] == "c":
                            val = cum[k[1]][v]
                            sem = esem[k[1]]
                        else:
                            val = 16 * v
                            sem = dsem[k[1]]
                        if waited.get(k, 0) < val:
                            eng.wait_ge(sem, val)
                            waited[k] = val
                    ins = o["fn"](eng)
                    if o["dma"] is not None:
                        ins.then_inc(dsem[o["dma"]], 16)
                    elif o["sig"]:
                        ins.then_inc(esem[e], 1)
                if e == "sp":
                    for s in final_slots:
                        eng.wait_ge(dsem[s], 16 * self.slot_cnt[s])

            @block.tensor
            def _(eng):
                run("pe", eng)

            @block.scalar
            def _(eng):
                run("act", eng)

            @block.vector
            def _(eng):
                run("dve", eng)

            @block.gpsimd
            def _(eng):
                run("pool", eng)

            @block.sync
            def _(eng):
                run("sp", eng)


def _qkv_cols():
    def head(base, h):
        return list(range(base + h * 64, base + (h + 1) * 64))
    chunks = []
    for (h0, h1) in [(0, 4), (1, 5), (2, 6), (3, 7)]:
        chunks.append(head(0, h0) + head(0, h1))
    for i in range(6):
        chunks.append(head(768, 2 * i) + head(768, 2 * i + 1))
    chunks.append(head(512, 0) + head(512, 1))
    for i in range(6):
        chunks.append(head(1536, 2 * i) + head(1536, 2 * i + 1))
    blocks = []
    for (a, b) in [(0, 4), (4, 8), (8, 11), (11, 15), (15, 17)]:
        blocks.append(sum(chunks[a:b], []))
    vheads = [head(640, 0), head(640, 1)] + [head(2304, i) for i in range(12)]
    blocks.append(sum(vheads[0:8], []))
    blocks.append(sum(vheads[8:14], []))
    return blocks


QKV_BLOCKS = _qkv_cols()
QK_CHUNK_RANGES = [(0, 4), (4, 8), (8, 11), (11, 15), (15, 17)]


def piece_table():
    tab = {}
    off = 0

    def add(name, kc, ncols):
        nonlocal off
        tab[name] = (off, kc, ncols)
        off += kc * ncols
    for b in range(7):
        for kh in range(2):
            add(("qkv", b, kh), 4, len(QKV_BLOCKS[b]))
    for cb in range(4):
        for kh in range(2):
            add(("wg", cb, kh), 4, 512)
    for cb in range(2):
        add(("wa", cb), 4, 512)
    add(("wb",), 2, 1024)
    for cb in range(2):
        for kh in range(2):
            add(("wo", cb, kh), 4, 512)
    for fb in range(8):
        for kh in range(2):
            add(("wu", fb, kh), 4, 512)
    for cb in range(2):
        for fg in range(8):
            add(("wd", cb, fg), 4, 512)
    for cb in range(2):
        for kh in range(2):
            add(("wpg", cb, kh), 4, 512)
    add(("wp",), 2, 1024)
    return tab, off


PIECES, WTOT = piece_table()


def _pm(W):
    K, C = W.shape
    return np.ascontiguousarray(W.reshape(K // 128, 128, C).transpose(1, 0, 2))


def pack_weights(inp):
    out = np.empty((4, 128, WTOT), np.float32)
    for l in range(4):
        def put(name, arr):
            off, kc, nc_ = PIECES[name]
            out[l, :, off:off + kc * nc_] = arr.reshape(128, kc * nc_)
        w_in = inp["w_in"][l]
        for b in range(7):
            wp = _pm(w_in[:, QKV_BLOCKS[b]])
            for kh in range(2):
                put(("qkv", b, kh), wp[:, kh * 4:(kh + 1) * 4, :])
        wg = _pm(w_in[:, 3072:5120])
        for cb in range(4):
            for kh in range(2):
                put(("wg", cb, kh), wg[:, kh * 4:(kh + 1) * 4, cb * 512:(cb + 1) * 512])
        rows = sum([list(range(h * 64, (h + 1) * 64)) for h in A_PERM], [])
        wa = _pm(inp["w_branch_a"][l][rows, :])
        for cb in range(2):
            put(("wa", cb), wa[:, :, cb * 512:(cb + 1) * 512])
        put(("wb",), _pm(inp["w_branch_b"][l]))
        wo = _pm(inp["w_out"][l])
        for cb in range(2):
            for kh in range(2):
                put(("wo", cb, kh), wo[:, kh * 4:(kh + 1) * 4, cb * 512:(cb + 1) * 512])
        wu = _pm(inp["w_up"][l])
        for fb in range(8):
            for kh in range(2):
                put(("wu", fb, kh), wu[:, kh * 4:(kh + 1) * 4, fb * 512:(fb + 1) * 512])
        wd = _pm(inp["w_down"][l])
        for cb in range(2):
            for fg in range(8):
                put(("wd", cb, fg), wd[:, fg * 4:(fg + 1) * 4, cb * 512:(cb + 1) * 512])
        wpg = _pm(inp["w_ple_gate"][l])
        for cb in range(2):
            for kh in range(2):
                put(("wpg", cb, kh), wpg[:, kh * 4:(kh + 1) * 4, cb * 512:(cb + 1) * 512])
        put(("wp",), _pm(inp["w_ple"][l]))
    return out


def make_consts(inp, lastl=3):
    c = {}
    c["identf"] = np.eye(128, dtype=np.float32)
    kl = np.arange(128)[:, None]
    ql = np.arange(128)[None, :]
    cols = []
    for (name, W, r) in GROUPS:
        Dm = GD[name]
        dls = range(-Dm, Dm + 1) if name != "g2" else [8, 0, 0, 0, 0, -8]
        for dl in dls:
            diff = 128 * dl + kl - ql
            cols.append((((np.abs(diff) <= W) & ((kl - ql) % r == 0)).astype(np.float32) - 1.0) * 30000.0)
    c["masks"] = np.concatenate(cols, axis=1).astype(ml_dtypes.bfloat16)
    pos = np.arange(S_LEN, dtype=np.float32)
    inv = (np.float32(500000.0) ** (-np.arange(0, 16, 2, dtype=np.float32) / np.float32(16))).astype(np.float32)
    ang = (pos[:, None] * inv[None, :]).astype(np.float32)
    c["cos"] = np.ascontiguousarray(np.broadcast_to(np.cos(ang).astype(np.float32).reshape(16, 128, 1, 8).transpose(1, 0, 2, 3), (128, 16, 4, 8)))
    c["sin"] = np.ascontiguousarray(np.broadcast_to(np.sin(ang).astype(np.float32).reshape(16, 128, 1, 8).transpose(1, 0, 2, 3), (128, 16, 4, 8)))
    def fm(v, nch):
        return np.ascontiguousarray(v.reshape(4, nch, 128).transpose(2, 0, 1))
    c["bg"] = fm(inp["b_gate"], 16)
    c["g1"] = fm(inp["ln1_g"], 8)
    c["b1"] = fm(inp["ln1_b"], 8)
    c["g2"] = fm(inp["ln2_g"], 8)
    c["b2"] = fm(inp["ln2_b"], 8)
    c["bpgb"] = np.ascontiguousarray(np.broadcast_to(inp["b_ple_gate"][:, None, :], (4, 128, 1024)))
    c["sink"] = np.ascontiguousarray(np.broadcast_to(inp["a_sink"][:, A_PERM].reshape(1, 32), (128, 32)))
    c["lnfg"] = np.ascontiguousarray(np.broadcast_to(inp["ln2_g"][lastl][None, :], (128, 1024)))
    c["lnfb"] = np.ascontiguousarray(np.broadcast_to(inp["ln2_b"][lastl][None, :], (128, 1024)))
    return c


CONST_SHAPES = {
    "identf": ([128, 128], F32), "masks": ([128, 17 * 128], BF), "cos": ([128, 16, 4, 8], F32),
    "sin": ([128, 16, 4, 8], F32), "bg": ([128, 4, 16], F32), "g1": ([128, 4, 8], F32),
    "b1": ([128, 4, 8], F32), "g2": ([128, 4, 8], F32), "b2": ([128, 4, 8], F32),
    "bpgb": ([4, 128, 1024], F32), "sink": ([128, 32], F32), "lnfg": ([128, 1024], F32),
    "lnfb": ([128, 1024], F32),
}


def build(nlayers=4, nseq=2):
    import os as _os
    nc = bass.Bass("TRN2", target_bir_lowering=False)
    S = Sched(nc)
    x_d = nc.dram_tensor("x", [nseq, S_LEN, D], F32, kind="ExternalInput").ap()
    p_d = nc.dram_tensor("p", [4, nseq, S_LEN, 256], F32, kind="ExternalInput").ap()
    w_d = nc.dram_tensor("w", [4, 128, WTOT], F32, kind="ExternalInput").ap()
    out_d = nc.dram_tensor("out", [nseq, S_LEN, D], F32, kind="ExternalOutput").ap()
    cd = {k: nc.dram_tensor("c_" + k, sh, dt, kind="ExternalInput").ap() for k, (sh, dt) in CONST_SHAPES.items()}
    KDBG = _os.environ.get("KDBG", "")
    dbg_d = nc.dram_tensor("dbg", [S_LEN, D], F32, kind="ExternalOutput").ap() if KDBG else None

    def dbg_dump(tag, t, ap, key, ncols=1024):
        if KDBG == tag:
            keys = key if isinstance(key, list) else [key]
            S.op("sp", I("dma_start", out=dbg_d[t * 128:(t + 1) * 128, 0:ncols], in_=ap), r=keys, w=[("dbg", t)], dma=("dbg", t % 4))

    def sb(name, shape, dt):
        return nc.alloc_sbuf_tensor(name, shape, dt).ap()

    xh = sb("xh", [128, 8, S_LEN], BF)
    xl = sb("xl", [128, 8, S_LEN], BF)
    RSZ = 49408
    R = sb("R", [128, RSZ], BF)
    wbuf = [sb("wbuf0", [128, 8, 512], BF),
            R[:, 40960:45056].rearrange("p (k c) -> p k c", c=512), R[:, 45056:49152].rearrange("p (k c) -> p k c", c=512)]
    NSTG = 3
    stg = [sb("stg%d" % i, [128, 512], F32) for i in range(NSTG)]
    wscr = nc.dram_tensor("wscr", [4, 128, WTOT], BF, kind="Internal").ap()
    T = sb("T", [128, 9216], BF)
    cs = {k: sb("k_" + k, sh, dt) for k, (sh, dt) in CONST_SHAPES.items() if k not in ("lnfg", "lnfb", "bpgb")}
    identb = sb("identb", [128, 128], BF)
    bgh = sb("bgh", [128, 4, 16], F32)
    esink = sb("esink", [128, 32], F32)
    mhalf = sb("mhalf", [128, 1], F32)
    onesc = sb("onesc", [128, 2], BF)
    tq = [T[:, i * 512:(i + 1) * 512] for i in range(2)] + [T[:, 7168:7680]]
    rp = [T[:, o_:o_ + 512].bitcast(F32).rearrange("p (a h d) -> p a h d", a=4, h=8) for o_ in (1024, 1536, 7680)]
    NPT = 6
    PT = [T[:, 2048 + i * 512:2048 + (i + 1) * 512] for i in range(3)] + [T[:, 5120 + i * 512:5120 + (i + 1) * 512] for i in range(3)]
    oab = [T[:, 3584 + i * 768:3584 + (i + 1) * 768].rearrange("p (h d) -> p h d", d=64) for i in range(2)]
    f512 = [T[:, i * 1024:(i + 1) * 1024].bitcast(F32) for i in range(3)]
    xf = [T[:, 3072:5120].bitcast(F32).rearrange("p (c t) -> p c t", t=128)]
    zt = [T[:, 5120 + i * 2048:5120 + (i + 1) * 2048].bitcast(F32) for i in range(2)]
    tfr = [T[:, o_:o_ + 256].bitcast(F32).rearrange("p (h d) -> p h d", d=16) for o_ in (6656, 6912, 8192)]
    TA = [("pcb",)] + [("tfr", i) for i in range(3)] + [("tq", i) for i in range(3)] + [("rp", i, j) for i in range(3) for j in range(4)] + [("PT", i) for i in range(6)] + [("oab", i, b) for i in range(2) for b in range(3)]
    TB = [("f", i) for i in range(3)] + [("xf", 0, h) for h in range(2)] + [("zt", i) for i in range(2)]
    sm = [sb("sm%d" % i, [128, 32], F32) for i in range(4)]
    pin = [sb("pin%d" % i, [128, 256], F32) for i in range(2)]
    pT = [sb("pT%d" % i, [128, 2, 128], BF) for i in range(4)]
    psb = [nc.alloc_psum_tensor("ps%d" % i, [128, 512], F32).ap() for i in range(8)]

    qT = R[:, 0:10 * 2048].rearrange("p (c t) -> p c t", t=2048)
    kT = R[:, 20480:20480 + 7 * 2048].rearrange("p (c t) -> p c t", t=2048)
    vv = R[:, 34816:34816 + 16 * 14 * 65].rearrange("p (t h d) -> p t h d", h=14, d=65)
    vflat = R[:, 34816:34816 + 16 * 14 * 65].rearrange("p (n d) -> p n d", d=65)
    o2 = 12288
    wg = R[:, o2:o2 + 8 * 2048].rearrange("p (k c) -> p k c", c=2048)
    wa = R[:, o2 + 16384:o2 + 16384 + 4 * 1024].rearrange("p (k c) -> p k c", c=1024)
    wb = R[:, o2 + 20480:o2 + 20480 + 2 * 1024].rearrange("p (k c) -> p k c", c=1024)
    wo = R[:, o2 + 22528:o2 + 22528 + 8 * 1024].rearrange("p (k c) -> p k c", c=1024)
    mg = R[:, o2 + 30720:o2 + 30720 + 8 * 512].rearrange("p (k c) -> p k c", c=512)
    uT = R[:, 0:32 * 512].rearrange("p (f t) -> p f t", t=512)
    yb = R[:, 16384:16384 + 8192].bitcast(F32).rearrange("p (t c) -> p t c", c=1024)
    wpg = R[:, 24576:24576 + 8 * 1024].rearrange("p (k c) -> p k c", c=1024)
    wpl = R[:, 32768:32768 + 2 * 1024].rearrange("p (k c) -> p k c", c=1024)
    lnfg = R[:, 34816:34816 + 2048].bitcast(F32)
    lnfb = R[:, 36864:36864 + 2048].bitcast(F32)
    bpgbv = R[:, 38912:38912 + 2048].bitcast(F32)
    ztb = [zt[0], zt[1], R[:, 47104:49152].bitcast(F32)]
    RKEYS_A = [("qT", t) for t in range(NT)] + [("kT", t) for t in range(NT)] + [("v", t) for t in range(NT)]
    RKEYS_B = ([(("wg", i), k) for i in range(4) for k in range(8)] + [(("wa", i), k) for i in range(2) for k in range(4)]
               + [(("wo", i), k) for i in range(2) for k in range(8)] + [("wb", k) for k in range(2)] + ["mg", ("zt", 2)])
    RKEYS_C = ["uT", "lnfg", "lnfb", "bpgb"] + [(("wpg", i), k) for i in range(2) for k in range(8)] + [("wpl", k) for k in range(2)] + [(("wbuf", i), k) for i in (1, 2) for k in range(8)] + [("yb", i, j) for i in range(4) for j in range(2)]

    state = {"zb": 0, "ce": 0, "bank": 0, "stg": 0, "wb": 0, "f": 0, "tq": 0, "PT": 0, "z": 0, "xf": 0, "sm": 0, "pin": 0, "oab": 0}

    def rot(name, n):
        i = state[name]
        state[name] = (i + 1) % n
        return i

    pool_ = {"l": list(range(8)), "i": 0}

    def set_pool(lst):
        pool_["l"] = list(lst)
        pool_["i"] = 0

    def bank():
        b = pool_["l"][pool_["i"] % len(pool_["l"])]
        pool_["i"] += 1
        return b

    def I(meth, *a, **kw):
        return lambda e: getattr(e, meth)(*a, **kw)

    def PS(i):
        return psb[i], ("ps", i)

    def mm(out, lhsT, rhs, start, stop, r, w):
        S.op("pe", I("matmul", out, lhsT=lhsT, rhs=rhs, start=start, stop=stop, skip_group_check=True), r=r, w=w)

    def tr(out, in_, ident, r, w):
        S.op("pe", I("transpose", out=out, in_=in_, identity=ident), r=r, w=w)

    def fence(old, new):
        if _os.environ.get("KOFF", "").find("fence") < 0:
            S.op("pool", I("nop", ), w=list(old) + list(new))

    def load_piece(l, name, dst, dkey):
        off, kc, ncol = PIECES[name]
        for k in range(kc):
            for c0 in range(0, ncol, 512):
                n = min(512, ncol - c0)
                si = rot("stg", 2)
                o = off + k * ncol + c0
                S.op("sp", I("dma_start", out=stg[si][:, 0:n], in_=w_d[l, :, o:o + n]),
                     w=[("stg", si)], dma=("stg", si))
                S.op("pool", I("tensor_copy", out=dst[:, k, c0:c0 + n], in_=stg[si][:, 0:n]),
                     r=[("stg", si)], w=[dkey])

    def load_block(l, pieces, dst, dkey, first):
        off0, _, ncol = PIECES[pieces[0]]
        K = sum(PIECES[p_][1] for p_ in pieces)
        skeys = [("scr", l, off0 + k * ncol + c0) for k in range(K) for c0 in range(0, ncol, 512)]
        sview = wscr[l, :, off0:off0 + K * ncol].rearrange("p (k c) -> p k c", c=ncol)
        if not first:
            S.op("sp", I("dma_start", out=dst, in_=sview), r=skeys, w=[(dkey, k) for k in range(K)], dma=("ld", dkey))
            return
        for k in range(K):
            for c0 in range(0, ncol, 512):
                n = min(512, ncol - c0)
                si = rot("stg", NSTG)
                o = off0 + k * ncol + c0
                S.op("sp", I("dma_start", out=stg[si][:, 0:n], in_=w_d[l, :, o:o + n]), w=[("stg", si)], dma=("stg", si))
                if rot("ce", 2) == 0:
                    S.op("dve", I("tensor_copy", out=dst[:, k, c0:c0 + n], in_=stg[si][:, 0:n]), r=[("stg", si)], w=[(dkey, k)])
                else:
                    S.op("act", I("activation", out=dst[:, k, c0:c0 + n], in_=stg[si][:, 0:n], func=AF.Copy), r=[("stg", si)], w=[(dkey, k)])
        S.op("sp", I("dma_start", out=sview, in_=dst), r=[(dkey, k) for k in range(K)], w=skeys, dma=("scrw", dkey))

    pc_out = [T[:, 8448:8960], T[:, 7168:7680]]
    pc_okey = [("pcb",), ("tq", 2)]
    pc = {"jobs": [], "n": 0, "in": 0}

    def precast_begin(l):
        names = []
        for cb in range(4):
            names += [("wg", cb, 0), ("wg", cb, 1)]
        names += [("wa", 0), ("wa", 1), ("wb",)]
        for cb in range(2):
            names += [("wo", cb, 0), ("wo", cb, 1)]
        for cb in range(2):
            names += [("wpg", cb, 0), ("wpg", cb, 1)]
        names += [("wp",)]
        for fb in range(8):
            names += [("wu", fb, 0), ("wu", fb, 1)]
        for cb in range(2):
            for fg in range(8):
                names += [("wd", cb, fg)]
        jobs = []
        for nm in names:
            off, kc, ncol = PIECES[nm]
            for k in range(kc):
                for c0 in range(0, ncol, 512):
                    jobs.append((l, off + k * ncol + c0, min(512, ncol - c0)))
        pc["jobs"], pc["n"], pc["in"] = jobs, 0, 0

    def precast_issue_in():
        m = pc["in"]
        if m < len(pc["jobs"]):
            l_, o, n_ = pc["jobs"][m]
            si = m % NSTG
            S.op("sp", I("dma_start", out=stg[si][:, 0:n_], in_=w_d[l_, :, o:o + n_]), w=[("stg", si)], dma=("stg", si))
            pc["in"] += 1

    def precast_tick():
        pc["tick"] = pc.get("tick", 0) + 1
        if pc["tick"] % 2 == 0:
            precast_step()

    def precast_step():
        n = pc["n"]
        if n >= len(pc["jobs"]):
            return
        while pc["in"] < min(n + NSTG, len(pc["jobs"])):
            precast_issue_in()
        l_, o, n_ = pc["jobs"][n]
        si, oi_ = n % NSTG, n % 2
        S.op("dve", I("tensor_copy", out=pc_out[oi_][:, 0:n_], in_=stg[si][:, 0:n_]), r=[("stg", si)], w=[pc_okey[oi_]])
        S.op("sp", I("dma_start", out=wscr[l_, :, o:o + n_], in_=pc_out[oi_][:, 0:n_]), r=[pc_okey[oi_]], w=[("scr", l_, o)],
             dma=("pco", oi_))
        pc["n"] += 1

    for k in cs:
        S.op("sp", I("dma_start", out=cs[k], in_=cd[k]), w=["c_" + k], dma="c_" + k)
    S.op("dve", I("tensor_copy", out=identb, in_=cs["identf"]), r=["c_identf"], w=["identb"])
    S.op("dve", I("memset", mhalf, -0.5), w=["mhalf"])
    S.op("dve", I("memset", onesc, 1.0), w=["onesc"])
    S.op("dve", I("tensor_scalar", out=bgh, in0=cs["bg"], scalar1=0.5, scalar2=None, op0=ALU.mult), r=["c_bg"], w=["bgh"])
    S.op("act", I("activation", out=esink, in_=cs["sink"], func=AF.Exp), r=["c_sink"], w=["esink"])
    CK = ["identb", "c_identf"]

    def to_hilo(t, src, skey, gb):
        xi = rot("xf", 1)
        skeys = skey if isinstance(skey, list) else [skey]
        for half in range(2):
            b = bank()
            pb, pk = PS(b)
            for c in range(4):
                tr(pb[:, c * 128:(c + 1) * 128], src[:, (half * 4 + c) * 128:(half * 4 + c + 1) * 128], cs["identf"],
                   r=skeys + ["c_identf"], w=[pk])
            if gb is None:
                S.op("act", I("activation",
                    out=xf[xi][:, half * 4:(half + 1) * 4, :], in_=pb.rearrange("p (c t) -> p c t", t=128), func=AF.Copy),
                    r=[pk], w=[("xf", xi, half)])
            else:
                g, bb, l = gb
                for c in range(4):
                    cc = half * 4 + c
                    S.op("dve", I("tensor_scalar",
                        out=xf[xi][:, cc, :], in0=pb[:, c * 128:(c + 1) * 128], scalar1=cs[g][:, l, cc:cc + 1],
                        scalar2=cs[bb][:, l, cc:cc + 1], op0=ALU.mult, op1=ALU.add),
                        r=[pk, "c_" + g, "c_" + bb], w=[("xf", xi, half)])
        ts = slice(t * 128, (t + 1) * 128)
        S.op("act", I("activation", out=xh[:, :, ts], in_=xf[xi], func=AF.Copy),
             r=[("xf", xi, 0), ("xf", xi, 1)], w=[("xh", t)])
        S.op("pool", I("tensor_tensor", out=xl[:, :, ts], in0=xf[xi], in1=xh[:, :, ts], op=ALU.subtract),
             r=[("xf", xi, 0), ("xf", xi, 1), ("xh", t)], w=[("xl", t)])

    def ln_stats_a(src_aps, skeys):
        si = rot("sm", 4)
        st = sm[si]
        for i, (a, k) in enumerate(zip(src_aps, skeys)):
            S.op("dve", I("bn_stats", out=st[:, i * 6:(i + 1) * 6], in_=a), r=[k], w=[("sm", si, i)])
        S.op("dve", I("bn_aggr", out=st[:, 12:14], in_=st[:, 0:12]), r=[("sm", si, 0), ("sm", si, 1)], w=[("sm", si, 2)])
        S.op("dve", I("tensor_scalar", out=st[:, 14:15], in0=st[:, 13:14], scalar1=EPS_P, scalar2=None, op0=ALU.add),
             r=[("sm", si, 2)], w=[("sm", si, 3)])
        S.op("pool", I("tensor_tensor", out=st[:, 15:16], in0=st[:, 14:15], in1=mhalf, op=ALU.pow),
             r=[("sm", si, 3), "mhalf"], w=[("sm", si, 4)])
        return si

    def ln_stats_b(si):
        st = sm[si]
        S.op("dve", I("tensor_scalar", out=st[:, 16:17], in0=st[:, 12:13], scalar1=-1.0, scalar2=st[:, 15:16],
                                              op0=ALU.mult, op1=ALU.mult), r=[("sm", si, 2), ("sm", si, 4)], w=[("sm", si, 5)])
        return st[:, 15:16], st[:, 16:17], [("sm", si, 4), ("sm", si, 5)]

    def layer_norm_stats(src_aps, skeys):
        si = rot("sm", 4)
        st = sm[si]
        for i, (a, k) in enumerate(zip(src_aps, skeys)):
            S.op("dve", I("bn_stats", out=st[:, i * 6:(i + 1) * 6], in_=a), r=[k], w=[("sm", si, i)])
        S.op("dve", I("bn_aggr", out=st[:, 12:14], in_=st[:, 0:12]), r=[("sm", si, 0), ("sm", si, 1)], w=[("sm", si, 2)])
        S.op("dve", I("tensor_scalar", out=st[:, 14:15], in0=st[:, 13:14], scalar1=EPS_P, scalar2=None, op0=ALU.add),
             r=[("sm", si, 2)], w=[("sm", si, 3)])
        S.op("pool", I("tensor_tensor", out=st[:, 15:16], in0=st[:, 14:15], in1=mhalf, op=ALU.pow),
             r=[("sm", si, 3), "mhalf"], w=[("sm", si, 4)])
        S.op("dve", I("tensor_scalar", out=st[:, 16:17], in0=st[:, 12:13], scalar1=-1.0, scalar2=st[:, 15:16],
                                              op0=ALU.mult, op1=ALU.mult), r=[("sm", si, 2), ("sm", si, 4)], w=[("sm", si, 5)])
        return st[:, 15:16], st[:, 16:17], [("sm", si, 4), ("sm", si, 5)]

    import os as _os
    STOP = _os.environ.get("KSTOP", "")

    class _Stop(Exception):
        pass

    KOFF = _os.environ.get("KOFF", "").split(",")

    def on(name):
        return name not in KOFF

    def stop_at(name):
        if STOP == name:
            raise _Stop()
    try:
      for sq in range(nseq):
          for t in range(NT):
              zi = rot("z", 2)
              S.op("sp", I("dma_start", out=zt[zi], in_=x_d[sq, t * 128:(t + 1) * 128, :]),
                   w=[("zt", zi)], dma=("zt", zi))
              to_hilo(t, zt[zi], ("zt", zi), None)

          for l in range(nlayers):
              last = (l == nlayers - 1)
              stop_at('load')
              fence(RKEYS_C + TB, RKEYS_A + TA)
              set_pool(range(8))
              if on("ones"):
                  S.op("dve", I("tensor_copy", out=vflat[:, :, 64:65], in_=onesc[:, 0:1].unsqueeze(1).to_broadcast([128, 224, 1])),
                       r=["onesc"], w=[("v", t) for t in range(NT)])
              p1pend = []
              for b in range(7):
                  ncol = len(QKV_BLOCKS[b])
                  wi = 0
                  load_block(l, [("qkv", b, 0), ("qkv", b, 1)], wbuf[wi][:, :, 0:ncol], ("wbuf", wi), sq == 0)
                  for t in range(NT):
                      ts = slice(t * 128, (t + 1) * 128)
                      bk = bank()
                      pb, pk = PS(bk)
                      for k in range(8):
                          mm(pb[:, 0:ncol], xh[:, k, ts], wbuf[wi][:, k, 0:ncol], k == 0, k == 7,
                             r=[("xh", t), (("wbuf", wi), k)], w=[pk])
                      if not on("evac"):
                          continue
                      if b >= 5:
                          h0, nh = (0, 8) if b == 5 else (8, 6)
                          S.op("act", I("activation",
                              out=vv[:, t, h0:h0 + nh, 0:64], in_=pb[:, 0:ncol].rearrange("p (h d) -> p h d", d=64), func=AF.Copy),
                              r=[pk], w=[("v", t)])
                          continue
                      nh = ncol // 64
                      qi = rot("tq", 3)
                      tqv = tq[qi][:, 0:ncol].rearrange("p (h d) -> p h d", d=64)
                      p3 = pb[:, 0:ncol].rearrange("p (h d) -> p h d", d=64)
                      S.op("act", I("activation", out=tqv, in_=p3, func=AF.Copy), r=[pk], w=[("tq", qi)])
                      tf = tfr[qi]
                      S.op("act", I("activation", out=tf[:, 0:nh, :], in_=p3[:, :, 0:16], func=AF.Copy), r=[pk], w=[("tfr", qi)])
                      if not on("rope"):
                          continue
                      rr = rp[qi]
                      for h0_ in range(0, nh, 4):
                          hn = min(4, nh - h0_)
                          for j, (lo, tabn) in enumerate([(0, "cos"), (8, "sin"), (8, "cos"), (0, "sin")]):
                              S.op("dve", I("tensor_tensor",
                                  out=rr[:, j, h0_:h0_ + hn, :], in0=tf[:, h0_:h0_ + hn, lo:lo + 8], in1=cs[tabn][:, t, 0:hn, :], op=ALU.mult),
                                  r=[("tfr", qi), "c_cos", "c_sin"], w=[("rp", qi, j)])
                      S.op("dve", I("tensor_tensor",
                          out=tqv[:, :, 0:8], in0=rr[:, 0, 0:nh, :], in1=rr[:, 1, 0:nh, :], op=ALU.subtract),
                          r=[("rp", qi, 0), ("rp", qi, 1)], w=[("tq", qi)])
                      S.op("dve", I("tensor_tensor",
                          out=tqv[:, :, 8:16], in0=rr[:, 2, 0:nh, :], in1=rr[:, 3, 0:nh, :], op=ALU.add),
                          r=[("rp", qi, 2), ("rp", qi, 3)], w=[("tq", qi)])
                      def p1_tail(b=b, t=t, ts=ts, qi=qi):
                          c0, c1 = QK_CHUNK_RANGES[b]
                          bt = bank()
                          pt_, ptk = PS(bt)
                          ptb = pt_.bitcast(BF)
                          for i in range(c1 - c0):
                              tr(ptb[:, i * 128:(i + 1) * 128], tq[qi][:, i * 128:(i + 1) * 128], identb, r=[("tq", qi), "identb"], w=[ptk])
                          for (a, bnd, dst, dk, base) in [(c0, min(c1, 10), qT, "qT", 0), (max(c0, 10), c1, kT, "kT", 10)]:
                              if bnd > a:
                                  S.op("act", I("activation",
                                      out=dst[:, a - base:bnd - base, ts],
                                      in_=ptb[:, (a - c0) * 128:(bnd - c0) * 128].rearrange("p (c t) -> p c t", t=128), func=AF.Copy),
                                      r=[ptk], w=[(dk, t)])
                      p1pend.append(p1_tail)
                      while len(p1pend) > 2:
                          p1pend.pop(0)()
              while p1pend:
                  p1pend.pop(0)()

              stop_at('p1')
              moff = {}
              o_ = 0
              for (name, W, r_) in GROUPS:
                  moff[name] = o_
                  o_ += (2 * GD[name] + 1) if name != "g2" else 6
              set_pool(range(4))
              if sq == 0:
                  precast_begin(l)
              p2pend = []
              lagq = []
              started = set()

              def stage(f2, lag):
                  lagq.append(f2)
                  while len(lagq) > lag:
                      lagq.pop(0)()
              o2T = [tq[0], tq[1], T[:, 1024:1536], T[:, 1536:2048]]
              o2k = [[("tq", 0)], [("tq", 1)], [("rp", 0, q_) for q_ in range(4)], [("rp", 1, q_) for q_ in range(4)]]
              for j in range(NT):
                  js = slice(j * 128, (j + 1) * 128)
                  oi = rot("oab", 2)
                  accs = [5, 6, 7]
                  si = rot("sm", 4)
                  st = sm[si]
                  if j % 4 == 0:
                      for s_ in range(4):
                          i_ = 8 + s_
                          qc2, kc2, hf2, vh2 = 4 + i_ // 2, 1 + i_ // 2, i_ % 2, 2 + i_
                          rows = slice(hf2 * 64, (hf2 + 1) * 64)
                          pa2, pa2k = PS(4)
                          first2 = True
                          for kt in range(max(0, j - 8), min(NT - 1, j + 3 + 8) + 1):
                              qa, qb = max(j, kt - 8), min(j + 3, kt + 8) + 1
                              n = qb - qa
                              pb, pk = PS(bank())
                              mm(pb[:, 0:n * 128], kT[rows, kc2, kt * 128:(kt + 1) * 128], qT[rows, qc2, qa * 128:qb * 128],
                                 True, False, r=[("kT", kt)] + [("qT", q_) for q_ in range(qa, qb)], w=[pk])
                              if kt - qa == 8:
                                  m0 = moff["g2"] * 128
                              elif kt - (qb - 1) == -8:
                                  m0 = (moff["g2"] + 6 - n) * 128
                              else:
                                  m0 = (moff["g2"] + 1) * 128
                              mm(pb[:, 0:n * 128], identb, cs["masks"][:, m0:m0 + n * 128], False, True, r=["identb", "c_masks"], w=[pk])
                              pi = rot("PT", NPT)
                              S.op("act", I("activation", out=PT[pi][:, 0:n * 128], in_=pb[:, 0:n * 128], func=AF.Exp, scale=0.125),
                                   r=[pk], w=[("PT", pi)])
                              if sq == 0:
                                  precast_tick()
                              stage(lambda pa2=pa2, pa2k=pa2k, qa=qa, qb=qb, kt=kt, vh2=vh2, pi=pi, n=n, first2=first2, j=j: mm(
                                  pa2[0:65, (qa - j) * 128:(qb - j) * 128], vv[:, kt, vh2, :], PT[pi][:, 0:n * 128], first2, False,
                                  r=[("PT", pi), ("v", kt)], w=[pa2k]), 2)
                              first2 = False
                          stage(lambda pa2=pa2, pa2k=pa2k, s_=s_: S.op("act", I("activation", out=o2T[s_][0:65, :], in_=pa2[0:65, :], func=AF.Copy),
                                                                       r=[pa2k], w=o2k[s_]), 2)

                  def pair_blocks(gname, qc, kc_, vhs, accb, slots):
                      Dm = GD[gname]
                      lo, hi = max(-Dm, -j), min(Dm, NT - 1 - j)
                      dls = list(range(lo, hi + 1))
                      pa, pak = PS(accb)
                      for b0 in range(0, len(dls), 4):
                          batch = dls[b0:b0 + 4]
                          n = len(batch)
                          pbs = [PS(bank()), PS(bank())]
                          for i, dl in enumerate(batch):
                              for hf in range(2):
                                  rows = slice(hf * 64, (hf + 1) * 64)
                                  pb, pk = pbs[hf]
                                  mm(pb[:, i * 128:(i + 1) * 128], kT[rows, kc_, (j + dl) * 128:(j + dl + 1) * 128], qT[rows, qc, js],
                                     i == 0, False, r=[("kT", j + dl), ("qT", j)], w=[pk])
                          if gname != "g2":
                              m0 = (moff[gname] + batch[0] + Dm) * 128
                          elif batch[0] == -8:
                              m0 = moff[gname] * 128
                          elif batch[-1] == 8:
                              m0 = (moff[gname] + 6 - n) * 128
                          else:
                              m0 = (moff[gname] + 1) * 128
                          pis = []
                          for hf in range(2):
                              pb, pk = pbs[hf]
                              mm(pb[:, 0:n * 128], identb, cs["masks"][:, m0:m0 + n * 128], False, True, r=["identb", "c_masks"], w=[pk])
                          for hf in range(2):
                              pb, pk = pbs[hf]
                              pi = rot("PT", NPT)
                              pis.append(pi)
                              S.op("act", I("activation", out=PT[pi][:, 0:n * 128], in_=pb[:, 0:n * 128], func=AF.Exp, scale=0.125),
                                   r=[pk], w=[("PT", pi)])
                          if sq == 0:
                              precast_tick()

                          def pvs(pis=pis, batch=batch, pa=pa, pak=pak, j=j):
                              for hf in range(2):
                                  pi = pis[hf]
                                  for i, dl in enumerate(batch):
                                      st_ = (j, accb) not in started
                                      started.add((j, accb))
                                      vo = 34816 + ((j + dl) * 14 + vhs[hf]) * 65
                                      nv = 128 if vo + 128 <= 34816 + 16 * 14 * 65 else 65
                                      mm(pa[:, slots[hf] * 128:slots[hf] * 128 + nv], PT[pi][:, i * 128:(i + 1) * 128], R[:, vo:vo + nv],
                                         st_, False, r=[("PT", pi), ("v", j + dl)], w=[pak])
                          stage(pvs, 1)

                  def normalise(bi, st=st, si=si, oi=oi, l=l):
                      pa, pak = PS(accs[bi])
                      pv = pa.rearrange("p (s c) -> p s c", c=128)
                      if bi < 2:
                          S.op("dve", I("tensor_tensor",
                              out=st[:, 20 + bi * 4:24 + bi * 4], in0=pv[:, :, 64], in1=esink[:, l * 8 + bi * 4:l * 8 + bi * 4 + 4], op=ALU.add),
                              r=[pak, "esink"], w=[("smd", si, bi)])
                      else:
                          S.op("dve", I("tensor_copy", out=st[:, 20 + bi * 4:24 + bi * 4], in_=pv[:, :, 64]),
                               r=[pak], w=[("smd", si, bi)])
                      S.op("dve", I("reciprocal", out=st[:, 20 + bi * 4:24 + bi * 4], in_=st[:, 20 + bi * 4:24 + bi * 4]),
                           r=[("smd", si, bi)], w=[("smr", si, bi)])
                      for s_ in range(4):
                          S.op("dve", I("tensor_scalar",
                              out=oab[oi][:, bi * 4 + s_, :], in0=pv[:, s_, 0:64], scalar1=st[:, 20 + bi * 4 + s_:21 + bi * 4 + s_],
                              scalar2=None, op0=ALU.mult), r=[pak, ("smr", si, bi)], w=[("oab", oi, bi)])

                  for s0 in (0, 2):
                      for g, gname in enumerate(["g0", "g1"]):
                          i_ = g * 4 + s0
                          pair_blocks(gname, 4 + i_ // 2, 1 + i_ // 2, (2 + i_, 3 + i_), accs[2], (s0, s0 + 1))
                      def g2acc(s0=s0, j=j):
                          for s_ in (s0, s0 + 1):
                              pa7, pa7k = PS(accs[2])
                              mm(pa7[:, s_ * 128:s_ * 128 + 65], o2T[s_][0:65, (j % 4) * 128:(j % 4 + 1) * 128], identb[0:65, 0:65],
                                 False, False, r=o2k[s_] + ["identb"], w=[pa7k])
                      stage(g2acc, 1)
                  stage(lambda: normalise(2), 1)
                  for c in range(4):
                      pair_blocks("A", c, 0, (0, 1), accs[c // 2], ((2 * c) % 4, (2 * c + 1) % 4))
                      if c == 1:
                          stage(lambda: normalise(0), 1)
                  stage(lambda: normalise(1), 1)
                  def p2_tail(oi=oi, js=js, j=j):
                      bt = bank()
                      pt_, ptk = PS(bt)
                      ptb = pt_.bitcast(BF)
                      of = oab[oi].rearrange("p h d -> p (h d)")
                      for i in range(6):
                          tr(ptb[:, i * 128:(i + 1) * 128], of[:, i * 128:(i + 1) * 128], identb,
                             r=[("oab", oi, 0), ("oab", oi, 1), ("oab", oi, 2), "identb"], w=[ptk])
                      S.op("act", I("activation",
                          out=qT[:, 0:6, js], in_=ptb[:, 0:768].rearrange("p (c t) -> p c t", t=128), func=AF.Copy), r=[ptk], w=[("qT", j)])
                  stage(p2_tail, 1)
              while lagq:
                  lagq.pop(0)()
              if sq == 0:
                  while pc["n"] < len(pc["jobs"]):
                      precast_step()

              stop_at('p2a')
              fence(RKEYS_A + TA, RKEYS_B + TB)
              set_pool(range(8))
              f0 = False

              def ld_wg(cb):
                  load_block(l, [("wg", cb, 0), ("wg", cb, 1)], wg[:, :, cb * 512:(cb + 1) * 512], ("wg", cb), f0)

              def ld_wa(cb):
                  load_block(l, [("wa", cb)], wa[:, :, cb * 512:(cb + 1) * 512], ("wa", cb), f0)

              def ld_wo(cb):
                  load_block(l, [("wo", cb, 0), ("wo", cb, 1)], wo[:, :, cb * 512:(cb + 1) * 512], ("wo", cb), f0)
              ld_wg(0)
              ld_wg(2)
              ld_wa(0)
              load_block(l, [("wb",)], wb, "wb", f0)
              ld_wg(1)
              ld_wg(3)
              ld_wa(1)
              ld_wo(0)
              ld_wo(1)
              pend = []

              def flush():
                  while pend:
                      pend.pop(0)()
              for tc in range(4):
                  cs_ = slice(tc * 512, (tc + 1) * 512)
                  tiles = list(range(tc * 4, tc * 4 + 4))
                  xk = [("xh", t) for t in tiles]
                  qk_ = [("qT", t) for t in tiles]
                  for dc in range(8):
                      if dc == 2:
                          flush()
                      tg_ = []
                      for gi in range(2):
                          b_ = bank()
                          pb, pk = PS(b_)
                          col = gi * 1024 + dc * 128
                          for k in range(8):
                              mm(pb, wg[:, k, col:col + 128], xh[:, k, cs_], k == 0, k == 7, r=[(("wg", col // 512), k)] + xk, w=[pk])
                          fi = rot("f", 3)
                          S.op("act", I("activation",
                              out=f512[fi], in_=pb, func=AF.Tanh, scale=0.5, bias=bgh[:, l, gi * 8 + dc:gi * 8 + dc + 1]),
                              r=[pk, "bgh"], w=[("f", fi)])
                          tg_.append(fi)
                      ba = bank()
                      pa, pak = PS(ba)
                      for k in range(4):
                          mm(pa, wa[:, k, dc * 128:(dc + 1) * 128], qT[:, k, cs_], k == 0, k == 3, r=[(("wa", dc // 4), k)] + qk_, w=[pak])
                      bb_ = bank()
                      pb2, pbk = PS(bb_)
                      for k in range(2):
                          mm(pb2, wb[:, k, dc * 128:(dc + 1) * 128], qT[:, 4 + k, cs_], k == 0, k == 1, r=[("wb", k)] + qk_, w=[pbk])
                      fa, fb_ = tg_
                      S.op("dve", I("scalar_tensor_tensor",
                          out=f512[fa], in0=f512[fa], scalar=1.0, in1=pa, op0=ALU.add, op1=ALU.mult), r=[("f", fa), pak], w=[("f", fa)])
                      S.op("dve", I("scalar_tensor_tensor",
                          out=f512[fb_], in0=f512[fb_], scalar=1.0, in1=pb2, op0=ALU.add, op1=ALU.mult), r=[("f", fb_), pbk], w=[("f", fb_)])
                      S.op("dve", I("tensor_tensor", out=f512[fa], in0=f512[fa], in1=f512[fb_], op=ALU.add),
                           r=[("f", fa), ("f", fb_)], w=[("f", fa)])
                      S.op("act", I("activation", out=mg[:, dc, :], in_=f512[fa], func=AF.Copy, scale=C_HALF), r=[("f", fa)], w=["mg"])
                  for ti, t in enumerate(tiles):
                      ts = slice(t * 128, (t + 1) * 128)
                      bks = [bank(), bank()]
                      for cb in range(2):
                          pb, pk = PS(bks[cb])
                          for k in range(8):
                              mm(pb, mg[:, k, ti * 128:(ti + 1) * 128], wo[:, k, cb * 512:(cb + 1) * 512], k == 0, False,
                                 r=["mg", (("wo", cb), k)], w=[pk])
                          for c in range(4):
                              mm(pb[:, c * 128:(c + 1) * 128], xh[:, cb * 4 + c, ts], identb, False, False, r=[("xh", t), "identb"], w=[pk])
                              mm(pb[:, c * 128:(c + 1) * 128], xl[:, cb * 4 + c, ts], identb, False, c == 3, r=[("xl", t), "identb"], w=[pk])
                      rstd, nmr, smk = layer_norm_stats([psb[bks[0]], psb[bks[1]]], [("ps", bks[0]), ("ps", bks[1])])
                      dbg_dump("sml", t, sm[(state["sm"] + 1) % 2], smk, 32)
                      zi = rot("zb", 3)
                      for cb in range(2):
                          S.op("act", I("activation",
                              out=ztb[zi][:, cb * 512:(cb + 1) * 512], in_=psb[bks[cb]], func=AF.Identity, scale=rstd, bias=nmr),
                              r=[("ps", bks[cb])] + smk, w=[("zt", zi)])
                      pend.append(lambda t=t, zi=zi, l=l: to_hilo(t, ztb[zi], ("zt", zi), ("g1", "b1", l)))
                      while len(pend) > 2:
                          pend.pop(0)()
              flush()

              stop_at('p2b')
              fence(RKEYS_B + [("qT", t) for t in range(NT)], RKEYS_C)
              set_pool(range(4))
              for cb in range(2):
                  load_block(l, [("wpg", cb, 0), ("wpg", cb, 1)], wpg[:, :, cb * 512:(cb + 1) * 512], ("wpg", cb), False)
              load_block(l, [("wp",)], wpl, "wpl", False)
              jobs = []
              for tc_ in range(4):
                  for fb in range(8):
                      jobs.append(([("wu", fb, 0), ("wu", fb, 1)], False))
                  for cb in range(2):
                      for fgp in range(4):
                          jobs.append(([("wd", cb, fgp * 2), ("wd", cb, fgp * 2 + 1)], False))
              jst = {"issued": 0, "used": 0}

              def next_block():
                  n = jst["used"]
                  while jst["issued"] < min(n + 3, len(jobs)):
                      m = jst["issued"]
                      load_block(l, jobs[m][0], wbuf[m % 3], ("wbuf", m % 3), jobs[m][1])
                      jst["issued"] += 1
                  jst["used"] += 1
                  return n % 3
              S.op("sp", I("dma_start", out=bpgbv, in_=cd["bpgb"][l]), w=["bpgb"], dma="bpgb")
              if last and sq == 0:
                  pass
              if last:
                  S.op("sp", I("dma_start", out=lnfg, in_=cd["lnfg"]), w=["lnfg"], dma="lnfg")
                  S.op("sp", I("dma_start", out=lnfb, in_=cd["lnfb"]), w=["lnfb"], dma="lnfb")
              for tc in range(4):
                  cs_ = slice(tc * 512, (tc + 1) * 512)
                  tiles = list(range(tc * 4, tc * 4 + 4))
                  xk = [("xh", t) for t in tiles]
                  pTi = {}
                  def p_dma(ti, tiles=tiles, l=l, sq=sq):
                      t = tiles[ti]
                      S.op("sp", I("dma_start", out=pin[ti % 2], in_=p_d[l, sq, t * 128:(t + 1) * 128, :]),
                           w=[("pin", ti % 2)], dma=("pin", ti % 2))

                  def p_tr(ti):
                      pt_, ptk = PS(bank())
                      for k2 in range(2):
                          tr(pt_[:, k2 * 128:(k2 + 1) * 128], pin[ti % 2][:, k2 * 128:(k2 + 1) * 128], cs["identf"],
                             r=[("pin", ti % 2), "c_identf"], w=[ptk])
                      S.op("act", I("activation",
                          out=pT[ti], in_=pt_[:, 0:256].rearrange("p (k t) -> p k t", t=128), func=AF.Copy), r=[ptk], w=[("pT", ti)])
                  for fb in range(8):
                      if fb == 0:
                          p_dma(0)
                          p_dma(1)
                      if fb == 2:
                          p_tr(0)
                          p_tr(1)
                          p_dma(2)
                          p_dma(3)
                      if fb == 6:
                          p_tr(2)
                          p_tr(3)
                      if fb == 4:
                          flush()
                      wi = next_block()
                      for fc in range(4):
                          b_ = bank()
                          pb, pk = PS(b_)
                          for k in range(8):
                              mm(pb, wbuf[wi][:, k, fc * 128:(fc + 1) * 128], xh[:, k, cs_], k == 0, k == 7, r=[(("wbuf", wi), k)] + xk, w=[pk])
                          fi = rot("f", 3)
                          S.op("act", I("activation", out=f512[fi], in_=pb, func=AF.Relu, scale=RELU_S),
                               r=[pk], w=[("f", fi)])
                          S.op("pool", I("tensor_tensor", out=uT[:, fb * 4 + fc, :], in0=f512[fi], in1=f512[fi],
                                                                                    op=ALU.mult), r=[("f", fi)], w=["uT"])
                  for cb in range(2):
                      ccs = slice(cb * 512, (cb + 1) * 512)
                      accb = [4, 5, 6, 7]
                      for fgp in range(4):
                          wi = next_block()
                          for f in range(8):
                              for ti in range(4):
                                  pb, pk = PS(accb[ti])
                                  mm(pb, uT[:, fgp * 8 + f, ti * 128:(ti + 1) * 128], wbuf[wi][:, f, :], fgp == 0 and f == 0, False,
                                     r=["uT", (("wbuf", wi), f)], w=[pk])
                      for ti, t in enumerate(tiles):
                          ts = slice(t * 128, (t + 1) * 128)
                          pb, pk = PS(accb[ti])
                          for c in range(4):
                              mm(pb[:, c * 128:(c + 1) * 128], xh[:, cb * 4 + c, ts], identb, False, False, r=[("xh", t), "identb"], w=[pk])
                              mm(pb[:, c * 128:(c + 1) * 128], xl[:, cb * 4 + c, ts], identb, False, c == 3, r=[("xl", t), "identb"], w=[pk])
                          bg_ = bank()
                          pg, pgk = PS(bg_)
                          for k in range(8):
                              mm(pg, xh[:, k, ts], wpg[:, k, ccs], k == 0, k == 7, r=[("xh", t), (("wpg", cb), k)], w=[pgk])
                          fi = rot("f", 3)
                          S.op("dve", I("tensor_tensor", out=f512[fi], in0=pg, in1=bpgbv[:, ccs], op=ALU.add),
                               r=[pgk, "bpgb"], w=[("f", fi)])
                          S.op("act", I("activation", out=f512[fi], in_=f512[fi], func=AF.Tanh, scale=0.5), r=[("f", fi)], w=[("f", fi)])
                          pi_ = ti
                          bw = bank()
                          pw_, pwk = PS(bw)
                          for k2 in range(2):
                              mm(pw_, pT[pi_][:, k2, :], wpl[:, k2, ccs], k2 == 0, k2 == 1, r=[("pT", pi_), ("wpl", k2)], w=[pwk])
                          S.op("dve", I("scalar_tensor_tensor",
                              out=f512[fi], in0=f512[fi], scalar=1.0, in1=pw_, op0=ALU.add, op1=ALU.mult), r=[("f", fi), pwk], w=[("f", fi)])
                          S.op("dve", I("scalar_tensor_tensor",
                              out=yb[:, ti, ccs], in0=f512[fi], scalar=C_HALF, in1=pb, op0=ALU.mult, op1=ALU.add),
                              r=[("f", fi), pk], w=[("yb", ti, cb)])
                  sis = []
                  for ti, t in enumerate(tiles):
                      yk = [("yb", ti, 0), ("yb", ti, 1)]
                      sis.append(ln_stats_a([yb[:, ti, 0:512], yb[:, ti, 512:1024]], yk))
                  for ti, t in enumerate(tiles):
                      yk = [("yb", ti, 0), ("yb", ti, 1)]
                      rstd, nmr, smk = ln_stats_b(sis[ti])
                      S.op("dve", I("tensor_scalar", out=yb[:, ti, :], in0=yb[:, ti, :], scalar1=rstd, scalar2=nmr,
                                    op0=ALU.mult, op1=ALU.add), r=yk + smk, w=yk)

                  def tail(tiles=tiles, l=l, sq=sq, last=last):
                      for ti, t in enumerate(tiles):
                          yk = [("yb", ti, 0), ("yb", ti, 1)]
                          if not last:
                              to_hilo(t, yb[:, ti, :], yk, ("g2", "b2", l))
                          else:
                              S.op("dve", I("tensor_tensor", out=yb[:, ti, :], in0=yb[:, ti, :], in1=lnfg, op=ALU.mult),
                                   r=yk + ["lnfg"], w=yk)
                              S.op("dve", I("tensor_tensor", out=yb[:, ti, :], in0=yb[:, ti, :], in1=lnfb, op=ALU.add),
                                   r=yk + ["lnfb"], w=yk)
                              S.op("sp", I("dma_start", out=out_d[sq, t * 128:(t + 1) * 128, :], in_=yb[:, ti, :]),
                                   r=yk, w=[("out", ti)], dma=("out", ti))
                  pend.append(tail)
              flush()

    except _Stop:
        pass
    S.emit(final_slots=[s for s in S.slots if isinstance(s, tuple) and s[0] == "out"])
    return nc


_CACHE = {}


def kernel(**inputs):
    inp = {k: np.asarray(v) for k, v in inputs.items()}
    wall = pack_weights(inp)
    consts = make_consts(inp)
    if "nc" not in _CACHE:
        _CACHE["nc"] = build()
    nc = _CACHE["nc"]
    in_maps = []
    for c in range(8):
        m = {"x": np.ascontiguousarray(inp["x"][2 * c:2 * c + 2]),
             "p": np.ascontiguousarray(inp["p"][:, 2 * c:2 * c + 2]),
             "w": wall}
        for k, v in consts.items():
            m["c_" + k] = v
        in_maps.append(m)
    res = run_bass_kernel_spmd(nc, in_maps, core_ids=list(range(8)))
    return np.concatenate([r["out"] for r in res.results], axis=0).astype(np.float32)
```

```python
import contextlib
import numpy as np
import ml_dtypes
import concourse.bass as bass
import concourse.mybir as mybir
from concourse.bass_utils import run_bass_kernel_spmd

F32 = mybir.dt.float32
BF = mybir.dt.bfloat16
AF = mybir.ActivationFunctionType
ALU = mybir.AluOpType

ENGS = ("pe", "act", "dve", "pool", "sp")
S_LEN = 2048
NT = 16
D = 1024
ALPHA = 8.0 ** 0.25
EPS_P = 1e-5 / (ALPHA * ALPHA)
C_HALF = 0.5 / ALPHA
RELU_S = ALPHA ** -0.5
A_PERM = [0, 4, 1, 5, 2, 6, 3, 7]
GROUPS = [("A", 128, 1), ("g0", 64, 1), ("g1", 256, 4), ("g2", 1024, 16)]
GD = {"A": 1, "g0": 1, "g1": 2, "g2": 8}


class Sched:
    def __init__(self, nc):
        self.nc = nc
        self.ops = {e: [] for e in ENGS}
        self.lastw = {}
        self.readers = {}
        self.slot_cnt = {}
        self.slots = []

    def op(self, eng, fn, r=(), w=(), dma=None):
        idx = len(self.ops[eng])
        w = list(w) + [("psr",) + tuple(k[1:]) for k in r if isinstance(k, tuple) and k[0] == "ps"]
        deps = set()
        for k in r:
            t = self.lastw.get(k)
            if t is not None:
                deps.add((t, "raw"))
        for k in w:
            t = self.lastw.get(k)
            if t is not None:
                deps.add((t, "waw"))
            for t in self.readers.get(k, ()):
                deps.add((t, "war"))
        if dma is not None:
            if dma not in self.slot_cnt:
                self.slot_cnt[dma] = 0
                self.slots.append(dma)
            self.slot_cnt[dma] += 1
            tok = ("d", dma, self.slot_cnt[dma])
        else:
            tok = ("c", eng, idx)
        for k in w:
            self.lastw[k] = tok
            self.readers[k] = []
        for k in r:
            self.readers.setdefault(k, []).append(tok)
        self.ops[eng].append(dict(fn=fn, deps=deps, dma=dma, sig=False))
        return tok

    def emit(self, final_slots=()):
        nc = self.nc
        for e in ENGS:
            for i, o in enumerate(self.ops[e]):
                lst = {}
                for (t, kind) in o["deps"]:
                    if t[0] == "c":
                        _, pe_, pi = t
                        if pe_ == e and o["dma"] is None:
                            if pe_ == "pe":
                                continue
                            if kind != "raw" or pi < i - 2:
                                continue
                        k = ("c", pe_)
                        lst[k] = max(lst.get(k, -1), pi)
                    else:
                        _, slot, cnt = t
                        k = ("d", slot)
                        lst[k] = max(lst.get(k, -1), cnt)
                o["need"] = lst
                for k, v in lst.items():
                    if k[0] == "c":
                        self.ops[k[1]][v]["sig"] = True
        cum = {}
        for e in ENGS:
            c = 0
            arr = []
            for o in self.ops[e]:
                if o["sig"] and o["dma"] is None:
                    c += 1
                arr.append(c)
            cum[e] = arr
        with contextlib.ExitStack() as st:
            esem = {e: st.enter_context(nc.semaphore("s_" + e)) for e in ENGS}
            dsem = {s: st.enter_context(nc.semaphore("d_%d" % i)) for i, s in enumerate(self.slots)}
            block = st.enter_context(nc.Block())

            def run(e, eng):
                waited = {}
                for o in self.ops[e]:
                    for k, v in o["need"].items():
                        if k[0] == "c":
                            val = cum[k[1]][v]
                            sem = esem[k[1]]
                        else:
                            val = 16 * v
                            sem = dsem[k[1]]
                        if waited.get(k, 0) < val:
                            eng.wait_ge(sem, val)
                            waited[k] = val
                    ins = o["fn"](eng)
                    if o["dma"] is not None:
                        ins.then_inc(dsem[o["dma"]], 16)
                    elif o["sig"]:
                        ins.then_inc(esem[e], 1)
                if e == "sp":
                    for s in final_slots:
                        eng.wait_ge(dsem[s], 16 * self.slot_cnt[s])

            @block.tensor
            def _(eng):
                run("pe", eng)

            @block.scalar
            def _(eng):
                run("act", eng)

            @block.vector
            def _(eng):
                run("dve", eng)

            @block.gpsimd
            def _(eng):
                run("pool", eng)

            @block.sync
            def _(eng):
                run("sp", eng)


def _qkv_cols():
    def head(base, h):
        return list(range(base + h * 64, base + (h + 1) * 64))
    chunks = []
    for (h0, h1) in [(0, 4), (1, 5), (2, 6), (3, 7)]:
        chunks.append(head(0, h0) + head(0, h1))
    for i in range(6):
        chunks.append(head(768, 2 * i) + head(768, 2 * i + 1))
    chunks.append(head(512, 0) + head(512, 1))
    for i in range(6):
        chunks.append(head(1536, 2 * i) + head(1536, 2 * i + 1))
    blocks = []
    for (a, b) in [(0, 4), (4, 8), (8, 11), (11, 15), (15, 17)]:
        blocks.append(sum(chunks[a:b], []))
    vheads = [head(640, 0), head(640, 1)] + [head(2304, i) for i in range(12)]
    blocks.append(sum(vheads[0:8], []))
    blocks.append(sum(vheads[8:14], []))
    return blocks


QKV_BLOCKS = _qkv_cols()
QK_CHUNK_RANGES = [(0, 4), (4, 8), (8, 11), (11, 15), (15, 17)]


def piece_table():
    tab = {}
    off = 0

    def add(name, kc, ncols):
        nonlocal off
        tab[name] = (off, kc, ncols)
        off += kc * ncols
    for b in range(7):
        for kh in range(2):
            add(("qkv", b, kh), 4, len(QKV_BLOCKS[b]))
    for cb in range(4):
        for kh in range(2):
            add(("wg", cb, kh), 4, 512)
    for cb in range(2):
        add(("wa", cb), 4, 512)
    add(("wb",), 2, 1024)
    for cb in range(2):
        for kh in range(2):
            add(("wo", cb, kh), 4, 512)
    for fb in range(8):
        for kh in range(2):
            add(("wu", fb, kh), 4, 512)
    for cb in range(2):
        for fg in range(8):
            add(("wd", cb, fg), 4, 512)
    for cb in range(2):
        for kh in range(2):
            add(("wpg", cb, kh), 4, 512)
    add(("wp",), 2, 1024)
    return tab, off


PIECES, WTOT = piece_table()


def _pm(W):
    K, C = W.shape
    return np.ascontiguousarray(W.reshape(K // 128, 128, C).transpose(1, 0, 2))


def pack_weights(inp):
    out = np.empty((4, 128, WTOT), np.float32)
    for l in range(4):
        def put(name, arr):
            off, kc, nc_ = PIECES[name]
            out[l, :, off:off + kc * nc_] = arr.reshape(128, kc * nc_)
        w_in = inp["w_in"][l]
        for b in range(7):
            wp = _pm(w_in[:, QKV_BLOCKS[b]])
            for kh in range(2):
                put(("qkv", b, kh), wp[:, kh * 4:(kh + 1) * 4, :])
        wg = _pm(w_in[:, 3072:5120])
        for cb in range(4):
            for kh in range(2):
                put(("wg", cb, kh), wg[:, kh * 4:(kh + 1) * 4, cb * 512:(cb + 1) * 512])
        rows = sum([list(range(h * 64, (h + 1) * 64)) for h in A_PERM], [])
        wa = _pm(inp["w_branch_a"][l][rows, :])
        for cb in range(2):
            put(("wa", cb), wa[:, :, cb * 512:(cb + 1) * 512])
        put(("wb",), _pm(inp["w_branch_b"][l]))
        wo = _pm(inp["w_out"][l])
        for cb in range(2):
            for kh in range(2):
                put(("wo", cb, kh), wo[:, kh * 4:(kh + 1) * 4, cb * 512:(cb + 1) * 512])
        wu = _pm(inp["w_up"][l])
        for fb in range(8):
            for kh in range(2):
                put(("wu", fb, kh), wu[:, kh * 4:(kh + 1) * 4, fb * 512:(fb + 1) * 512])
        wd = _pm(inp["w_down"][l])
        for cb in range(2):
            for fg in range(8):
                put(("wd", cb, fg), wd[:, fg * 4:(fg + 1) * 4, cb * 512:(cb + 1) * 512])
        wpg = _pm(inp["w_ple_gate"][l])
        for cb in range(2):
            for kh in range(2):
                put(("wpg", cb, kh), wpg[:, kh * 4:(kh + 1) * 4, cb * 512:(cb + 1) * 512])
        put(("wp",), _pm(inp["w_ple"][l]))
    return out


def make_consts(inp, lastl=3):
    c = {}
    c["identf"] = np.eye(128, dtype=np.float32)
    kl = np.arange(128)[:, None]
    ql = np.arange(128)[None, :]
    cols = []
    for (name, W, r) in GROUPS:
        Dm = GD[name]
        dls = range(-Dm, Dm + 1) if name != "g2" else [8, 0, 0, 0, 0, -8]
        for dl in dls:
            diff = 128 * dl + kl - ql
            cols.append((((np.abs(diff) <= W) & ((kl - ql) % r == 0)).astype(np.float32) - 1.0) * 30000.0)
    c["masks"] = np.concatenate(cols, axis=1).astype(ml_dtypes.bfloat16)
    pos = np.arange(S_LEN, dtype=np.float32)
    inv = (np.float32(500000.0) ** (-np.arange(0, 16, 2, dtype=np.float32) / np.float32(16))).astype(np.float32)
    ang = (pos[:, None] * inv[None, :]).astype(np.float32)
    c["cos"] = np.ascontiguousarray(np.broadcast_to(np.cos(ang).astype(np.float32).reshape(16, 128, 1, 8).transpose(1, 0, 2, 3), (128, 16, 4, 8)))
    c["sin"] = np.ascontiguousarray(np.broadcast_to(np.sin(ang).astype(np.float32).reshape(16, 128, 1, 8).transpose(1, 0, 2, 3), (128, 16, 4, 8)))
    def fm(v, nch):
        return np.ascontiguousarray(v.reshape(4, nch, 128).transpose(2, 0, 1))
    c["bg"] = fm(inp["b_gate"], 16)
    c["g1"] = fm(inp["ln1_g"], 8)
    c["b1"] = fm(inp["ln1_b"], 8)
    c["g2"] = fm(inp["ln2_g"], 8)
    c["b2"] = fm(inp["ln2_b"], 8)
    c["bpgb"] = np.ascontiguousarray(np.broadcast_to(inp["b_ple_gate"][:, None, :], (4, 128, 1024)))
    c["sink"] = np.ascontiguousarray(np.broadcast_to(inp["a_sink"][:, A_PERM].reshape(1, 32), (128, 32)))
    c["lnfg"] = np.ascontiguousarray(np.broadcast_to(inp["ln2_g"][lastl][None, :], (128, 1024)))
    c["lnfb"] = np.ascontiguousarray(np.broadcast_to(inp["ln2_b"][lastl][None, :], (128, 1024)))
    return c


CONST_SHAPES = {
    "identf": ([128, 128], F32), "masks": ([128, 17 * 128], BF), "cos": ([128, 16, 4, 8], F32),
    "sin": ([128, 16, 4, 8], F32), "bg": ([128, 4, 16], F32), "g1": ([128, 4, 8], F32),
    "b1": ([128, 4, 8], F32), "g2": ([128, 4, 8], F32), "b2": ([128, 4, 8], F32),
    "bpgb": ([4, 128, 1024], F32), "sink": ([128, 32], F32), "lnfg": ([128, 1024], F32),
    "lnfb": ([128, 1024], F32),
}


def build(nlayers=4, nseq=2):
    import os as _os
    nc = bass.Bass("TRN2", target_bir_lowering=False)
    S = Sched(nc)
    x_d = nc.dram_tensor("x", [nseq, S_LEN, D], F32, kind="ExternalInput").ap()
    p_d = nc.dram_tensor("p", [4, nseq, S_LEN, 256], F32, kind="ExternalInput").ap()
    w_d = nc.dram_tensor("w", [4, 128, WTOT], F32, kind="ExternalInput").ap()
    out_d = nc.dram_tensor("out", [nseq, S_LEN, D], F32, kind="ExternalOutput").ap()
    cd = {k: nc.dram_tensor("c_" + k, sh, dt, kind="ExternalInput").ap() for k, (sh, dt) in CONST_SHAPES.items()}
    KDBG = _os.environ.get("KDBG", "")
    dbg_d = nc.dram_tensor("dbg", [S_LEN, D], F32, kind="ExternalOutput").ap() if KDBG else None

    def dbg_dump(tag, t, ap, key, ncols=1024):
        if KDBG == tag:
            keys = key if isinstance(key, list) else [key]
            S.op("sp", I("dma_start", out=dbg_d[t * 128:(t + 1) * 128, 0:ncols], in_=ap), r=keys, w=[("dbg", t)], dma=("dbg", t % 4))

    def sb(name, shape, dt):
        return nc.alloc_sbuf_tensor(name, shape, dt).ap()

    xh = sb("xh", [128, 8, S_LEN], BF)
    xl = sb("xl", [128, 8, S_LEN], BF)
    RSZ = 49408
    R = sb("R", [128, RSZ], BF)
    wbuf = [sb("wbuf0", [128, 8, 512], BF),
            R[:, 40960:45056].rearrange("p (k c) -> p k c", c=512), R[:, 45056:49152].rearrange("p (k c) -> p k c", c=512)]
    NSTG = 3
    stg = [sb("stg%d" % i, [128, 512], F32) for i in range(NSTG)]
    wscr = nc.dram_tensor("wscr", [4, 128, WTOT], BF, kind="Internal").ap()
    T = sb("T", [128, 9216], BF)
    cs = {k: sb("k_" + k, sh, dt) for k, (sh, dt) in CONST_SHAPES.items() if k not in ("lnfg", "lnfb", "bpgb")}
    identb = sb("identb", [128, 128], BF)
    bgh = sb("bgh", [128, 4, 16], F32)
    esink = sb("esink", [128, 32], F32)
    mhalf = sb("mhalf", [128, 1], F32)
    onesc = sb("onesc", [128, 2], BF)
    tq = [T[:, i * 512:(i + 1) * 512] for i in range(2)] + [T[:, 7168:7680]]
    rp = [T[:, o_:o_ + 512].bitcast(F32).rearrange("p (a h d) -> p a h d", a=4, h=8) for o_ in (1024, 1536, 7680)]
    NPT = 6
    PT = [T[:, 2048 + i * 512:2048 + (i + 1) * 512] for i in range(3)] + [T[:, 5120 + i * 512:5120 + (i + 1) * 512] for i in range(3)]
    oab = [T[:, 3584 + i * 768:3584 + (i + 1) * 768].rearrange("p (h d) -> p h d", d=64) for i in range(2)]
    f512 = [T[:, i * 1024:(i + 1) * 1024].bitcast(F32) for i in range(3)]
    xf = [T[:, 3072:5120].bitcast(F32).rearrange("p (c t) -> p c t", t=128)]
    zt = [T[:, 5120 + i * 2048:5120 + (i + 1) * 2048].bitcast(F32) for i in range(2)]
    tfr = [T[:, o_:o_ + 256].bitcast(F32).rearrange("p (h d) -> p h d", d=16) for o_ in (6656, 6912, 8192)]
    TA = [("pcb",)] + [("tfr", i) for i in range(3)] + [("tq", i) for i in range(3)] + [("rp", i, j) for i in range(3) for j in range(4)] + [("PT", i) for i in range(6)] + [("oab", i, b) for i in range(2) for b in range(3)]
    TB = [("f", i) for i in range(3)] + [("xf", 0, h) for h in range(2)] + [("zt", i) for i in range(2)]
    sm = [sb("sm%d" % i, [128, 32], F32) for i in range(4)]
    pin = [sb("pin%d" % i, [128, 256], F32) for i in range(2)]
    pT = [sb("pT%d" % i, [128, 2, 128], BF) for i in range(4)]
    psb = [nc.alloc_psum_tensor("ps%d" % i, [128, 512], F32).ap() for i in range(8)]

    qT = R[:, 0:10 * 2048].rearrange("p (c t) -> p c t", t=2048)
    kT = R[:, 20480:20480 + 7 * 2048].rearrange("p (c t) -> p c t", t=2048)
    vv = R[:, 34816:34816 + 16 * 14 * 65].rearrange("p (t h d) -> p t h d", h=14, d=65)
    vflat = R[:, 34816:34816 + 16 * 14 * 65].rearrange("p (n d) -> p n d", d=65)
    o2 = 12288
    wg = R[:, o2:o2 + 8 * 2048].rearrange("p (k c) -> p k c", c=2048)
    wa = R[:, o2 + 16384:o2 + 16384 + 4 * 1024].rearrange("p (k c) -> p k c", c=1024)
    wb = R[:, o2 + 20480:o2 + 20480 + 2 * 1024].rearrange("p (k c) -> p k c", c=1024)
    wo = R[:, o2 + 22528:o2 + 22528 + 8 * 1024].rearrange("p (k c) -> p k c", c=1024)
    mg = R[:, o2 + 30720:o2 + 30720 + 8 * 512].rearrange("p (k c) -> p k c", c=512)
    uT = R[:, 0:32 * 512].rearrange("p (f t) -> p f t", t=512)
    yb = R[:, 16384:16384 + 8192].bitcast(F32).rearrange("p (t c) -> p t c", c=1024)
    wpg = R[:, 24576:24576 + 8 * 1024].rearrange("p (k c) -> p k c", c=1024)
    wpl = R[:, 32768:32768 + 2 * 1024].rearrange("p (k c) -> p k c", c=1024)
    lnfg = R[:, 34816:34816 + 2048].bitcast(F32)
    lnfb = R[:, 36864:36864 + 2048].bitcast(F32)
    bpgbv = R[:, 38912:38912 + 2048].bitcast(F32)
    ztb = [zt[0], zt[1], R[:, 47104:49152].bitcast(F32)]
    RKEYS_A = [("qT", t) for t in range(NT)] + [("kT", t) for t in range(NT)] + [("v", t) for t in range(NT)]
    RKEYS_B = ([(("wg", i), k) for i in range(4) for k in range(8)] + [(("wa", i), k) for i in range(2) for k in range(4)]
               + [(("wo", i), k) for i in range(2) for k in range(8)] + [("wb", k) for k in range(2)] + ["mg", ("zt", 2)])
    RKEYS_C = ["uT", "lnfg", "lnfb", "bpgb"] + [(("wpg", i), k) for i in range(2) for k in range(8)] + [("wpl", k) for k in range(2)] + [(("wbuf", i), k) for i in (1, 2) for k in range(8)] + [("yb", i, j) for i in range(4) for j in range(2)]

    state = {"zb": 0, "ce": 0, "bank": 0, "stg": 0, "wb": 0, "f": 0, "tq": 0, "PT": 0, "z": 0, "xf": 0, "sm": 0, "pin": 0, "oab": 0}

    def rot(name, n):
        i = state[name]
        state[name] = (i + 1) % n
        return i

    pool_ = {"l": list(range(8)), "i": 0}

    def set_pool(lst):
        pool_["l"] = list(lst)
        pool_["i"] = 0

    def bank():
        b = pool_["l"][pool_["i"] % len(pool_["l"])]
        pool_["i"] += 1
        return b

    def I(meth, *a, **kw):
        return lambda e: getattr(e, meth)(*a, **kw)

    def PS(i):
        return psb[i], ("ps", i)

    def mm(out, lhsT, rhs, start, stop, r, w):
        S.op("pe", I("matmul", out, lhsT=lhsT, rhs=rhs, start=start, stop=stop, skip_group_check=True), r=r, w=w)

    def tr(out, in_, ident, r, w):
        S.op("pe", I("transpose", out=out, in_=in_, identity=ident), r=r, w=w)

    def fence(old, new):
        if _os.environ.get("KOFF", "").find("fence") < 0:
            S.op("pool", I("nop", ), w=list(old) + list(new))

    def load_piece(l, name, dst, dkey):
        off, kc, ncol = PIECES[name]
        for k in range(kc):
            for c0 in range(0, ncol, 512):
                n = min(512, ncol - c0)
                si = rot("stg", 2)
                o = off + k * ncol + c0
                S.op("sp", I("dma_start", out=stg[si][:, 0:n], in_=w_d[l, :, o:o + n]),
                     w=[("stg", si)], dma=("stg", si))
                S.op("pool", I("tensor_copy", out=dst[:, k, c0:c0 + n], in_=stg[si][:, 0:n]),
                     r=[("stg", si)], w=[dkey])

    def load_block(l, pieces, dst, dkey, first, extra_w=()):
        off0, _, ncol = PIECES[pieces[0]]
        K = sum(PIECES[p_][1] for p_ in pieces)
        skeys = [("scr", l, off0 + k * ncol + c0) for k in range(K) for c0 in range(0, ncol, 512)]
        sview = wscr[l, :, off0:off0 + K * ncol].rearrange("p (k c) -> p k c", c=ncol)
        if not first:
            S.op("sp", I("dma_start", out=dst, in_=sview), r=skeys, w=[(dkey, k) for k in range(K)] + list(extra_w), dma=("ld", dkey))
            return
        for k in range(K):
            for c0 in range(0, ncol, 512):
                n = min(512, ncol - c0)
                si = rot("stg", NSTG)
                o = off0 + k * ncol + c0
                S.op("sp", I("dma_start", out=stg[si][:, 0:n], in_=w_d[l, :, o:o + n]), w=[("stg", si)], dma=("stg", si))
                if rot("ce", 2) == 0:
                    S.op("dve", I("tensor_copy", out=dst[:, k, c0:c0 + n], in_=stg[si][:, 0:n]), r=[("stg", si)], w=[(dkey, k)] + list(extra_w))
                else:
                    S.op("act", I("activation", out=dst[:, k, c0:c0 + n], in_=stg[si][:, 0:n], func=AF.Copy), r=[("stg", si)], w=[(dkey, k)] + list(extra_w))
        S.op("sp", I("dma_start", out=sview, in_=dst), r=[(dkey, k) for k in range(K)], w=skeys, dma=("scrw", dkey))

    pc_out = [T[:, 8448:8960], T[:, 7168:7680]]
    pc_okey = [("pcb",), ("tq", 2)]
    pc = {"jobs": [], "n": 0, "in": 0}

    def precast_begin(l):
        names = []
        for cb in range(4):
            names += [("wg", cb, 0), ("wg", cb, 1)]
        names += [("wa", 0), ("wa", 1), ("wb",)]
        for cb in range(2):
            names += [("wo", cb, 0), ("wo", cb, 1)]
        for cb in range(2):
            names += [("wpg", cb, 0), ("wpg", cb, 1)]
        names += [("wp",)]
        for fb in range(8):
            names += [("wu", fb, 0), ("wu", fb, 1)]
        for cb in range(2):
            for fg in range(8):
                names += [("wd", cb, fg)]
        jobs = []
        for nm in names:
            off, kc, ncol = PIECES[nm]
            for k in range(kc):
                for c0 in range(0, ncol, 512):
                    jobs.append((l, off + k * ncol + c0, min(512, ncol - c0)))
        pc["jobs"], pc["n"], pc["in"] = jobs, 0, 0

    def precast_issue_in():
        m = pc["in"]
        if m < len(pc["jobs"]):
            l_, o, n_ = pc["jobs"][m]
            si = m % NSTG
            S.op("sp", I("dma_start", out=stg[si][:, 0:n_], in_=w_d[l_, :, o:o + n_]), w=[("stg", si)], dma=("stg", si))
            pc["in"] += 1

    def precast_tick():
        pc["tick"] = pc.get("tick", 0) + 1
        if pc["tick"] % 2 == 0:
            precast_step()

    def precast_step():
        n = pc["n"]
        if n >= len(pc["jobs"]):
            return
        while pc["in"] < min(n + NSTG, len(pc["jobs"])):
            precast_issue_in()
        l_, o, n_ = pc["jobs"][n]
        si, oi_ = n % NSTG, n % 2
        S.op("dve", I("tensor_copy", out=pc_out[oi_][:, 0:n_], in_=stg[si][:, 0:n_]), r=[("stg", si)], w=[pc_okey[oi_]])
        S.op("sp", I("dma_start", out=wscr[l_, :, o:o + n_], in_=pc_out[oi_][:, 0:n_]), r=[pc_okey[oi_]], w=[("scr", l_, o)],
             dma=("pco", oi_))
        pc["n"] += 1

    for k in cs:
        S.op("sp", I("dma_start", out=cs[k], in_=cd[k]), w=["c_" + k], dma="c_" + k)
    S.op("dve", I("tensor_copy", out=identb, in_=cs["identf"]), r=["c_identf"], w=["identb"])
    S.op("dve", I("memset", mhalf, -0.5), w=["mhalf"])
    S.op("dve", I("memset", onesc, 1.0), w=["onesc"])
    S.op("dve", I("tensor_scalar", out=bgh, in0=cs["bg"], scalar1=0.5, scalar2=None, op0=ALU.mult), r=["c_bg"], w=["bgh"])
    S.op("act", I("activation", out=esink, in_=cs["sink"], func=AF.Exp), r=["c_sink"], w=["esink"])
    CK = ["identb", "c_identf"]

    def to_hilo(t, src, skey, gb):
        xi = rot("xf", 1)
        skeys = skey if isinstance(skey, list) else [skey]
        for half in range(2):
            b = bank()
            pb, pk = PS(b)
            for c in range(4):
                tr(pb[:, c * 128:(c + 1) * 128], src[:, (half * 4 + c) * 128:(half * 4 + c + 1) * 128], cs["identf"],
                   r=skeys + ["c_identf"], w=[pk])
            if gb is None:
                S.op("act", I("activation",
                    out=xf[xi][:, half * 4:(half + 1) * 4, :], in_=pb.rearrange("p (c t) -> p c t", t=128), func=AF.Copy),
                    r=[pk], w=[("xf", xi, half)])
            else:
                g, bb, l = gb
                for c in range(4):
                    cc = half * 4 + c
                    S.op("dve", I("tensor_scalar",
                        out=xf[xi][:, cc, :], in0=pb[:, c * 128:(c + 1) * 128], scalar1=cs[g][:, l, cc:cc + 1],
                        scalar2=cs[bb][:, l, cc:cc + 1], op0=ALU.mult, op1=ALU.add),
                        r=[pk, "c_" + g, "c_" + bb], w=[("xf", xi, half)])
        ts = slice(t * 128, (t + 1) * 128)
        S.op("act", I("activation", out=xh[:, :, ts], in_=xf[xi], func=AF.Copy),
             r=[("xf", xi, 0), ("xf", xi, 1)], w=[("xh", t)])
        S.op("pool", I("tensor_tensor", out=xl[:, :, ts], in0=xf[xi], in1=xh[:, :, ts], op=ALU.subtract),
             r=[("xf", xi, 0), ("xf", xi, 1), ("xh", t)], w=[("xl", t)])

    def ln_stats_a(src_aps, skeys):
        si = rot("sm", 4)
        st = sm[si]
        for i, (a, k) in enumerate(zip(src_aps, skeys)):
            S.op("dve", I("bn_stats", out=st[:, i * 6:(i + 1) * 6], in_=a), r=[k], w=[("sm", si, i)])
        S.op("dve", I("bn_aggr", out=st[:, 12:14], in_=st[:, 0:12]), r=[("sm", si, 0), ("sm", si, 1)], w=[("sm", si, 2)])
        S.op("dve", I("tensor_scalar", out=st[:, 14:15], in0=st[:, 13:14], scalar1=EPS_P, scalar2=None, op0=ALU.add),
             r=[("sm", si, 2)], w=[("sm", si, 3)])
        S.op("pool", I("tensor_tensor", out=st[:, 15:16], in0=st[:, 14:15], in1=mhalf, op=ALU.pow),
             r=[("sm", si, 3), "mhalf"], w=[("sm", si, 4)])
        return si

    def ln_stats_b(si):
        st = sm[si]
        S.op("dve", I("tensor_scalar", out=st[:, 16:17], in0=st[:, 12:13], scalar1=-1.0, scalar2=st[:, 15:16],
                                              op0=ALU.mult, op1=ALU.mult), r=[("sm", si, 2), ("sm", si, 4)], w=[("sm", si, 5)])
        return st[:, 15:16], st[:, 16:17], [("sm", si, 4), ("sm", si, 5)]

    def layer_norm_stats(src_aps, skeys):
        si = rot("sm", 4)
        st = sm[si]
        for i, (a, k) in enumerate(zip(src_aps, skeys)):
            S.op("dve", I("bn_stats", out=st[:, i * 6:(i + 1) * 6], in_=a), r=[k], w=[("sm", si, i)])
        S.op("dve", I("bn_aggr", out=st[:, 12:14], in_=st[:, 0:12]), r=[("sm", si, 0), ("sm", si, 1)], w=[("sm", si, 2)])
        S.op("dve", I("tensor_scalar", out=st[:, 14:15], in0=st[:, 13:14], scalar1=EPS_P, scalar2=None, op0=ALU.add),
             r=[("sm", si, 2)], w=[("sm", si, 3)])
        S.op("pool", I("tensor_tensor", out=st[:, 15:16], in0=st[:, 14:15], in1=mhalf, op=ALU.pow),
             r=[("sm", si, 3), "mhalf"], w=[("sm", si, 4)])
        S.op("dve", I("tensor_scalar", out=st[:, 16:17], in0=st[:, 12:13], scalar1=-1.0, scalar2=st[:, 15:16],
                                              op0=ALU.mult, op1=ALU.mult), r=[("sm", si, 2), ("sm", si, 4)], w=[("sm", si, 5)])
        return st[:, 15:16], st[:, 16:17], [("sm", si, 4), ("sm", si, 5)]

    import os as _os
    STOP = _os.environ.get("KSTOP", "")

    class _Stop(Exception):
        pass

    KOFF = _os.environ.get("KOFF", "").split(",")

    def on(name):
        return name not in KOFF

    def stop_at(name):
        if STOP == name:
            raise _Stop()
    try:
      for sq in range(nseq):
          for t in range(NT):
              zi = rot("z", 2)
              S.op("sp", I("dma_start", out=zt[zi], in_=x_d[sq, t * 128:(t + 1) * 128, :]),
                   w=[("zt", zi)], dma=("zt", zi))
              to_hilo(t, zt[zi], ("zt", zi), None)

          for l in range(nlayers):
              last = (l == nlayers - 1)
              stop_at('load')
              fence(RKEYS_C + TB, RKEYS_A + TA)
              set_pool(range(8))
              if on("ones"):
                  S.op("dve", I("tensor_copy", out=vflat[:, :, 64:65], in_=onesc[:, 0:1].unsqueeze(1).to_broadcast([128, 224, 1])),
                       r=["onesc"], w=[("v", t) for t in range(NT)])
              p1pend = []
              wbufP = T[:, 2048:6144].rearrange("p (k c) -> p k c", c=512)
              AL = [("PT", i_) for i_ in range(5)] + [("oab", i_, b_) for i_ in range(2) for b_ in range(3)]
              p1w = [(wbuf[0], ("wbuf", 0), []), (wbufP, ("wbuf", 3), AL)]

              def p1_load(b):
                  wb_, wk_, al_ = p1w[b % 2]
                  load_block(l, [("qkv", b, 0), ("qkv", b, 1)], wb_[:, :, 0:len(QKV_BLOCKS[b])], wk_, sq == 0, extra_w=al_)
              p1_load(0)
              for b in range(7):
                  ncol = len(QKV_BLOCKS[b])
                  if b + 1 < 7:
                      p1_load(b + 1)
                  for t in range(NT):
                      ts = slice(t * 128, (t + 1) * 128)
                      bk = bank()
                      pb, pk = PS(bk)
                      for k in range(8):
                          mm(pb[:, 0:ncol], xh[:, k, ts], p1w[b % 2][0][:, k, 0:ncol], k == 0, k == 7,
                             r=[("xh", t), (p1w[b % 2][1], k)] + p1w[b % 2][2], w=[pk])
                      if not on("evac"):
                          continue
                      if b >= 5:
                          h0, nh = (0, 8) if b == 5 else (8, 6)
                          S.op("act", I("activation",
                              out=vv[:, t, h0:h0 + nh, 0:64], in_=pb[:, 0:ncol].rearrange("p (h d) -> p h d", d=64), func=AF.Copy),
                              r=[pk], w=[("v", t)])
                          continue
                      nh = ncol // 64
                      qi = rot("tq", 3)
                      tqv = tq[qi][:, 0:ncol].rearrange("p (h d) -> p h d", d=64)
                      p3 = pb[:, 0:ncol].rearrange("p (h d) -> p h d", d=64)
                      S.op("act", I("activation", out=tqv, in_=p3, func=AF.Copy), r=[pk], w=[("tq", qi)])
                      tf = tfr[qi]
                      S.op("act", I("activation", out=tf[:, 0:nh, :], in_=p3[:, :, 0:16], func=AF.Copy), r=[pk], w=[("tfr", qi)])
                      if not on("rope"):
                          continue
                      rr = rp[qi]
                      for h0_ in range(0, nh, 4):
                          hn = min(4, nh - h0_)
                          for j, (lo, tabn) in enumerate([(0, "cos"), (8, "sin"), (8, "cos"), (0, "sin")]):
                              S.op("dve", I("tensor_tensor",
                                  out=rr[:, j, h0_:h0_ + hn, :], in0=tf[:, h0_:h0_ + hn, lo:lo + 8], in1=cs[tabn][:, t, 0:hn, :], op=ALU.mult),
                                  r=[("tfr", qi), "c_cos", "c_sin"], w=[("rp", qi, j)])
                      S.op("dve", I("tensor_tensor",
                          out=tqv[:, :, 0:8], in0=rr[:, 0, 0:nh, :], in1=rr[:, 1, 0:nh, :], op=ALU.subtract),
                          r=[("rp", qi, 0), ("rp", qi, 1)], w=[("tq", qi)])
                      S.op("dve", I("tensor_tensor",
                          out=tqv[:, :, 8:16], in0=rr[:, 2, 0:nh, :], in1=rr[:, 3, 0:nh, :], op=ALU.add),
                          r=[("rp", qi, 2), ("rp", qi, 3)], w=[("tq", qi)])
                      def p1_tail(b=b, t=t, ts=ts, qi=qi):
                          c0, c1 = QK_CHUNK_RANGES[b]
                          bt = bank()
                          pt_, ptk = PS(bt)
                          ptb = pt_.bitcast(BF)
                          for i in range(c1 - c0):
                              tr(ptb[:, i * 128:(i + 1) * 128], tq[qi][:, i * 128:(i + 1) * 128], identb, r=[("tq", qi), "identb"], w=[ptk])
                          for (a, bnd, dst, dk, base) in [(c0, min(c1, 10), qT, "qT", 0), (max(c0, 10), c1, kT, "kT", 10)]:
                              if bnd > a:
                                  S.op("act", I("activation",
                                      out=dst[:, a - base:bnd - base, ts],
                                      in_=ptb[:, (a - c0) * 128:(bnd - c0) * 128].rearrange("p (c t) -> p c t", t=128), func=AF.Copy),
                                      r=[ptk], w=[(dk, t)])
                      p1pend.append(p1_tail)
                      while len(p1pend) > 2:
                          p1pend.pop(0)()
              while p1pend:
                  p1pend.pop(0)()

              stop_at('p1')
              moff = {}
              o_ = 0
              for (name, W, r_) in GROUPS:
                  moff[name] = o_
                  o_ += (2 * GD[name] + 1) if name != "g2" else 6
              set_pool(range(4))
              if sq == 0:
                  precast_begin(l)
              p2pend = []
              lagq = []
              started = set()

              def stage(f2, lag):
                  lagq.append(f2)
                  while len(lagq) > lag:
                      lagq.pop(0)()
              o2T = [tq[0], tq[1], T[:, 1024:1536], T[:, 1536:2048]]
              o2k = [[("tq", 0)], [("tq", 1)], [("rp", 0, q_) for q_ in range(4)], [("rp", 1, q_) for q_ in range(4)]]
              for j in range(NT):
                  js = slice(j * 128, (j + 1) * 128)
                  oi = rot("oab", 2)
                  accs = [5, 6, 7]
                  si = rot("sm", 4)
                  st = sm[si]
                  if j % 4 == 0:
                      for s_ in range(4):
                          i_ = 8 + s_
                          qc2, kc2, hf2, vh2 = 4 + i_ // 2, 1 + i_ // 2, i_ % 2, 2 + i_
                          rows = slice(hf2 * 64, (hf2 + 1) * 64)
                          pa2, pa2k = PS(4)
                          first2 = True
                          for kt in range(max(0, j - 8), min(NT - 1, j + 3 + 8) + 1):
                              qa, qb = max(j, kt - 8), min(j + 3, kt + 8) + 1
                              n = qb - qa
                              pb, pk = PS(bank())
                              mm(pb[:, 0:n * 128], kT[rows, kc2, kt * 128:(kt + 1) * 128], qT[rows, qc2, qa * 128:qb * 128],
                                 True, False, r=[("kT", kt)] + [("qT", q_) for q_ in range(qa, qb)], w=[pk])
                              if kt - qa == 8:
                                  m0 = moff["g2"] * 128
                              elif kt - (qb - 1) == -8:
                                  m0 = (moff["g2"] + 6 - n) * 128
                              else:
                                  m0 = (moff["g2"] + 1) * 128
                              mm(pb[:, 0:n * 128], identb, cs["masks"][:, m0:m0 + n * 128], False, True, r=["identb", "c_masks"], w=[pk])
                              pi = rot("PT", NPT)
                              S.op("act", I("activation", out=PT[pi][:, 0:n * 128], in_=pb[:, 0:n * 128], func=AF.Exp, scale=0.125),
                                   r=[pk], w=[("PT", pi)])
                              if sq == 0:
                                  precast_tick()
                              stage(lambda pa2=pa2, pa2k=pa2k, qa=qa, qb=qb, kt=kt, vh2=vh2, pi=pi, n=n, first2=first2, j=j: mm(
                                  pa2[0:65, (qa - j) * 128:(qb - j) * 128], vv[:, kt, vh2, :], PT[pi][:, 0:n * 128], first2, False,
                                  r=[("PT", pi), ("v", kt)], w=[pa2k]), 2)
                              first2 = False
                          stage(lambda pa2=pa2, pa2k=pa2k, s_=s_: S.op("act", I("activation", out=o2T[s_][0:65, :], in_=pa2[0:65, :], func=AF.Copy),
                                                                       r=[pa2k], w=o2k[s_]), 2)

                  def pair_blocks(gname, qc, kc_, vhs, accb, slots):
                      Dm = GD[gname]
                      lo, hi = max(-Dm, -j), min(Dm, NT - 1 - j)
                      dls = list(range(lo, hi + 1))
                      pa, pak = PS(accb)
                      for b0 in range(0, len(dls), 4):
                          batch = dls[b0:b0 + 4]
                          n = len(batch)
                          pbs = [PS(bank()), PS(bank())]
                          for i, dl in enumerate(batch):
                              for hf in range(2):
                                  rows = slice(hf * 64, (hf + 1) * 64)
                                  pb, pk = pbs[hf]
                                  mm(pb[:, i * 128:(i + 1) * 128], kT[rows, kc_, (j + dl) * 128:(j + dl + 1) * 128], qT[rows, qc, js],
                                     i == 0, False, r=[("kT", j + dl), ("qT", j)], w=[pk])
                          if gname != "g2":
                              m0 = (moff[gname] + batch[0] + Dm) * 128
                          elif batch[0] == -8:
                              m0 = moff[gname] * 128
                          elif batch[-1] == 8:
                              m0 = (moff[gname] + 6 - n) * 128
                          else:
                              m0 = (moff[gname] + 1) * 128
                          pis = []
                          for hf in range(2):
                              pb, pk = pbs[hf]
                              mm(pb[:, 0:n * 128], identb, cs["masks"][:, m0:m0 + n * 128], False, True, r=["identb", "c_masks"], w=[pk])
                          for hf in range(2):
                              pb, pk = pbs[hf]
                              pi = rot("PT", NPT)
                              pis.append(pi)
                              S.op("act", I("activation", out=PT[pi][:, 0:n * 128], in_=pb[:, 0:n * 128], func=AF.Exp, scale=0.125),
                                   r=[pk], w=[("PT", pi)])
                          if sq == 0:
                              precast_tick()

                          def pvs(pis=pis, batch=batch, pa=pa, pak=pak, j=j):
                              for hf in range(2):
                                  pi = pis[hf]
                                  for i, dl in enumerate(batch):
                                      st_ = (j, accb) not in started
                                      started.add((j, accb))
                                      vo = 34816 + ((j + dl) * 14 + vhs[hf]) * 65
                                      nv = 128 if vo + 128 <= 34816 + 16 * 14 * 65 else 65
                                      mm(pa[:, slots[hf] * 128:slots[hf] * 128 + nv], PT[pi][:, i * 128:(i + 1) * 128], R[:, vo:vo + nv],
                                         st_, False, r=[("PT", pi), ("v", j + dl)], w=[pak])
                          stage(pvs, 1)

                  def normalise(bi, st=st, si=si, oi=oi, l=l):
                      pa, pak = PS(accs[bi])
                      pv = pa.rearrange("p (s c) -> p s c", c=128)
                      if bi < 2:
                          S.op("dve", I("tensor_tensor",
                              out=st[:, 20 + bi * 4:24 + bi * 4], in0=pv[:, :, 64], in1=esink[:, l * 8 + bi * 4:l * 8 + bi * 4 + 4], op=ALU.add),
                              r=[pak, "esink"], w=[("smd", si, bi)])
                      else:
                          S.op("dve", I("tensor_copy", out=st[:, 20 + bi * 4:24 + bi * 4], in_=pv[:, :, 64]),
                               r=[pak], w=[("smd", si, bi)])
                      S.op("dve", I("reciprocal", out=st[:, 20 + bi * 4:24 + bi * 4], in_=st[:, 20 + bi * 4:24 + bi * 4]),
                           r=[("smd", si, bi)], w=[("smr", si, bi)])
                      for s_ in range(4):
                          S.op("dve", I("tensor_scalar",
                              out=oab[oi][:, bi * 4 + s_, :], in0=pv[:, s_, 0:64], scalar1=st[:, 20 + bi * 4 + s_:21 + bi * 4 + s_],
                              scalar2=None, op0=ALU.mult), r=[pak, ("smr", si, bi)], w=[("oab", oi, bi)])

                  for s0 in (0, 2):
                      for g, gname in enumerate(["g0", "g1"]):
                          i_ = g * 4 + s0
                          pair_blocks(gname, 4 + i_ // 2, 1 + i_ // 2, (2 + i_, 3 + i_), accs[2], (s0, s0 + 1))
                      def g2acc(s0=s0, j=j):
                          for s_ in (s0, s0 + 1):
                              pa7, pa7k = PS(accs[2])
                              mm(pa7[:, s_ * 128:s_ * 128 + 65], o2T[s_][0:65, (j % 4) * 128:(j % 4 + 1) * 128], identb[0:65, 0:65],
                                 False, False, r=o2k[s_] + ["identb"], w=[pa7k])
                      stage(g2acc, 1)
                  stage(lambda: normalise(2), 1)
                  for c in range(4):
                      pair_blocks("A", c, 0, (0, 1), accs[c // 2], ((2 * c) % 4, (2 * c + 1) % 4))
                      if c == 1:
                          stage(lambda: normalise(0), 1)
                  stage(lambda: normalise(1), 1)
                  def p2_tail(oi=oi, js=js, j=j):
                      bt = bank()
                      pt_, ptk = PS(bt)
                      ptb = pt_.bitcast(BF)
                      of = oab[oi].rearrange("p h d -> p (h d)")
                      for i in range(6):
                          tr(ptb[:, i * 128:(i + 1) * 128], of[:, i * 128:(i + 1) * 128], identb,
                             r=[("oab", oi, 0), ("oab", oi, 1), ("oab", oi, 2), "identb"], w=[ptk])
                      S.op("act", I("activation",
                          out=qT[:, 0:6, js], in_=ptb[:, 0:768].rearrange("p (c t) -> p c t", t=128), func=AF.Copy), r=[ptk], w=[("qT", j)])
                  stage(p2_tail, 1)
              while lagq:
                  lagq.pop(0)()
              if sq == 0:
                  while pc["n"] < len(pc["jobs"]):
                      precast_step()

              stop_at('p2a')
              fence(RKEYS_A + TA, RKEYS_B + TB)
              set_pool(range(8))
              f0 = False

              def ld_wg(cb):
                  load_block(l, [("wg", cb, 0), ("wg", cb, 1)], wg[:, :, cb * 512:(cb + 1) * 512], ("wg", cb), f0)

              def ld_wa(cb):
                  load_block(l, [("wa", cb)], wa[:, :, cb * 512:(cb + 1) * 512], ("wa", cb), f0)

              def ld_wo(cb):
                  load_block(l, [("wo", cb, 0), ("wo", cb, 1)], wo[:, :, cb * 512:(cb + 1) * 512], ("wo", cb), f0)
              ld_wg(0)
              ld_wg(2)
              ld_wa(0)
              load_block(l, [("wb",)], wb, "wb", f0)
              ld_wg(1)
              ld_wg(3)
              ld_wa(1)
              ld_wo(0)
              ld_wo(1)
              pend = []

              def flush():
                  while pend:
                      pend.pop(0)()
              for tc in range(4):
                  cs_ = slice(tc * 512, (tc + 1) * 512)
                  tiles = list(range(tc * 4, tc * 4 + 4))
                  xk = [("xh", t) for t in tiles]
                  qk_ = [("qT", t) for t in tiles]
                  for dc in range(8):
                      if dc == 2:
                          flush()
                      tg_ = []
                      for gi in range(2):
                          b_ = bank()
                          pb, pk = PS(b_)
                          col = gi * 1024 + dc * 128
                          for k in range(8):
                              mm(pb, wg[:, k, col:col + 128], xh[:, k, cs_], k == 0, k == 7, r=[(("wg", col // 512), k)] + xk, w=[pk])
                          fi = rot("f", 3)
                          S.op("act", I("activation",
                              out=f512[fi], in_=pb, func=AF.Tanh, scale=0.5, bias=bgh[:, l, gi * 8 + dc:gi * 8 + dc + 1]),
                              r=[pk, "bgh"], w=[("f", fi)])
                          tg_.append(fi)
                      ba = bank()
                      pa, pak = PS(ba)
                      for k in range(4):
                          mm(pa, wa[:, k, dc * 128:(dc + 1) * 128], qT[:, k, cs_], k == 0, k == 3, r=[(("wa", dc // 4), k)] + qk_, w=[pak])
                      bb_ = bank()
                      pb2, pbk = PS(bb_)
                      for k in range(2):
                          mm(pb2, wb[:, k, dc * 128:(dc + 1) * 128], qT[:, 4 + k, cs_], k == 0, k == 1, r=[("wb", k)] + qk_, w=[pbk])
                      fa, fb_ = tg_
                      S.op("dve", I("scalar_tensor_tensor",
                          out=f512[fa], in0=f512[fa], scalar=1.0, in1=pa, op0=ALU.add, op1=ALU.mult), r=[("f", fa), pak], w=[("f", fa)])
                      S.op("dve", I("scalar_tensor_tensor",
                          out=f512[fb_], in0=f512[fb_], scalar=1.0, in1=pb2, op0=ALU.add, op1=ALU.mult), r=[("f", fb_), pbk], w=[("f", fb_)])
                      S.op("dve", I("tensor_tensor", out=f512[fa], in0=f512[fa], in1=f512[fb_], op=ALU.add),
                           r=[("f", fa), ("f", fb_)], w=[("f", fa)])
                      S.op("act", I("activation", out=mg[:, dc, :], in_=f512[fa], func=AF.Copy, scale=C_HALF), r=[("f", fa)], w=["mg"])
                  for ti, t in enumerate(tiles):
                      ts = slice(t * 128, (t + 1) * 128)
                      bks = [bank(), bank()]
                      for cb in range(2):
                          pb, pk = PS(bks[cb])
                          for k in range(8):
                              mm(pb, mg[:, k, ti * 128:(ti + 1) * 128], wo[:, k, cb * 512:(cb + 1) * 512], k == 0, False,
                                 r=["mg", (("wo", cb), k)], w=[pk])
                          for c in range(4):
                              mm(pb[:, c * 128:(c + 1) * 128], xh[:, cb * 4 + c, ts], identb, False, False, r=[("xh", t), "identb"], w=[pk])
                              mm(pb[:, c * 128:(c + 1) * 128], xl[:, cb * 4 + c, ts], identb, False, c == 3, r=[("xl", t), "identb"], w=[pk])
                      rstd, nmr, smk = layer_norm_stats([psb[bks[0]], psb[bks[1]]], [("ps", bks[0]), ("ps", bks[1])])
                      dbg_dump("sml", t, sm[(state["sm"] + 1) % 2], smk, 32)
                      zi = rot("zb", 3)
                      for cb in range(2):
                          S.op("act", I("activation",
                              out=ztb[zi][:, cb * 512:(cb + 1) * 512], in_=psb[bks[cb]], func=AF.Identity, scale=rstd, bias=nmr),
                              r=[("ps", bks[cb])] + smk, w=[("zt", zi)])
                      pend.append(lambda t=t, zi=zi, l=l: to_hilo(t, ztb[zi], ("zt", zi), ("g1", "b1", l)))
                      while len(pend) > 2:
                          pend.pop(0)()
              flush()

              stop_at('p2b')
              fence(RKEYS_B + [("qT", t) for t in range(NT)], RKEYS_C)
              set_pool(range(4))
              for cb in range(2):
                  load_block(l, [("wpg", cb, 0), ("wpg", cb, 1)], wpg[:, :, cb * 512:(cb + 1) * 512], ("wpg", cb), False)
              load_block(l, [("wp",)], wpl, "wpl", False)
              jobs = []
              for tc_ in range(4):
                  for fb in range(8):
                      jobs.append(([("wu", fb, 0), ("wu", fb, 1)], False))
                  for cb in range(2):
                      for fgp in range(4):
                          jobs.append(([("wd", cb, fgp * 2), ("wd", cb, fgp * 2 + 1)], False))
              jst = {"issued": 0, "used": 0}

              def next_block():
                  n = jst["used"]
                  while jst["issued"] < min(n + 3, len(jobs)):
                      m = jst["issued"]
                      load_block(l, jobs[m][0], wbuf[m % 3], ("wbuf", m % 3), jobs[m][1])
                      jst["issued"] += 1
                  jst["used"] += 1
                  return n % 3
              S.op("sp", I("dma_start", out=bpgbv, in_=cd["bpgb"][l]), w=["bpgb"], dma="bpgb")
              if last and sq == 0:
                  pass
              if last:
                  S.op("sp", I("dma_start", out=lnfg, in_=cd["lnfg"]), w=["lnfg"], dma="lnfg")
                  S.op("sp", I("dma_start", out=lnfb, in_=cd["lnfb"]), w=["lnfb"], dma="lnfb")
              for tc in range(4):
                  cs_ = slice(tc * 512, (tc + 1) * 512)
                  tiles = list(range(tc * 4, tc * 4 + 4))
                  xk = [("xh", t) for t in tiles]
                  pTi = {}
                  def p_dma(ti, tiles=tiles, l=l, sq=sq):
                      t = tiles[ti]
                      S.op("sp", I("dma_start", out=pin[ti % 2], in_=p_d[l, sq, t * 128:(t + 1) * 128, :]),
                           w=[("pin", ti % 2)], dma=("pin", ti % 2))

                  def p_tr(ti):
                      pt_, ptk = PS(bank())
                      for k2 in range(2):
                          tr(pt_[:, k2 * 128:(k2 + 1) * 128], pin[ti % 2][:, k2 * 128:(k2 + 1) * 128], cs["identf"],
                             r=[("pin", ti % 2), "c_identf"], w=[ptk])
                      S.op("act", I("activation",
                          out=pT[ti], in_=pt_[:, 0:256].rearrange("p (k t) -> p k t", t=128), func=AF.Copy), r=[ptk], w=[("pT", ti)])
                  for fb in range(8):
                      if fb == 0:
                          p_dma(0)
                          p_dma(1)
                      if fb == 2:
                          p_tr(0)
                          p_tr(1)
                          p_dma(2)
                          p_dma(3)
                      if fb == 6:
                          p_tr(2)
                          p_tr(3)
                      if fb == 4:
                          flush()
                      wi = next_block()
                      for fc in range(4):
                          b_ = bank()
                          pb, pk = PS(b_)
                          for k in range(8):
                              mm(pb, wbuf[wi][:, k, fc * 128:(fc + 1) * 128], xh[:, k, cs_], k == 0, k == 7, r=[(("wbuf", wi), k)] + xk, w=[pk])
                          fi = rot("f", 3)
                          S.op("act", I("activation", out=f512[fi], in_=pb, func=AF.Relu, scale=RELU_S),
                               r=[pk], w=[("f", fi)])
                          S.op("pool", I("tensor_tensor", out=uT[:, fb * 4 + fc, :], in0=f512[fi], in1=f512[fi],
                                                                                    op=ALU.mult), r=[("f", fi)], w=["uT"])
                  for cb in range(2):
                      ccs = slice(cb * 512, (cb + 1) * 512)
                      accb = [4, 5, 6, 7]
                      for fgp in range(4):
                          wi = next_block()
                          for f in range(8):
                              for ti in range(4):
                                  pb, pk = PS(accb[ti])
                                  mm(pb, uT[:, fgp * 8 + f, ti * 128:(ti + 1) * 128], wbuf[wi][:, f, :], fgp == 0 and f == 0, False,
                                     r=["uT", (("wbuf", wi), f)], w=[pk])
                      for ti, t in enumerate(tiles):
                          ts = slice(t * 128, (t + 1) * 128)
                          pb, pk = PS(accb[ti])
                          for c in range(4):
                              mm(pb[:, c * 128:(c + 1) * 128], xh[:, cb * 4 + c, ts], identb, False, False, r=[("xh", t), "identb"], w=[pk])
                              mm(pb[:, c * 128:(c + 1) * 128], xl[:, cb * 4 + c, ts], identb, False, c == 3, r=[("xl", t), "identb"], w=[pk])
                          bg_ = bank()
                          pg, pgk = PS(bg_)
                          for k in range(8):
                              mm(pg, xh[:, k, ts], wpg[:, k, ccs], k == 0, k == 7, r=[("xh", t), (("wpg", cb), k)], w=[pgk])
                          fi = rot("f", 3)
                          S.op("dve", I("tensor_tensor", out=f512[fi], in0=pg, in1=bpgbv[:, ccs], op=ALU.add),
                               r=[pgk, "bpgb"], w=[("f", fi)])
                          S.op("act", I("activation", out=f512[fi], in_=f512[fi], func=AF.Tanh, scale=0.5), r=[("f", fi)], w=[("f", fi)])
                          pi_ = ti
                          bw = bank()
                          pw_, pwk = PS(bw)
                          for k2 in range(2):
                              mm(pw_, pT[pi_][:, k2, :], wpl[:, k2, ccs], k2 == 0, k2 == 1, r=[("pT", pi_), ("wpl", k2)], w=[pwk])
                          S.op("dve", I("scalar_tensor_tensor",
                              out=f512[fi], in0=f512[fi], scalar=1.0, in1=pw_, op0=ALU.add, op1=ALU.mult), r=[("f", fi), pwk], w=[("f", fi)])
                          S.op("dve", I("scalar_tensor_tensor",
                              out=yb[:, ti, ccs], in0=f512[fi], scalar=C_HALF, in1=pb, op0=ALU.mult, op1=ALU.add),
                              r=[("f", fi), pk], w=[("yb", ti, cb)])
                  sis = []
                  for ti, t in enumerate(tiles):
                      yk = [("yb", ti, 0), ("yb", ti, 1)]
                      sis.append(ln_stats_a([yb[:, ti, 0:512], yb[:, ti, 512:1024]], yk))
                  for ti, t in enumerate(tiles):
                      yk = [("yb", ti, 0), ("yb", ti, 1)]
                      rstd, nmr, smk = ln_stats_b(sis[ti])
                      S.op("dve", I("tensor_scalar", out=yb[:, ti, :], in0=yb[:, ti, :], scalar1=rstd, scalar2=nmr,
                                    op0=ALU.mult, op1=ALU.add), r=yk + smk, w=yk)

                  def tail(tiles=tiles, l=l, sq=sq, last=last):
                      for ti, t in enumerate(tiles):
                          yk = [("yb", ti, 0), ("yb", ti, 1)]
                          if not last:
                              to_hilo(t, yb[:, ti, :], yk, ("g2", "b2", l))
                          else:
                              S.op("dve", I("tensor_tensor", out=yb[:, ti, :], in0=yb[:, ti, :], in1=lnfg, op=ALU.mult),
                                   r=yk + ["lnfg"], w=yk)
                              S.op("dve", I("tensor_tensor", out=yb[:, ti, :], in0=yb[:, ti, :], in1=lnfb, op=ALU.add),
                                   r=yk + ["lnfb"], w=yk)
                              S.op("sp", I("dma_start", out=out_d[sq, t * 128:(t + 1) * 128, :], in_=yb[:, ti, :]),
                                   r=yk, w=[("out", ti)], dma=("out", ti))
                  pend.append(tail)
              flush()

    except _Stop:
        pass
    S.emit(final_slots=[s for s in S.slots if isinstance(s, tuple) and s[0] == "out"])
    return nc


_CACHE = {}


def kernel(**inputs):
    inp = {k: np.asarray(v) for k, v in inputs.items()}
    wall = pack_weights(inp)
    consts = make_consts(inp)
    if "nc" not in _CACHE:
        _CACHE["nc"] = build()
    nc = _CACHE["nc"]
    in_maps = []
    for c in range(8):
        m = {"x": np.ascontiguousarray(inp["x"][2 * c:2 * c + 2]),
             "p": np.ascontiguousarray(inp["p"][:, 2 * c:2 * c + 2]),
             "w": wall}
        for k, v in consts.items():
            m["c_" + k] = v
        in_maps.append(m)
    res = run_bass_kernel_spmd(nc, in_maps, core_ids=list(range(8)))
    return np.concatenate([r["out"] for r in res.results], axis=0).astype(np.float32)
```

```python
import contextlib
import numpy as np
import ml_dtypes
import concourse.bass as bass
import concourse.mybir as mybir
from concourse.bass_utils import run_bass_kernel_spmd

F32 = mybir.dt.float32
BF = mybir.dt.bfloat16
AF = mybir.ActivationFunctionType
ALU = mybir.AluOpType

ENGS = ("pe", "act", "dve", "pool", "sp")
S_LEN = 2048
NT = 16
D = 1024
ALPHA = 8.0 ** 0.25
EPS_P = 1e-5 / (ALPHA * ALPHA)
C_HALF = 0.5 / ALPHA
RELU_S = ALPHA ** -0.5
A_PERM = [0, 4, 1, 5, 2, 6, 3, 7]
GROUPS = [("A", 128, 1), ("g0", 64, 1), ("g1", 256, 4), ("g2", 1024, 16)]
GD = {"A": 1, "g0": 1, "g1": 2, "g2": 8}


class Sched:
    def __init__(self, nc):
        self.nc = nc
        self.ops = {e: [] for e in ENGS}
        self.lastw = {}
        self.readers = {}
        self.slot_cnt = {}
        self.slots = []

    def op(self, eng, fn, r=(), w=(), dma=None):
        idx = len(self.ops[eng])
        w = list(w) + [("psr",) + tuple(k[1:]) for k in r if isinstance(k, tuple) and k[0] == "ps"]
        deps = set()
        for k in r:
            t = self.lastw.get(k)
            if t is not None:
                deps.add((t, "raw"))
        for k in w:
            t = self.lastw.get(k)
            if t is not None:
                deps.add((t, "waw"))
            for t in self.readers.get(k, ()):
                deps.add((t, "war"))
        if dma is not None:
            if dma not in self.slot_cnt:
                self.slot_cnt[dma] = 0
                self.slots.append(dma)
            self.slot_cnt[dma] += 1
            tok = ("d", dma, self.slot_cnt[dma])
        else:
            tok = ("c", eng, idx)
        for k in w:
            self.lastw[k] = tok
            self.readers[k] = []
        for k in r:
            self.readers.setdefault(k, []).append(tok)
        self.ops[eng].append(dict(fn=fn, deps=deps, dma=dma, sig=False))
        return tok

    def emit(self, final_slots=()):
        nc = self.nc
        for e in ENGS:
            for i, o in enumerate(self.ops[e]):
                lst = {}
                for (t, kind) in o["deps"]:
                    if t[0] == "c":
                        _, pe_, pi = t
                        if pe_ == e and o["dma"] is None:
                            if pe_ == "pe":
                                continue
                            if kind != "raw" or pi < i - 2:
                                continue
                        k = ("c", pe_)
                        lst[k] = max(lst.get(k, -1), pi)
                    else:
                        _, slot, cnt = t
                        k = ("d", slot)
                        lst[k] = max(lst.get(k, -1), cnt)
                o["need"] = lst
                for k, v in lst.items():
                    if k[0] == "c":
                        self.ops[k[1]][v]["sig"] = True
        cum = {}
        for e in ENGS:
            c = 0
            arr = []
            for o in self.ops[e]:
                if o["sig"] and o["dma"] is None:
                    c += 1
                arr.append(c)
            cum[e] = arr
        with contextlib.ExitStack() as st:
            esem = {e: st.enter_context(nc.semaphore("s_" + e)) for e in ENGS}
            dsem = {s: st.enter_context(nc.semaphore("d_%d" % i)) for i, s in enumerate(self.slots)}
            block = st.enter_context(nc.Block())

            def run(e, eng):
                waited = {}
                for o in self.ops[e]:
                    for k, v in o["need"].items():
                        if k[0] == "c":
                            val = cum[k[1]][v]
                            sem = esem[k[1]]
                        else:
                            val = 16 * v
                            sem = dsem[k[1]]
                        if waited.get(k, 0) < val:
                            eng.wait_ge(sem, val)
                            waited[k] = val
                    ins = o["fn"](eng)
                    if o["dma"] is not None:
                        ins.then_inc(dsem[o["dma"]], 16)
                    elif o["sig"]:
                        ins.then_inc(esem[e], 1)
                if e == "sp":
                    for s in final_slots:
                        eng.wait_ge(dsem[s], 16 * self.slot_cnt[s])

            @block.tensor
            def _(eng):
                run("pe", eng)

            @block.scalar
            def _(eng):
                run("act", eng)

            @block.vector
            def _(eng):
                run("dve", eng)

            @block.gpsimd
            def _(eng):
                run("pool", eng)

            @block.sync
            def _(eng):
                run("sp", eng)


def _qkv_cols():
    def head(base, h):
        return list(range(base + h * 64, base + (h + 1) * 64))
    chunks = []
    for (h0, h1) in [(0, 4), (1, 5), (2, 6), (3, 7)]:
        chunks.append(head(0, h0) + head(0, h1))
    for i in range(6):
        chunks.append(head(768, 2 * i) + head(768, 2 * i + 1))
    chunks.append(head(512, 0) + head(512, 1))
    for i in range(6):
        chunks.append(head(1536, 2 * i) + head(1536, 2 * i + 1))
    blocks = []
    for (a, b) in [(0, 4), (4, 8), (8, 11), (11, 15), (15, 17)]:
        blocks.append(sum(chunks[a:b], []))
    vheads = [head(640, 0), head(640, 1)] + [head(2304, i) for i in range(12)]
    blocks.append(sum(vheads[0:8], []))
    blocks.append(sum(vheads[8:14], []))
    return blocks


QKV_BLOCKS = _qkv_cols()
QK_CHUNK_RANGES = [(0, 4), (4, 8), (8, 11), (11, 15), (15, 17)]


def piece_table():
    tab = {}
    off = 0

    def add(name, kc, ncols):
        nonlocal off
        tab[name] = (off, kc, ncols)
        off += kc * ncols
    for b in range(7):
        for kh in range(2):
            add(("qkv", b, kh), 4, len(QKV_BLOCKS[b]))
    for cb in range(4):
        for kh in range(2):
            add(("wg", cb, kh), 4, 512)
    for cb in range(2):
        add(("wa", cb), 4, 512)
    add(("wb",), 2, 1024)
    for cb in range(2):
        for kh in range(2):
            add(("wo", cb, kh), 4, 512)
    for fb in range(8):
        for kh in range(2):
            add(("wu", fb, kh), 4, 512)
    for cb in range(2):
        for fg in range(8):
            add(("wd", cb, fg), 4, 512)
    for cb in range(2):
        for kh in range(2):
            add(("wpg", cb, kh), 4, 512)
    add(("wp",), 2, 1024)
    return tab, off


PIECES, WTOT = piece_table()


def _pm(W):
    K, C = W.shape
    return np.ascontiguousarray(W.reshape(K // 128, 128, C).transpose(1, 0, 2))


def pack_weights(inp):
    out = np.empty((4, 128, WTOT), np.float32)
    for l in range(4):
        def put(name, arr):
            off, kc, nc_ = PIECES[name]
            out[l, :, off:off + kc * nc_] = arr.reshape(128, kc * nc_)
        w_in = inp["w_in"][l]
        for b in range(7):
            wp = _pm(w_in[:, QKV_BLOCKS[b]])
            for kh in range(2):
                put(("qkv", b, kh), wp[:, kh * 4:(kh + 1) * 4, :])
        wg = _pm(w_in[:, 3072:5120])
        for cb in range(4):
            for kh in range(2):
                put(("wg", cb, kh), wg[:, kh * 4:(kh + 1) * 4, cb * 512:(cb + 1) * 512])
        rows = sum([list(range(h * 64, (h + 1) * 64)) for h in A_PERM], [])
        wa = _pm(inp["w_branch_a"][l][rows, :])
        for cb in range(2):
            put(("wa", cb), wa[:, :, cb * 512:(cb + 1) * 512])
        put(("wb",), _pm(inp["w_branch_b"][l]))
        wo = _pm(inp["w_out"][l])
        for cb in range(2):
            for kh in range(2):
                put(("wo", cb, kh), wo[:, kh * 4:(kh + 1) * 4, cb * 512:(cb + 1) * 512])
        wu = _pm(inp["w_up"][l])
        for fb in range(8):
            for kh in range(2):
                put(("wu", fb, kh), wu[:, kh * 4:(kh + 1) * 4, fb * 512:(fb + 1) * 512])
        wd = _pm(inp["w_down"][l])
        for cb in range(2):
            for fg in range(8):
                put(("wd", cb, fg), wd[:, fg * 4:(fg + 1) * 4, cb * 512:(cb + 1) * 512])
        wpg = _pm(inp["w_ple_gate"][l])
        for cb in range(2):
            for kh in range(2):
                put(("wpg", cb, kh), wpg[:, kh * 4:(kh + 1) * 4, cb * 512:(cb + 1) * 512])
        put(("wp",), _pm(inp["w_ple"][l]))
    return out


def make_consts(inp, lastl=3):
    c = {}
    c["identf"] = np.eye(128, dtype=np.float32)
    kl = np.arange(128)[:, None]
    ql = np.arange(128)[None, :]
    cols = []
    for (name, W, r) in GROUPS:
        Dm = GD[name]
        dls = range(-Dm, Dm + 1) if name != "g2" else [8, 0, 0, 0, 0, -8]
        for dl in dls:
            diff = 128 * dl + kl - ql
            cols.append((((np.abs(diff) <= W) & ((kl - ql) % r == 0)).astype(np.float32) - 1.0) * 30000.0)
    c["masks"] = np.concatenate(cols, axis=1).astype(ml_dtypes.bfloat16)
    pos = np.arange(S_LEN, dtype=np.float32)
    inv = (np.float32(500000.0) ** (-np.arange(0, 16, 2, dtype=np.float32) / np.float32(16))).astype(np.float32)
    ang = (pos[:, None] * inv[None, :]).astype(np.float32)
    c["cos"] = np.ascontiguousarray(np.broadcast_to(np.cos(ang).astype(np.float32).reshape(16, 128, 1, 8).transpose(1, 0, 2, 3), (128, 16, 4, 8)))
    c["sin"] = np.ascontiguousarray(np.broadcast_to(np.sin(ang).astype(np.float32).reshape(16, 128, 1, 8).transpose(1, 0, 2, 3), (128, 16, 4, 8)))
    def fm(v, nch):
        return np.ascontiguousarray(v.reshape(4, nch, 128).transpose(2, 0, 1))
    c["bg"] = fm(inp["b_gate"], 16)
    c["g1"] = fm(inp["ln1_g"], 8)
    c["b1"] = fm(inp["ln1_b"], 8)
    c["g2"] = fm(inp["ln2_g"], 8)
    c["b2"] = fm(inp["ln2_b"], 8)
    c["bpgb"] = np.ascontiguousarray(np.broadcast_to(inp["b_ple_gate"][:, None, :], (4, 128, 1024)))
    c["sink"] = np.ascontiguousarray(np.broadcast_to(inp["a_sink"][:, A_PERM].reshape(1, 32), (128, 32)))
    c["lnfg"] = np.ascontiguousarray(np.broadcast_to(inp["ln2_g"][lastl][None, :], (128, 1024)))
    c["lnfb"] = np.ascontiguousarray(np.broadcast_to(inp["ln2_b"][lastl][None, :], (128, 1024)))
    return c


CONST_SHAPES = {
    "identf": ([128, 128], F32), "masks": ([128, 17 * 128], BF), "cos": ([128, 16, 4, 8], F32),
    "sin": ([128, 16, 4, 8], F32), "bg": ([128, 4, 16], F32), "g1": ([128, 4, 8], F32),
    "b1": ([128, 4, 8], F32), "g2": ([128, 4, 8], F32), "b2": ([128, 4, 8], F32),
    "bpgb": ([4, 128, 1024], F32), "sink": ([128, 32], F32), "lnfg": ([128, 1024], F32),
    "lnfb": ([128, 1024], F32),
}


def build(nlayers=4, nseq=2):
    import os as _os
    nc = bass.Bass("TRN2", target_bir_lowering=False)
    S = Sched(nc)
    x_d = nc.dram_tensor("x", [nseq, S_LEN, D], F32, kind="ExternalInput").ap()
    p_d = nc.dram_tensor("p", [4, nseq, S_LEN, 256], F32, kind="ExternalInput").ap()
    w_d = nc.dram_tensor("w", [4, 128, WTOT], F32, kind="ExternalInput").ap()
    out_d = nc.dram_tensor("out", [nseq, S_LEN, D], F32, kind="ExternalOutput").ap()
    cd = {k: nc.dram_tensor("c_" + k, sh, dt, kind="ExternalInput").ap() for k, (sh, dt) in CONST_SHAPES.items()}
    KDBG = _os.environ.get("KDBG", "")
    dbg_d = nc.dram_tensor("dbg", [S_LEN, D], F32, kind="ExternalOutput").ap() if KDBG else None

    def dbg_dump(tag, t, ap, key, ncols=1024):
        if KDBG == tag:
            keys = key if isinstance(key, list) else [key]
            S.op("sp", I("dma_start", out=dbg_d[t * 128:(t + 1) * 128, 0:ncols], in_=ap), r=keys, w=[("dbg", t)], dma=("dbg", t % 4))

    def sb(name, shape, dt):
        return nc.alloc_sbuf_tensor(name, shape, dt).ap()

    xh = sb("xh", [128, 8, S_LEN], BF)
    xl = sb("xl", [128, 8, S_LEN], BF)
    RSZ = 49408
    R = sb("R", [128, RSZ], BF)
    wbuf = [sb("wbuf0", [128, 8, 512], BF),
            R[:, 40960:45056].rearrange("p (k c) -> p k c", c=512), R[:, 45056:49152].rearrange("p (k c) -> p k c", c=512)]
    NSTG = 3
    stg = [sb("stg%d" % i, [128, 512], F32) for i in range(NSTG)]
    wscr = nc.dram_tensor("wscr", [4, 128, WTOT], BF, kind="Internal").ap()
    T = sb("T", [128, 9216], BF)
    cs = {k: sb("k_" + k, sh, dt) for k, (sh, dt) in CONST_SHAPES.items() if k not in ("lnfg", "lnfb", "bpgb")}
    identb = sb("identb", [128, 128], BF)
    bgh = sb("bgh", [128, 4, 16], F32)
    esink = sb("esink", [128, 32], F32)
    mhalf = sb("mhalf", [128, 1], F32)
    onesc = sb("onesc", [128, 2], BF)
    tq = [T[:, i * 512:(i + 1) * 512] for i in range(2)] + [T[:, 7168:7680]]
    rp = [T[:, o_:o_ + 512].bitcast(F32).rearrange("p (a h d) -> p a h d", a=4, h=8) for o_ in (1024, 1536, 7680)]
    NPT = 6
    PT = [T[:, 2048 + i * 512:2048 + (i + 1) * 512] for i in range(3)] + [T[:, 5120 + i * 512:5120 + (i + 1) * 512] for i in range(3)]
    oab = [T[:, 3584 + i * 768:3584 + (i + 1) * 768].rearrange("p (h d) -> p h d", d=64) for i in range(2)]
    f512 = [T[:, i * 1024:(i + 1) * 1024].bitcast(F32) for i in range(3)]
    xf = [T[:, 3072:5120].bitcast(F32).rearrange("p (c t) -> p c t", t=128)]
    zt = [T[:, 5120 + i * 2048:5120 + (i + 1) * 2048].bitcast(F32) for i in range(2)]
    tfr = [T[:, o_:o_ + 256].bitcast(F32).rearrange("p (h d) -> p h d", d=16) for o_ in (6656, 6912, 8192)]
    TA = [("pcb",)] + [("tfr", i) for i in range(3)] + [("tq", i) for i in range(3)] + [("rp", i, j) for i in range(3) for j in range(4)] + [("PT", i) for i in range(6)] + [("oab", i, b) for i in range(2) for b in range(3)]
    TB = [("f", i) for i in range(3)] + [("xf", 0, h) for h in range(2)] + [("zt", i) for i in range(2)]
    sm = [sb("sm%d" % i, [128, 32], F32) for i in range(4)]
    pin = [sb("pin%d" % i, [128, 256], F32) for i in range(2)]
    pT = [sb("pT%d" % i, [128, 2, 128], BF) for i in range(4)]
    psb = [nc.alloc_psum_tensor("ps%d" % i, [128, 512], F32).ap() for i in range(8)]

    qT = R[:, 0:10 * 2048].rearrange("p (c t) -> p c t", t=2048)
    kT = R[:, 20480:20480 + 7 * 2048].rearrange("p (c t) -> p c t", t=2048)
    vv = R[:, 34816:34816 + 16 * 14 * 65].rearrange("p (t h d) -> p t h d", h=14, d=65)
    vflat = R[:, 34816:34816 + 16 * 14 * 65].rearrange("p (n d) -> p n d", d=65)
    o2 = 12288
    wg = R[:, o2:o2 + 8 * 2048].rearrange("p (k c) -> p k c", c=2048)
    wa = R[:, o2 + 16384:o2 + 16384 + 4 * 1024].rearrange("p (k c) -> p k c", c=1024)
    wb = R[:, o2 + 20480:o2 + 20480 + 2 * 1024].rearrange("p (k c) -> p k c", c=1024)
    wo = R[:, o2 + 22528:o2 + 22528 + 8 * 1024].rearrange("p (k c) -> p k c", c=1024)
    mg = R[:, o2 + 30720:o2 + 30720 + 8 * 512].rearrange("p (k c) -> p k c", c=512)
    uT = R[:, 0:32 * 512].rearrange("p (f t) -> p f t", t=512)
    yb = R[:, 16384:16384 + 8192].bitcast(F32).rearrange("p (t c) -> p t c", c=1024)
    wpg = R[:, 24576:24576 + 8 * 1024].rearrange("p (k c) -> p k c", c=1024)
    wpl = R[:, 32768:32768 + 2 * 1024].rearrange("p (k c) -> p k c", c=1024)
    lnfg = R[:, 34816:34816 + 2048].bitcast(F32)
    lnfb = R[:, 36864:36864 + 2048].bitcast(F32)
    bpgbv = R[:, 38912:38912 + 2048].bitcast(F32)
    ztb = [zt[0], zt[1], R[:, 47104:49152].bitcast(F32)]
    RKEYS_A = [("qT", t) for t in range(NT)] + [("kT", t) for t in range(NT)] + [("v", t) for t in range(NT)]
    RKEYS_B = ([(("wg", i), k) for i in range(4) for k in range(8)] + [(("wa", i), k) for i in range(2) for k in range(4)]
               + [(("wo", i), k) for i in range(2) for k in range(8)] + [("wb", k) for k in range(2)] + ["mg", ("zt", 2)])
    RKEYS_C = [("uT", i) for i in range(32)] + ["lnfg", "lnfb", "bpgb"] + [(("wpg", i), k) for i in range(2) for k in range(8)] + [("wpl", k) for k in range(2)] + [(("wbuf", i), k) for i in (1, 2) for k in range(8)] + [("yb", i, j) for i in range(4) for j in range(2)]

    state = {"zb": 0, "ce": 0, "bank": 0, "stg": 0, "wb": 0, "f": 0, "tq": 0, "PT": 0, "z": 0, "xf": 0, "sm": 0, "pin": 0, "oab": 0}

    def rot(name, n):
        i = state[name]
        state[name] = (i + 1) % n
        return i

    pool_ = {"l": list(range(8)), "i": 0}

    def set_pool(lst):
        pool_["l"] = list(lst)
        pool_["i"] = 0

    def bank():
        b = pool_["l"][pool_["i"] % len(pool_["l"])]
        pool_["i"] += 1
        return b

    def I(meth, *a, **kw):
        return lambda e: getattr(e, meth)(*a, **kw)

    def PS(i):
        return psb[i], ("ps", i)

    def mm(out, lhsT, rhs, start, stop, r, w):
        S.op("pe", I("matmul", out, lhsT=lhsT, rhs=rhs, start=start, stop=stop, skip_group_check=True), r=r, w=w)

    def tr(out, in_, ident, r, w):
        S.op("pe", I("transpose", out=out, in_=in_, identity=ident), r=r, w=w)

    def fence(old, new):
        if _os.environ.get("KOFF", "").find("fence") < 0:
            S.op("pool", I("nop", ), w=list(old) + list(new))

    def load_piece(l, name, dst, dkey):
        off, kc, ncol = PIECES[name]
        for k in range(kc):
            for c0 in range(0, ncol, 512):
                n = min(512, ncol - c0)
                si = rot("stg", 2)
                o = off + k * ncol + c0
                S.op("sp", I("dma_start", out=stg[si][:, 0:n], in_=w_d[l, :, o:o + n]),
                     w=[("stg", si)], dma=("stg", si))
                S.op("pool", I("tensor_copy", out=dst[:, k, c0:c0 + n], in_=stg[si][:, 0:n]),
                     r=[("stg", si)], w=[dkey])

    def load_block(l, pieces, dst, dkey, first, extra_w=()):
        off0, _, ncol = PIECES[pieces[0]]
        K = sum(PIECES[p_][1] for p_ in pieces)
        skeys = [("scr", l, off0 + k * ncol + c0) for k in range(K) for c0 in range(0, ncol, 512)]
        sview = wscr[l, :, off0:off0 + K * ncol].rearrange("p (k c) -> p k c", c=ncol)
        if not first:
            S.op("sp", I("dma_start", out=dst, in_=sview), r=skeys, w=[(dkey, k) for k in range(K)] + list(extra_w), dma=("ld", dkey))
            return
        for k in range(K):
            for c0 in range(0, ncol, 512):
                n = min(512, ncol - c0)
                si = rot("stg", NSTG)
                o = off0 + k * ncol + c0
                S.op("sp", I("dma_start", out=stg[si][:, 0:n], in_=w_d[l, :, o:o + n]), w=[("stg", si)], dma=("stg", si))
                if rot("ce", 2) == 0:
                    S.op("dve", I("tensor_copy", out=dst[:, k, c0:c0 + n], in_=stg[si][:, 0:n]), r=[("stg", si)], w=[(dkey, k)] + list(extra_w))
                else:
                    S.op("act", I("activation", out=dst[:, k, c0:c0 + n], in_=stg[si][:, 0:n], func=AF.Copy), r=[("stg", si)], w=[(dkey, k)] + list(extra_w))
        S.op("sp", I("dma_start", out=sview, in_=dst), r=[(dkey, k) for k in range(K)], w=skeys, dma=("scrw", dkey))

    pc_out = [T[:, 8448:8960], T[:, 7168:7680]]
    pc_okey = [("pcb",), ("tq", 2)]
    pc = {"jobs": [], "n": 0, "in": 0}

    def precast_begin(l):
        names = []
        for cb in range(4):
            names += [("wg", cb, 0), ("wg", cb, 1)]
        names += [("wa", 0), ("wa", 1), ("wb",)]
        for cb in range(2):
            names += [("wo", cb, 0), ("wo", cb, 1)]
        for cb in range(2):
            names += [("wpg", cb, 0), ("wpg", cb, 1)]
        names += [("wp",)]
        for fb in range(8):
            names += [("wu", fb, 0), ("wu", fb, 1)]
        for cb in range(2):
            for fg in range(8):
                names += [("wd", cb, fg)]
        jobs = []
        for nm in names:
            off, kc, ncol = PIECES[nm]
            for k in range(kc):
                for c0 in range(0, ncol, 512):
                    jobs.append((l, off + k * ncol + c0, min(512, ncol - c0)))
        pc["jobs"], pc["n"], pc["in"] = jobs, 0, 0

    def precast_issue_in():
        m = pc["in"]
        if m < len(pc["jobs"]):
            l_, o, n_ = pc["jobs"][m]
            si = m % NSTG
            S.op("sp", I("dma_start", out=stg[si][:, 0:n_], in_=w_d[l_, :, o:o + n_]), w=[("stg", si)], dma=("stg", si))
            pc["in"] += 1

    def precast_tick():
        pc["tick"] = pc.get("tick", 0) + 1
        if pc["tick"] % 2 == 0:
            precast_step()

    def precast_step():
        n = pc["n"]
        if n >= len(pc["jobs"]):
            return
        while pc["in"] < min(n + NSTG, len(pc["jobs"])):
            precast_issue_in()
        l_, o, n_ = pc["jobs"][n]
        si, oi_ = n % NSTG, n % 2
        S.op("dve", I("tensor_copy", out=pc_out[oi_][:, 0:n_], in_=stg[si][:, 0:n_]), r=[("stg", si)], w=[pc_okey[oi_]])
        S.op("sp", I("dma_start", out=wscr[l_, :, o:o + n_], in_=pc_out[oi_][:, 0:n_]), r=[pc_okey[oi_]], w=[("scr", l_, o)],
             dma=("pco", oi_))
        pc["n"] += 1

    for k in cs:
        S.op("sp", I("dma_start", out=cs[k], in_=cd[k]), w=["c_" + k], dma="c_" + k)
    S.op("dve", I("tensor_copy", out=identb, in_=cs["identf"]), r=["c_identf"], w=["identb"])
    S.op("dve", I("memset", mhalf, -0.5), w=["mhalf"])
    S.op("dve", I("memset", onesc, 1.0), w=["onesc"])
    S.op("dve", I("tensor_scalar", out=bgh, in0=cs["bg"], scalar1=0.5, scalar2=None, op0=ALU.mult), r=["c_bg"], w=["bgh"])
    S.op("act", I("activation", out=esink, in_=cs["sink"], func=AF.Exp), r=["c_sink"], w=["esink"])
    CK = ["identb", "c_identf"]

    def to_hilo(t, src, skey, gb):
        xi = rot("xf", 1)
        skeys = skey if isinstance(skey, list) else [skey]
        for half in range(2):
            b = bank()
            pb, pk = PS(b)
            for c in range(4):
                tr(pb[:, c * 128:(c + 1) * 128], src[:, (half * 4 + c) * 128:(half * 4 + c + 1) * 128], cs["identf"],
                   r=skeys + ["c_identf"], w=[pk])
            if gb is None:
                S.op("act", I("activation",
                    out=xf[xi][:, half * 4:(half + 1) * 4, :], in_=pb.rearrange("p (c t) -> p c t", t=128), func=AF.Copy),
                    r=[pk], w=[("xf", xi, half)])
            else:
                g, bb, l = gb
                for c in range(4):
                    cc = half * 4 + c
                    S.op("dve", I("tensor_scalar",
                        out=xf[xi][:, cc, :], in0=pb[:, c * 128:(c + 1) * 128], scalar1=cs[g][:, l, cc:cc + 1],
                        scalar2=cs[bb][:, l, cc:cc + 1], op0=ALU.mult, op1=ALU.add),
                        r=[pk, "c_" + g, "c_" + bb], w=[("xf", xi, half)])
        ts = slice(t * 128, (t + 1) * 128)
        S.op("act", I("activation", out=xh[:, :, ts], in_=xf[xi], func=AF.Copy),
             r=[("xf", xi, 0), ("xf", xi, 1)], w=[("xh", t)])
        S.op("pool", I("tensor_tensor", out=xl[:, :, ts], in0=xf[xi], in1=xh[:, :, ts], op=ALU.subtract),
             r=[("xf", xi, 0), ("xf", xi, 1), ("xh", t)], w=[("xl", t)])

    def ln_stats_a(src_aps, skeys):
        si = rot("sm", 4)
        st = sm[si]
        for i, (a, k) in enumerate(zip(src_aps, skeys)):
            S.op("dve", I("bn_stats", out=st[:, i * 6:(i + 1) * 6], in_=a), r=[k], w=[("sm", si, i)])
        S.op("dve", I("bn_aggr", out=st[:, 12:14], in_=st[:, 0:12]), r=[("sm", si, 0), ("sm", si, 1)], w=[("sm", si, 2)])
        S.op("dve", I("tensor_scalar", out=st[:, 14:15], in0=st[:, 13:14], scalar1=EPS_P, scalar2=None, op0=ALU.add),
             r=[("sm", si, 2)], w=[("sm", si, 3)])
        S.op("pool", I("tensor_tensor", out=st[:, 15:16], in0=st[:, 14:15], in1=mhalf, op=ALU.pow),
             r=[("sm", si, 3), "mhalf"], w=[("sm", si, 4)])
        return si

    def ln_stats_b(si):
        st = sm[si]
        S.op("dve", I("tensor_scalar", out=st[:, 16:17], in0=st[:, 12:13], scalar1=-1.0, scalar2=st[:, 15:16],
                                              op0=ALU.mult, op1=ALU.mult), r=[("sm", si, 2), ("sm", si, 4)], w=[("sm", si, 5)])
        return st[:, 15:16], st[:, 16:17], [("sm", si, 4), ("sm", si, 5)]

    def layer_norm_stats(src_aps, skeys):
        si = rot("sm", 4)
        st = sm[si]
        for i, (a, k) in enumerate(zip(src_aps, skeys)):
            S.op("dve", I("bn_stats", out=st[:, i * 6:(i + 1) * 6], in_=a), r=[k], w=[("sm", si, i)])
        S.op("dve", I("bn_aggr", out=st[:, 12:14], in_=st[:, 0:12]), r=[("sm", si, 0), ("sm", si, 1)], w=[("sm", si, 2)])
        S.op("dve", I("tensor_scalar", out=st[:, 14:15], in0=st[:, 13:14], scalar1=EPS_P, scalar2=None, op0=ALU.add),
             r=[("sm", si, 2)], w=[("sm", si, 3)])
        S.op("pool", I("tensor_tensor", out=st[:, 15:16], in0=st[:, 14:15], in1=mhalf, op=ALU.pow),
             r=[("sm", si, 3), "mhalf"], w=[("sm", si, 4)])
        S.op("dve", I("tensor_scalar", out=st[:, 16:17], in0=st[:, 12:13], scalar1=-1.0, scalar2=st[:, 15:16],
                                              op0=ALU.mult, op1=ALU.mult), r=[("sm", si, 2), ("sm", si, 4)], w=[("sm", si, 5)])
        return st[:, 15:16], st[:, 16:17], [("sm", si, 4), ("sm", si, 5)]

    import os as _os
    STOP = _os.environ.get("KSTOP", "")

    class _Stop(Exception):
        pass

    KOFF = _os.environ.get("KOFF", "").split(",")

    def on(name):
        return name not in KOFF

    def stop_at(name):
        if STOP == name:
            raise _Stop()
    try:
      for sq in range(nseq):
          for t in range(NT):
              zi = rot("z", 2)
              S.op("sp", I("dma_start", out=zt[zi], in_=x_d[sq, t * 128:(t + 1) * 128, :]),
                   w=[("zt", zi)], dma=("zt", zi))
              to_hilo(t, zt[zi], ("zt", zi), None)

          for l in range(nlayers):
              last = (l == nlayers - 1)
              stop_at('load')
              fence(RKEYS_C + TB, RKEYS_A + TA)
              set_pool(range(8))
              if on("ones"):
                  S.op("dve", I("tensor_copy", out=vflat[:, :, 64:65], in_=onesc[:, 0:1].unsqueeze(1).to_broadcast([128, 224, 1])),
                       r=["onesc"], w=[("v", t) for t in range(NT)])
              p1pend = []
              wbufP = T[:, 2048:6144].rearrange("p (k c) -> p k c", c=512)
              AL = [("PT", i_) for i_ in range(5)] + [("oab", i_, b_) for i_ in range(2) for b_ in range(3)]
              p1w = [(wbuf[0], ("wbuf", 0), []), (wbufP, ("wbuf", 3), AL)]

              def p1_load(b):
                  wb_, wk_, al_ = p1w[b % 2]
                  load_block(l, [("qkv", b, 0), ("qkv", b, 1)], wb_[:, :, 0:len(QKV_BLOCKS[b])], wk_, sq == 0, extra_w=al_)
              p1_load(0)
              for b in range(7):
                  ncol = len(QKV_BLOCKS[b])
                  if b + 1 < 7:
                      p1_load(b + 1)
                  for t in range(NT):
                      ts = slice(t * 128, (t + 1) * 128)
                      bk = bank()
                      pb, pk = PS(bk)
                      for k in range(8):
                          mm(pb[:, 0:ncol], xh[:, k, ts], p1w[b % 2][0][:, k, 0:ncol], k == 0, k == 7,
                             r=[("xh", t), (p1w[b % 2][1], k)] + p1w[b % 2][2], w=[pk])
                      if not on("evac"):
                          continue
                      if b >= 5:
                          h0, nh = (0, 8) if b == 5 else (8, 6)
                          S.op("act", I("activation",
                              out=vv[:, t, h0:h0 + nh, 0:64], in_=pb[:, 0:ncol].rearrange("p (h d) -> p h d", d=64), func=AF.Copy),
                              r=[pk], w=[("v", t)])
                          continue
                      nh = ncol // 64
                      qi = rot("tq", 3)
                      tqv = tq[qi][:, 0:ncol].rearrange("p (h d) -> p h d", d=64)
                      p3 = pb[:, 0:ncol].rearrange("p (h d) -> p h d", d=64)
                      S.op("act", I("activation", out=tqv, in_=p3, func=AF.Copy), r=[pk], w=[("tq", qi)])
                      tf = tfr[qi]
                      S.op("act", I("activation", out=tf[:, 0:nh, :], in_=p3[:, :, 0:16], func=AF.Copy), r=[pk], w=[("tfr", qi)])
                      if not on("rope"):
                          continue
                      rr = rp[qi]
                      for h0_ in range(0, nh, 4):
                          hn = min(4, nh - h0_)
                          for j, (lo, tabn) in enumerate([(0, "cos"), (8, "sin"), (8, "cos"), (0, "sin")]):
                              S.op("dve", I("tensor_tensor",
                                  out=rr[:, j, h0_:h0_ + hn, :], in0=tf[:, h0_:h0_ + hn, lo:lo + 8], in1=cs[tabn][:, t, 0:hn, :], op=ALU.mult),
                                  r=[("tfr", qi), "c_cos", "c_sin"], w=[("rp", qi, j)])
                      S.op("dve", I("tensor_tensor",
                          out=tqv[:, :, 0:8], in0=rr[:, 0, 0:nh, :], in1=rr[:, 1, 0:nh, :], op=ALU.subtract),
                          r=[("rp", qi, 0), ("rp", qi, 1)], w=[("tq", qi)])
                      S.op("dve", I("tensor_tensor",
                          out=tqv[:, :, 8:16], in0=rr[:, 2, 0:nh, :], in1=rr[:, 3, 0:nh, :], op=ALU.add),
                          r=[("rp", qi, 2), ("rp", qi, 3)], w=[("tq", qi)])
                      def p1_tail(b=b, t=t, ts=ts, qi=qi):
                          c0, c1 = QK_CHUNK_RANGES[b]
                          bt = bank()
                          pt_, ptk = PS(bt)
                          ptb = pt_.bitcast(BF)
                          for i in range(c1 - c0):
                              tr(ptb[:, i * 128:(i + 1) * 128], tq[qi][:, i * 128:(i + 1) * 128], identb, r=[("tq", qi), "identb"], w=[ptk])
                          for (a, bnd, dst, dk, base) in [(c0, min(c1, 10), qT, "qT", 0), (max(c0, 10), c1, kT, "kT", 10)]:
                              if bnd > a:
                                  S.op("act", I("activation",
                                      out=dst[:, a - base:bnd - base, ts],
                                      in_=ptb[:, (a - c0) * 128:(bnd - c0) * 128].rearrange("p (c t) -> p c t", t=128), func=AF.Copy),
                                      r=[ptk], w=[(dk, t)])
                      p1pend.append(p1_tail)
                      while len(p1pend) > 2:
                          p1pend.pop(0)()
              while p1pend:
                  p1pend.pop(0)()

              stop_at('p1')
              moff = {}
              o_ = 0
              for (name, W, r_) in GROUPS:
                  moff[name] = o_
                  o_ += (2 * GD[name] + 1) if name != "g2" else 6
              set_pool(range(4))
              if sq == 0:
                  precast_begin(l)
              p2pend = []
              lagq = []
              started = set()

              def stage(f2, lag):
                  lagq.append(f2)
                  while len(lagq) > lag:
                      lagq.pop(0)()
              o2T = [tq[0], tq[1], T[:, 1024:1536], T[:, 1536:2048]]
              o2k = [[("tq", 0)], [("tq", 1)], [("rp", 0, q_) for q_ in range(4)], [("rp", 1, q_) for q_ in range(4)]]
              for j in range(NT):
                  js = slice(j * 128, (j + 1) * 128)
                  oi = rot("oab", 2)
                  accs = [5, 6, 7]
                  si = rot("sm", 4)
                  st = sm[si]
                  if j % 4 == 0:
                      for s_ in range(4):
                          i_ = 8 + s_
                          qc2, kc2, hf2, vh2 = 4 + i_ // 2, 1 + i_ // 2, i_ % 2, 2 + i_
                          rows = slice(hf2 * 64, (hf2 + 1) * 64)
                          pa2, pa2k = PS(4)
                          first2 = True
                          for kt in range(max(0, j - 8), min(NT - 1, j + 3 + 8) + 1):
                              qa, qb = max(j, kt - 8), min(j + 3, kt + 8) + 1
                              n = qb - qa
                              pb, pk = PS(bank())
                              mm(pb[:, 0:n * 128], kT[rows, kc2, kt * 128:(kt + 1) * 128], qT[rows, qc2, qa * 128:qb * 128],
                                 True, False, r=[("kT", kt)] + [("qT", q_) for q_ in range(qa, qb)], w=[pk])
                              if kt - qa == 8:
                                  m0 = moff["g2"] * 128
                              elif kt - (qb - 1) == -8:
                                  m0 = (moff["g2"] + 6 - n) * 128
                              else:
                                  m0 = (moff["g2"] + 1) * 128
                              mm(pb[:, 0:n * 128], identb, cs["masks"][:, m0:m0 + n * 128], False, True, r=["identb", "c_masks"], w=[pk])
                              pi = rot("PT", NPT)
                              S.op("act", I("activation", out=PT[pi][:, 0:n * 128], in_=pb[:, 0:n * 128], func=AF.Exp, scale=0.125),
                                   r=[pk], w=[("PT", pi)])
                              if sq == 0:
                                  precast_tick()
                              stage(lambda pa2=pa2, pa2k=pa2k, qa=qa, qb=qb, kt=kt, vh2=vh2, pi=pi, n=n, first2=first2, j=j: mm(
                                  pa2[0:65, (qa - j) * 128:(qb - j) * 128], vv[:, kt, vh2, :], PT[pi][:, 0:n * 128], first2, False,
                                  r=[("PT", pi), ("v", kt)], w=[pa2k]), 2)
                              first2 = False
                          stage(lambda pa2=pa2, pa2k=pa2k, s_=s_: S.op("act", I("activation", out=o2T[s_][0:65, :], in_=pa2[0:65, :], func=AF.Copy),
                                                                       r=[pa2k], w=o2k[s_]), 2)

                  def pair_blocks(gname, qc, kc_, vhs, accb, slots):
                      Dm = GD[gname]
                      lo, hi = max(-Dm, -j), min(Dm, NT - 1 - j)
                      dls = list(range(lo, hi + 1))
                      pa, pak = PS(accb)
                      for b0 in range(0, len(dls), 4):
                          batch = dls[b0:b0 + 4]
                          n = len(batch)
                          pbs = [PS(bank()), PS(bank())]
                          for i, dl in enumerate(batch):
                              for hf in range(2):
                                  rows = slice(hf * 64, (hf + 1) * 64)
                                  pb, pk = pbs[hf]
                                  mm(pb[:, i * 128:(i + 1) * 128], kT[rows, kc_, (j + dl) * 128:(j + dl + 1) * 128], qT[rows, qc, js],
                                     i == 0, False, r=[("kT", j + dl), ("qT", j)], w=[pk])
                          if gname != "g2":
                              m0 = (moff[gname] + batch[0] + Dm) * 128
                          elif batch[0] == -8:
                              m0 = moff[gname] * 128
                          elif batch[-1] == 8:
                              m0 = (moff[gname] + 6 - n) * 128
                          else:
                              m0 = (moff[gname] + 1) * 128
                          pis = []
                          for hf in range(2):
                              pb, pk = pbs[hf]
                              mm(pb[:, 0:n * 128], identb, cs["masks"][:, m0:m0 + n * 128], False, True, r=["identb", "c_masks"], w=[pk])
                          for hf in range(2):
                              pb, pk = pbs[hf]
                              pi = rot("PT", NPT)
                              pis.append(pi)
                              S.op("act", I("activation", out=PT[pi][:, 0:n * 128], in_=pb[:, 0:n * 128], func=AF.Exp, scale=0.125),
                                   r=[pk], w=[("PT", pi)])
                          if sq == 0:
                              precast_tick()

                          def pvs(pis=pis, batch=batch, pa=pa, pak=pak, j=j):
                              for hf in range(2):
                                  pi = pis[hf]
                                  for i, dl in enumerate(batch):
                                      st_ = (j, accb) not in started
                                      started.add((j, accb))
                                      vo = 34816 + ((j + dl) * 14 + vhs[hf]) * 65
                                      nv = 128 if vo + 128 <= 34816 + 16 * 14 * 65 else 65
                                      mm(pa[:, slots[hf] * 128:slots[hf] * 128 + nv], PT[pi][:, i * 128:(i + 1) * 128], R[:, vo:vo + nv],
                                         st_, False, r=[("PT", pi), ("v", j + dl)], w=[pak])
                          stage(pvs, 1)

                  def normalise(bi, st=st, si=si, oi=oi, l=l):
                      pa, pak = PS(accs[bi])
                      pv = pa.rearrange("p (s c) -> p s c", c=128)
                      if bi < 2:
                          S.op("dve", I("tensor_tensor",
                              out=st[:, 20 + bi * 4:24 + bi * 4], in0=pv[:, :, 64], in1=esink[:, l * 8 + bi * 4:l * 8 + bi * 4 + 4], op=ALU.add),
                              r=[pak, "esink"], w=[("smd", si, bi)])
                      else:
                          S.op("dve", I("tensor_copy", out=st[:, 20 + bi * 4:24 + bi * 4], in_=pv[:, :, 64]),
                               r=[pak], w=[("smd", si, bi)])
                      S.op("dve", I("reciprocal", out=st[:, 20 + bi * 4:24 + bi * 4], in_=st[:, 20 + bi * 4:24 + bi * 4]),
                           r=[("smd", si, bi)], w=[("smr", si, bi)])
                      for s_ in range(4):
                          S.op("dve", I("tensor_scalar",
                              out=oab[oi][:, bi * 4 + s_, :], in0=pv[:, s_, 0:64], scalar1=st[:, 20 + bi * 4 + s_:21 + bi * 4 + s_],
                              scalar2=None, op0=ALU.mult), r=[pak, ("smr", si, bi)], w=[("oab", oi, bi)])

                  for s0 in (0, 2):
                      for g, gname in enumerate(["g0", "g1"]):
                          i_ = g * 4 + s0
                          pair_blocks(gname, 4 + i_ // 2, 1 + i_ // 2, (2 + i_, 3 + i_), accs[2], (s0, s0 + 1))
                      def g2acc(s0=s0, j=j):
                          for s_ in (s0, s0 + 1):
                              pa7, pa7k = PS(accs[2])
                              mm(pa7[:, s_ * 128:s_ * 128 + 65], o2T[s_][0:65, (j % 4) * 128:(j % 4 + 1) * 128], identb[0:65, 0:65],
                                 False, False, r=o2k[s_] + ["identb"], w=[pa7k])
                      stage(g2acc, 1)
                  stage(lambda: normalise(2), 1)
                  for c in range(4):
                      pair_blocks("A", c, 0, (0, 1), accs[c // 2], ((2 * c) % 4, (2 * c + 1) % 4))
                      if c == 1:
                          stage(lambda: normalise(0), 1)
                  stage(lambda: normalise(1), 1)
                  def p2_tail(oi=oi, js=js, j=j):
                      bt = bank()
                      pt_, ptk = PS(bt)
                      ptb = pt_.bitcast(BF)
                      of = oab[oi].rearrange("p h d -> p (h d)")
                      for i in range(6):
                          tr(ptb[:, i * 128:(i + 1) * 128], of[:, i * 128:(i + 1) * 128], identb,
                             r=[("oab", oi, 0), ("oab", oi, 1), ("oab", oi, 2), "identb"], w=[ptk])
                      S.op("act", I("activation",
                          out=qT[:, 0:6, js], in_=ptb[:, 0:768].rearrange("p (c t) -> p c t", t=128), func=AF.Copy), r=[ptk], w=[("qT", j)])
                  stage(p2_tail, 1)
              while lagq:
                  lagq.pop(0)()
              if sq == 0:
                  while pc["n"] < len(pc["jobs"]):
                      precast_step()

              stop_at('p2a')
              fence(RKEYS_A + TA, RKEYS_B + TB)
              set_pool(range(8))
              f0 = False

              def ld_wg(cb):
                  load_block(l, [("wg", cb, 0), ("wg", cb, 1)], wg[:, :, cb * 512:(cb + 1) * 512], ("wg", cb), f0)

              def ld_wa(cb):
                  load_block(l, [("wa", cb)], wa[:, :, cb * 512:(cb + 1) * 512], ("wa", cb), f0)

              def ld_wo(cb):
                  load_block(l, [("wo", cb, 0), ("wo", cb, 1)], wo[:, :, cb * 512:(cb + 1) * 512], ("wo", cb), f0)
              ld_wg(0)
              ld_wg(2)
              ld_wa(0)
              load_block(l, [("wb",)], wb, "wb", f0)
              ld_wg(1)
              ld_wg(3)
              ld_wa(1)
              ld_wo(0)
              ld_wo(1)
              pend = []

              def flush():
                  while pend:
                      pend.pop(0)()
              for tc in range(4):
                  cs_ = slice(tc * 512, (tc + 1) * 512)
                  tiles = list(range(tc * 4, tc * 4 + 4))
                  xk = [("xh", t) for t in tiles]
                  qk_ = [("qT", t) for t in tiles]
                  for dc in range(8):
                      if dc == 2:
                          flush()
                      tg_ = []
                      for gi in range(2):
                          b_ = bank()
                          pb, pk = PS(b_)
                          col = gi * 1024 + dc * 128
                          for k in range(8):
                              mm(pb, wg[:, k, col:col + 128], xh[:, k, cs_], k == 0, k == 7, r=[(("wg", col // 512), k)] + xk, w=[pk])
                          fi = rot("f", 3)
                          S.op("act", I("activation",
                              out=f512[fi], in_=pb, func=AF.Tanh, scale=0.5, bias=bgh[:, l, gi * 8 + dc:gi * 8 + dc + 1]),
                              r=[pk, "bgh"], w=[("f", fi)])
                          tg_.append(fi)
                      ba = bank()
                      pa, pak = PS(ba)
                      for k in range(4):
                          mm(pa, wa[:, k, dc * 128:(dc + 1) * 128], qT[:, k, cs_], k == 0, k == 3, r=[(("wa", dc // 4), k)] + qk_, w=[pak])
                      bb_ = bank()
                      pb2, pbk = PS(bb_)
                      for k in range(2):
                          mm(pb2, wb[:, k, dc * 128:(dc + 1) * 128], qT[:, 4 + k, cs_], k == 0, k == 1, r=[("wb", k)] + qk_, w=[pbk])
                      fa, fb_ = tg_
                      S.op("dve", I("scalar_tensor_tensor",
                          out=f512[fa], in0=f512[fa], scalar=1.0, in1=pa, op0=ALU.add, op1=ALU.mult), r=[("f", fa), pak], w=[("f", fa)])
                      S.op("dve", I("scalar_tensor_tensor",
                          out=f512[fb_], in0=f512[fb_], scalar=1.0, in1=pb2, op0=ALU.add, op1=ALU.mult), r=[("f", fb_), pbk], w=[("f", fb_)])
                      S.op("dve", I("tensor_tensor", out=f512[fa], in0=f512[fa], in1=f512[fb_], op=ALU.add),
                           r=[("f", fa), ("f", fb_)], w=[("f", fa)])
                      S.op("act", I("activation", out=mg[:, dc, :], in_=f512[fa], func=AF.Copy, scale=C_HALF), r=[("f", fa)], w=["mg"])
                  for ti, t in enumerate(tiles):
                      ts = slice(t * 128, (t + 1) * 128)
                      bks = [bank(), bank()]
                      for cb in range(2):
                          pb, pk = PS(bks[cb])
                          for k in range(8):
                              mm(pb, mg[:, k, ti * 128:(ti + 1) * 128], wo[:, k, cb * 512:(cb + 1) * 512], k == 0, False,
                                 r=["mg", (("wo", cb), k)], w=[pk])
                          for c in range(4):
                              mm(pb[:, c * 128:(c + 1) * 128], xh[:, cb * 4 + c, ts], identb, False, False, r=[("xh", t), "identb"], w=[pk])
                              mm(pb[:, c * 128:(c + 1) * 128], xl[:, cb * 4 + c, ts], identb, False, c == 3, r=[("xl", t), "identb"], w=[pk])
                      rstd, nmr, smk = layer_norm_stats([psb[bks[0]], psb[bks[1]]], [("ps", bks[0]), ("ps", bks[1])])
                      dbg_dump("sml", t, sm[(state["sm"] + 1) % 2], smk, 32)
                      zi = rot("zb", 3)
                      for cb in range(2):
                          S.op("act", I("activation",
                              out=ztb[zi][:, cb * 512:(cb + 1) * 512], in_=psb[bks[cb]], func=AF.Identity, scale=rstd, bias=nmr),
                              r=[("ps", bks[cb])] + smk, w=[("zt", zi)])
                      pend.append(lambda t=t, zi=zi, l=l: to_hilo(t, ztb[zi], ("zt", zi), ("g1", "b1", l)))
                      while len(pend) > 2:
                          pend.pop(0)()
              flush()

              stop_at('p2b')
              fence(RKEYS_B + [("qT", t) for t in range(NT)], RKEYS_C)
              set_pool(range(4))
              for cb in range(2):
                  load_block(l, [("wpg", cb, 0), ("wpg", cb, 1)], wpg[:, :, cb * 512:(cb + 1) * 512], ("wpg", cb), False)
              load_block(l, [("wp",)], wpl, "wpl", False)
              jobs = []
              for tc_ in range(4):
                  for fb in range(8):
                      jobs.append(([("wu", fb, 0), ("wu", fb, 1)], False))
                  for cb in range(2):
                      for fgp in range(4):
                          jobs.append(([("wd", cb, fgp * 2), ("wd", cb, fgp * 2 + 1)], False))
              jst = {"issued": 0, "used": 0}

              def next_block():
                  n = jst["used"]
                  while jst["issued"] < min(n + 3, len(jobs)):
                      m = jst["issued"]
                      load_block(l, jobs[m][0], wbuf[m % 3], ("wbuf", m % 3), jobs[m][1])
                      jst["issued"] += 1
                  jst["used"] += 1
                  return n % 3
              S.op("sp", I("dma_start", out=bpgbv, in_=cd["bpgb"][l]), w=["bpgb"], dma="bpgb")
              if last and sq == 0:
                  pass
              if last:
                  S.op("sp", I("dma_start", out=lnfg, in_=cd["lnfg"]), w=["lnfg"], dma="lnfg")
                  S.op("sp", I("dma_start", out=lnfb, in_=cd["lnfb"]), w=["lnfb"], dma="lnfb")
              for tc in range(4):
                  cs_ = slice(tc * 512, (tc + 1) * 512)
                  tiles = list(range(tc * 4, tc * 4 + 4))
                  xk = [("xh", t) for t in tiles]
                  pTi = {}
                  def p_dma(ti, tiles=tiles, l=l, sq=sq):
                      t = tiles[ti]
                      S.op("sp", I("dma_start", out=pin[ti % 2], in_=p_d[l, sq, t * 128:(t + 1) * 128, :]),
                           w=[("pin", ti % 2)], dma=("pin", ti % 2))

                  def p_tr(ti):
                      pt_, ptk = PS(bank())
                      for k2 in range(2):
                          tr(pt_[:, k2 * 128:(k2 + 1) * 128], pin[ti % 2][:, k2 * 128:(k2 + 1) * 128], cs["identf"],
                             r=[("pin", ti % 2), "c_identf"], w=[ptk])
                      S.op("act", I("activation",
                          out=pT[ti], in_=pt_[:, 0:256].rearrange("p (k t) -> p k t", t=128), func=AF.Copy), r=[ptk], w=[("pT", ti)])
                  for fb in range(8):
                      if fb == 0:
                          p_dma(0)
                          p_dma(1)
                      if fb == 2:
                          p_tr(0)
                          p_tr(1)
                          p_dma(2)
                          p_dma(3)
                      if fb == 6:
                          p_tr(2)
                          p_tr(3)
                      if fb == 4:
                          flush()
                      wi = next_block()
                      for fc in range(4):
                          b_ = bank()
                          pb, pk = PS(b_)
                          for k in range(8):
                              mm(pb, wbuf[wi][:, k, fc * 128:(fc + 1) * 128], xh[:, k, cs_], k == 0, k == 7, r=[(("wbuf", wi), k)] + xk, w=[pk])
                          fi = rot("f", 3)
                          S.op("act", I("activation", out=f512[fi], in_=pb, func=AF.Relu, scale=RELU_S),
                               r=[pk], w=[("f", fi)])
                          S.op("pool" if fc % 2 == 0 else "dve", I("tensor_tensor", out=uT[:, fb * 4 + fc, :], in0=f512[fi], in1=f512[fi],
                                                                                    op=ALU.mult), r=[("f", fi)], w=[("uT", fb * 4 + fc)])
                  for cb in range(2):
                      ccs = slice(cb * 512, (cb + 1) * 512)
                      accb = [4, 5, 6, 7]
                      for fgp in range(4):
                          wi = next_block()
                          for f in range(8):
                              for ti in range(4):
                                  pb, pk = PS(accb[ti])
                                  mm(pb, uT[:, fgp * 8 + f, ti * 128:(ti + 1) * 128], wbuf[wi][:, f, :], fgp == 0 and f == 0, False,
                                     r=[("uT", fgp * 8 + f), (("wbuf", wi), f)], w=[pk])
                      for ti, t in enumerate(tiles):
                          ts = slice(t * 128, (t + 1) * 128)
                          pb, pk = PS(accb[ti])
                          for c in range(4):
                              mm(pb[:, c * 128:(c + 1) * 128], xh[:, cb * 4 + c, ts], identb, False, False, r=[("xh", t), "identb"], w=[pk])
                              mm(pb[:, c * 128:(c + 1) * 128], xl[:, cb * 4 + c, ts], identb, False, c == 3, r=[("xl", t), "identb"], w=[pk])
                          bg_ = bank()
                          pg, pgk = PS(bg_)
                          for k in range(8):
                              mm(pg, xh[:, k, ts], wpg[:, k, ccs], k == 0, k == 7, r=[("xh", t), (("wpg", cb), k)], w=[pgk])
                          fi = rot("f", 3)
                          S.op("dve", I("tensor_tensor", out=f512[fi], in0=pg, in1=bpgbv[:, ccs], op=ALU.add),
                               r=[pgk, "bpgb"], w=[("f", fi)])
                          S.op("act", I("activation", out=f512[fi], in_=f512[fi], func=AF.Tanh, scale=0.5), r=[("f", fi)], w=[("f", fi)])
                          pi_ = ti
                          bw = bank()
                          pw_, pwk = PS(bw)
                          for k2 in range(2):
                              mm(pw_, pT[pi_][:, k2, :], wpl[:, k2, ccs], k2 == 0, k2 == 1, r=[("pT", pi_), ("wpl", k2)], w=[pwk])
                          S.op("dve", I("scalar_tensor_tensor",
                              out=f512[fi], in0=f512[fi], scalar=1.0, in1=pw_, op0=ALU.add, op1=ALU.mult), r=[("f", fi), pwk], w=[("f", fi)])
                          S.op("dve", I("scalar_tensor_tensor",
                              out=yb[:, ti, ccs], in0=f512[fi], scalar=C_HALF, in1=pb, op0=ALU.mult, op1=ALU.add),
                              r=[("f", fi), pk], w=[("yb", ti, cb)])
                  sis = []
                  for ti, t in enumerate(tiles):
                      yk = [("yb", ti, 0), ("yb", ti, 1)]
                      sis.append(ln_stats_a([yb[:, ti, 0:512], yb[:, ti, 512:1024]], yk))
                  for ti, t in enumerate(tiles):
                      yk = [("yb", ti, 0), ("yb", ti, 1)]
                      rstd, nmr, smk = ln_stats_b(sis[ti])
                      S.op("dve", I("tensor_scalar", out=yb[:, ti, :], in0=yb[:, ti, :], scalar1=rstd, scalar2=nmr,
                                    op0=ALU.mult, op1=ALU.add), r=yk + smk, w=yk)

                  def tail(tiles=tiles, l=l, sq=sq, last=last):
                      for ti, t in enumerate(tiles):
                          yk = [("yb", ti, 0), ("yb", ti, 1)]
                          if not last:
                              to_hilo(t, yb[:, ti, :], yk, ("g2", "b2", l))
                          else:
                              S.op("dve", I("tensor_tensor", out=yb[:, ti, :], in0=yb[:, ti, :], in1=lnfg, op=ALU.mult),
                                   r=yk + ["lnfg"], w=yk)
                              S.op("dve", I("tensor_tensor", out=yb[:, ti, :], in0=yb[:, ti, :], in1=lnfb, op=ALU.add),
                                   r=yk + ["lnfb"], w=yk)
                              S.op("sp", I("dma_start", out=out_d[sq, t * 128:(t + 1) * 128, :], in_=yb[:, ti, :]),
                                   r=yk, w=[("out", ti)], dma=("out", ti))
                  pend.append(tail)
              flush()

    except _Stop:
        pass
    S.emit(final_slots=[s for s in S.slots if isinstance(s, tuple) and s[0] == "out"])
    return nc


_CACHE = {}


def kernel(**inputs):
    inp = {k: np.asarray(v) for k, v in inputs.items()}
    wall = pack_weights(inp)
    consts = make_consts(inp)
    if "nc" not in _CACHE:
        _CACHE["nc"] = build()
    nc = _CACHE["nc"]
    in_maps = []
    for c in range(8):
        m = {"x": np.ascontiguousarray(inp["x"][2 * c:2 * c + 2]),
             "p": np.ascontiguousarray(inp["p"][:, 2 * c:2 * c + 2]),
             "w": wall}
        for k, v in consts.items():
            m["c_" + k] = v
        in_maps.append(m)
    res = run_bass_kernel_spmd(nc, in_maps, core_ids=list(range(8)))
    return np.concatenate([r["out"] for r in res.results], axis=0).astype(np.float32)
```
